# Optimizing a Trainium2 kernel written in Bass

```python
import jax, jax.numpy as jnp
from jax import lax
import numpy as np

D_MODEL = 1024
BATCH = 16
SEQ = 256
DEPTH = 1
DEC_BATCH = 4
DEC_SEQ = 2048
PAST_LEN = 512

GRID_W = 64
N_HEADS = 16
HEAD_DIM = D_MODEL // N_HEADS
D_RWKV = N_HEADS * HEAD_DIM
D_POOL = D_MODEL // 2
N_POOL_GROUPS = 4
POOL_GROUP = D_POOL // N_POOL_GROUPS
POOL_OUT_GROUP = D_MODEL // N_POOL_GROUPS
POOL_WINDOWS = (2, 4, 8, 16)
D_FF = 4 * D_MODEL
DECAY_LORA = 64
AAA_LORA = 64
GATE_LORA = 128
N_DIR = 2
N_MOD = 6
D_IN = D_POOL + 3 * D_RWKV + 2 * D_MODEL
RMS_EPS = 1e-6
LNX_EPS = 64e-5
NORM_EPS = 1e-12

kernel_name = "hybrid_pool_rwkv7_diffusion_step"

LAYER_KEYS = ("w_ada", "b_ada", "g_norm1", "g_norm2", "w_in", "mu_rkv", "mu_wag",
              "w_dec0", "w_dec1", "w_dec2", "a0", "a1", "a2", "gate_w1", "gate_w2",
              "k_k", "k_a", "r_k", "ln_x_w", "ln_x_b", "pool_w", "pool_scale",
              "w_out", "w_ff1", "w_ff2")


def rmsnorm(x, g):
    xf = x.astype(jnp.float32)
    y = xf * lax.rsqrt(jnp.mean(xf * xf, axis=-1, keepdims=True) + RMS_EPS)
    return (y * g.astype(jnp.float32)).astype(x.dtype)


def centred_shift(x):
    prev = jnp.pad(x[:, :-1], ((0, 0), (1, 0), (0, 0)))
    nxt = jnp.pad(x[:, 1:], ((0, 0), (0, 1), (0, 0)))
    return 0.5 * (prev + nxt)


def window_mean(z, window):
    lr = z.shape[2]
    cs = jnp.cumsum(z.astype(jnp.float32), axis=2)
    cs = jnp.pad(cs, ((0, 0), (0, 0), (1, 0), (0, 0)))
    t = jnp.arange(lr)
    lo = jnp.clip(t - window // 2, 0, lr)
    hi = jnp.clip(t + window - window // 2, 0, lr)
    total = jnp.take(cs, hi, axis=2) - jnp.take(cs, lo, axis=2)
    cnt = (hi - lo).astype(jnp.float32)[:, None]
    return (total / cnt).astype(z.dtype)


def pool_branch(z, rows, pool_w, pool_scale):
    b, l, _ = z.shape
    zr = z.reshape(b, rows, l // rows, N_POOL_GROUPS, POOL_GROUP)
    outs = []
    for g, win in enumerate(POOL_WINDOWS):
        zg = zr[..., g, :]
        mixed = window_mean(zg, win) - zg
        outs.append(jnp.einsum('brtc,cd->brtd', mixed, pool_w[g]))
    out = jnp.concatenate(outs, axis=-1).reshape(b, l, D_MODEL)
    return out * pool_scale


def wkv_scan(r, w, k, v, kk, a, s0, reverse):
    b, l, _ = r.shape

    def heads(t):
        return jnp.transpose(t.astype(jnp.float32).reshape(b, l, N_HEADS, HEAD_DIM), (1, 0, 2, 3))

    xs = (heads(r), heads(w), heads(k), heads(v), heads(kk), heads(a))

    def step(S, inp):
        r_t, w_t, k_t, v_t, kk_t, a_t = inp
        s_kk = jnp.einsum('bhij,bhj->bhi', S, kk_t)
        S = (S * w_t[:, :, None, :]
             - s_kk[..., :, None] * (kk_t * a_t)[..., None, :]
             + v_t[..., :, None] * k_t[..., None, :])
        y_t = jnp.einsum('bhij,bhj->bhi', S, r_t)
        return S, y_t

    s_final, ys = lax.scan(step, s0.astype(jnp.float32), xs, reverse=reverse)
    return jnp.transpose(ys, (1, 0, 2, 3)).reshape(b, l, D_RWKV), s_final


def rwkv_branch(h, rkv, s0_fwd, s0_bwd, p):
    b, l, _ = h.shape
    f32 = jnp.float32
    rkv = rkv + (centred_shift(rkv) - rkv) * p["mu_rkv"].reshape(-1)
    r, k, v = jnp.split(rkv, 3, axis=-1)
    hd = centred_shift(h) - h
    xw = h + hd * p["mu_wag"][0]
    xa = h + hd * p["mu_wag"][1]
    xg = h + hd * p["mu_wag"][2]
    g = jax.nn.sigmoid(xg @ p["gate_w1"]) @ p["gate_w2"]
    kk = (k * p["k_k"]).astype(f32).reshape(b, l, N_HEADS, HEAD_DIM)
    kk = (kk * lax.rsqrt(jnp.sum(kk * kk, axis=-1, keepdims=True) + NORM_EPS)).reshape(b, l, D_RWKV)
    s0 = (s0_fwd, s0_bwd)
    ys, states = [], []
    for d in range(N_DIR):
        pre_w = (p["w_dec0"][d] + jnp.tanh(xw @ p["w_dec1"][d]) @ p["w_dec2"][d]).astype(f32)
        decay = jnp.exp(-jnp.exp(-jax.nn.softplus(-pre_w) - 0.5))
        a = jax.nn.sigmoid((p["a0"][d] + (xa @ p["a1"][d]) @ p["a2"][d]).astype(f32))
        kd = k * (1.0 + (a - 1.0) * p["k_a"])
        y_d, s_d = wkv_scan(r, decay, kd, v, kk, a, s0[d], reverse=(d == 1))
        ys.append(y_d)
        states.append(s_d)
    y = (ys[0] + ys[1]).reshape(b, l, N_HEADS, HEAD_DIM)
    mu = jnp.mean(y, axis=-1, keepdims=True)
    var = jnp.mean(jnp.square(y - mu), axis=-1, keepdims=True)
    y = ((y - mu) * lax.rsqrt(var + LNX_EPS)).reshape(b, l, D_RWKV) * p["ln_x_w"] + p["ln_x_b"]
    bonus = jnp.sum((r * k * p["r_k"].reshape(-1)).astype(f32).reshape(b, l, N_HEADS, HEAD_DIM),
                    axis=-1, keepdims=True) * v.astype(f32).reshape(b, l, N_HEADS, HEAD_DIM)
    y = y + bonus.reshape(b, l, D_RWKV)
    return (y * g).astype(h.dtype), states[0], states[1]


def mixer(h, s0_fwd, s0_bwd, rows, p):
    z = h @ p["w_in"]
    z_pool = z[..., :D_POOL]
    rkv = z[..., D_POOL:D_POOL + 3 * D_RWKV]
    gate_logits = z[..., D_POOL + 3 * D_RWKV:]
    a_out = pool_branch(z_pool, rows, p["pool_w"], p["pool_scale"])
    b_out, s_fwd, s_bwd = rwkv_branch(h, rkv, s0_fwd, s0_bwd, p)
    gate_a, gate_b = jnp.split(jax.nn.sigmoid(gate_logits), 2, axis=-1)
    out = (gate_a * a_out + gate_b * b_out) @ p["w_out"]
    return out, s_fwd, s_bwd


def layer(x, cvec, s0_fwd, s0_bwd, rows, p):
    mod = jax.nn.silu(cvec) @ p["w_ada"] + p["b_ada"]
    sh1, sc1, ga1, sh2, sc2, ga2 = jnp.split(mod[:, None, :], N_MOD, axis=-1)
    h = rmsnorm(x, p["g_norm1"]) * (1.0 + sc1) + sh1
    mix, s_fwd, s_bwd = mixer(h, s0_fwd, s0_bwd, rows, p)
    x = x + ga1 * mix
    h2 = rmsnorm(x, p["g_norm2"]) * (1.0 + sc2) + sh2
    ff = jnp.square(jax.nn.relu(h2 @ p["w_ff1"])) @ p["w_ff2"]
    x = x + ga2 * ff
    return x, s_fwd, s_bwd


def setup_inputs(seed: int = 0) -> dict:
    key = jax.random.key(seed)
    ks = iter(jax.random.split(key, 40))
    L = DEPTH
    D = D_MODEL

    def nrm(shape, scale):
        return scale * jax.random.normal(next(ks), shape, jnp.float32)

    def uni(shape, lo, hi):
        return jax.random.uniform(next(ks), shape, jnp.float32, lo, hi)

    return {
        "x_prompt": nrm((BATCH, SEQ, D), 1.0),
        "x_sample": nrm((DEC_BATCH, DEC_SEQ, D), 1.0),
        "state_rwkv": nrm((DEC_BATCH, DEPTH, N_DIR, N_HEADS, HEAD_DIM, HEAD_DIM), 0.5),
        "c": nrm((DEC_BATCH, D), 1.0),
        "c_ctx": nrm((D,), 1.0),
        "w_ada": nrm((L, D, N_MOD * D), 0.5 * D ** -0.5),
        "b_ada": nrm((L, N_MOD * D), 0.02),
        "g_norm1": 1.0 + nrm((L, D), 0.05),
        "g_norm2": 1.0 + nrm((L, D), 0.05),
        "w_in": nrm((L, D, D_IN), D ** -0.5),
        "mu_rkv": uni((L, 3, D_RWKV), 0.0, 1.0),
        "mu_wag": uni((L, 3, D), 0.0, 1.0),
        "w_dec0": uni((L, N_DIR, D_RWKV), -4.0, -1.0),
        "w_dec1": nrm((L, N_DIR, D, DECAY_LORA), D ** -0.5),
        "w_dec2": nrm((L, N_DIR, DECAY_LORA, D_RWKV), 0.5 * DECAY_LORA ** -0.5),
        "a0": nrm((L, N_DIR, D_RWKV), 0.5),
        "a1": nrm((L, N_DIR, D, AAA_LORA), D ** -0.5),
        "a2": nrm((L, N_DIR, AAA_LORA, D_RWKV), 0.5 * AAA_LORA ** -0.5),
        "gate_w1": nrm((L, D, GATE_LORA), D ** -0.5),
        "gate_w2": nrm((L, GATE_LORA, D_RWKV), GATE_LORA ** -0.5),
        "k_k": 0.85 + nrm((L, D_RWKV), 0.05),
        "k_a": 1.0 + nrm((L, D_RWKV), 0.05),
        "r_k": nrm((L, N_HEADS, HEAD_DIM), 0.1),
        "ln_x_w": 1.0 + nrm((L, D_RWKV), 0.05),
        "ln_x_b": nrm((L, D_RWKV), 0.02),
        "pool_w": nrm((L, N_POOL_GROUPS, POOL_GROUP, POOL_OUT_GROUP), POOL_GROUP ** -0.5),
        "pool_scale": 0.5 + nrm((L, D), 0.05),
        "w_out": nrm((L, D, D), D ** -0.5),
        "w_ff1": nrm((L, D, D_FF), D ** -0.5),
        "w_ff2": nrm((L, D_FF, D), D_FF ** -0.5),
        "g_final": 1.0 + nrm((D,), 0.05),
    }


def reference(x_prompt, x_sample, state_rwkv, c, c_ctx, w_ada, b_ada, g_norm1, g_norm2, w_in,
              mu_rkv, mu_wag, w_dec0, w_dec1, w_dec2, a0, a1, a2, gate_w1, gate_w2, k_k, k_a,
              r_k, ln_x_w, ln_x_b, pool_w, pool_scale, w_out, w_ff1, w_ff2, g_final):
    weights = (w_ada, b_ada, g_norm1, g_norm2, w_in, mu_rkv, mu_wag, w_dec0, w_dec1, w_dec2,
               a0, a1, a2, gate_w1, gate_w2, k_k, k_a, r_k, ln_x_w, ln_x_b, pool_w, pool_scale,
               w_out, w_ff1, w_ff2)
    rows = x_sample.shape[1] // GRID_W
    zero_state = jnp.zeros((x_prompt.shape[0], N_HEADS, HEAD_DIM, HEAD_DIM), jnp.float32)
    cvec_ctx = c_ctx[None, :]
    ctx = x_prompt
    lat = x_sample
    new_states = []
    for li in range(DEPTH):
        p = dict(zip(LAYER_KEYS, (w[li] for w in weights)))
        ctx, sf, sb = layer(ctx, cvec_ctx, zero_state, zero_state, 1, p)
        new_states.append(jnp.stack([sf, sb], axis=1))
        lat, _, _ = layer(lat, c, state_rwkv[:, li, 0], state_rwkv[:, li, 1], rows, p)
    y_prompt = rmsnorm(ctx, g_final)
    y_sample = rmsnorm(lat, g_final)
    state_rwkv_new = jnp.stack(new_states, axis=1)
    return (y_prompt, y_sample, state_rwkv_new)
```

```python
import os
import numpy as np
from contextlib import ExitStack
import concourse.bass as bass
import concourse.mybir as mybir
from concourse.bass_utils import run_bass_kernel_spmd

F32 = mybir.dt.float32
BF16 = mybir.dt.bfloat16
AF = mybir.ActivationFunctionType
ALU = mybir.AluOpType

NT = 2560
NTILE = NT // 128
D = 1024
SEQS = [(0, 256), (256, 256), (512, 2048)]
WINS = (2, 4, 8, 16)
ROWNAMES = ["g_norm1", "g_norm2", "mu_r", "mu_k", "mu_v", "mu_w", "mu_a", "mu_g", "a0f", "a0b",
            "k_k", "k_a", "r_k", "ln_w", "ln_b", "pool_scale", "sh1", "sc1", "ga1", "sh2", "sc2",
            "ga2", "c0", "c1"]
RI = {n: i for i, n in enumerate(ROWNAMES)}
NR = 32


def seq_of_tile(i):
    t = i * 128
    for si, (s0, ln) in enumerate(SEQS):
        if s0 <= t < s0 + ln:
            return si
    raise ValueError


class KB:
    def __init__(self, nc):
        self.nc = nc
        self.es = ExitStack()
        self.E = {"pe": nc.tensor, "act": nc.scalar, "dve": nc.vector, "pool": nc.gpsimd, "sp": nc.sync}
        self.sems = {}
        self.cnt = {}
        self.seen = {e: {} for e in self.E}
        self.lastw = {}
        self.readers = {}
        self.pending = {e: {} for e in self.E}
        self.epoch = {e: 0 for e in self.E}
        self.pe_r = set()
        self.pe_w = set()
        self.n_ins = 0

    def sem(self, name):
        if name not in self.sems:
            self.sems[name] = self.es.enter_context(self.nc.semaphore(name))
            self.cnt[name] = 0
        return self.sems[name]

    def _deps(self, r, w):
        d = {}

        def add(s, v):
            if d.get(s, 0) < v:
                d[s] = v

        for k in r:
            if k in self.lastw:
                add(*self.lastw[k])
        for k in w:
            if k in self.lastw:
                add(*self.lastw[k])
            for s, v in self.readers.get(k, {}).items():
                add(s, v)
        return d

    def _emit_waits(self, eng, d):
        for s, v in self.pending[eng].items():
            if d.get(s, 0) < v:
                d[s] = v
        self.pending[eng] = {}
        for s, v in d.items():
            if eng == "pe" and s.startswith("c_pe"):
                continue
            if self.seen[eng].get(s, 0) >= v:
                continue
            self.E[eng].wait_ge(self.sems[s], v)
            self.seen[eng][s] = v
            self.n_ins += 1

    def _record(self, tok, r, w):
        s, v = tok
        for k in r:
            rd = self.readers.setdefault(k, {})
            if rd.get(s, 0) < v:
                rd[s] = v
        for k in w:
            self.lastw[k] = tok
            self.readers[k] = {}

    def op(self, eng, fn, r=(), w=(), inc=True):
        d = self._deps(r, w)
        self._emit_waits(eng, d)
        ins = fn(self.E[eng])
        self.n_ins += 1
        if eng == "pe" and not inc:
            self.pe_r.update(r)
            self.pe_w.update(w)
            return
        name = "c_%s%d" % (eng, self.epoch[eng])
        sem = self.sem(name)
        self.cnt[name] += 1
        ins.then_inc(sem, 1)
        tok = (name, self.cnt[name])
        if eng == "pe":
            r = set(r) | self.pe_r
            w = set(w) | self.pe_w
            self.pe_r = set()
            self.pe_w = set()
        self._record(tok, r, w)
        if self.cnt[name] >= 30000:
            self.epoch[eng] += 1

    def dma(self, q, slot, out, in_, r=(), w=(), **kw):
        d = self._deps(r, w)
        self._emit_waits(q, d)
        name = "d_" + slot
        sem = self.sem(name)
        self.cnt[name] += 16
        self.E[q].dma_start(out=out, in_=in_, **kw).then_inc(sem, 16)
        self.n_ins += 1
        self._record((name, self.cnt[name]), r, w)

    def barrier(self):
        assert not self.pe_r and not self.pe_w
        toks = {n: c for n, c in self.cnt.items() if c > 0}
        for e in self.E:
            self.pending[e] = dict(toks)
        self.lastw = {}
        self.readers = {}

    def finish(self):
        self.barrier()
        for e in self.E:
            self._emit_waits(e, {})


def build(debug=False, stop_after="E"):
    run = lambda p: "MABCDE".index(p) <= "MABCDE".index(stop_after)
    nc = bass.Bass("TRN2", target_bir_lowering=False)
    kb = KB(nc)
    dram = {}

    def din(name, shape, dt=F32):
        dram[name] = nc.dram_tensor(name, list(shape), dt, kind="ExternalInput").ap()
        return dram[name]

    def dout(name, shape, dt=F32):
        dram[name] = nc.dram_tensor(name, list(shape), dt, kind="ExternalOutput").ap()
        return dram[name]

    def dscr(name, shape, dt):
        kind = "ExternalOutput" if debug else "Internal"
        dram[name] = nc.dram_tensor(name, list(shape), dt, kind=kind).ap()
        return dram[name]

    xc = din("xc", [NT, D])
    rows_in = din("rows", [NR, D])
    st0 = din("st0", [2, 16, 64, 64])
    w_ada = din("w_ada", [D, 6 * D])
    w_in = din("w_in", [D, 5632])
    wd0 = din("w_dec0", [2, D])
    wd1 = din("w_dec1", [2, D, 64])
    wd2 = din("w_dec2", [2, 64, D])
    a1 = din("a1", [2, D, 64])
    a2 = din("a2", [2, 64, D])
    gw1 = din("gate_w1", [D, 128])
    gw2 = din("gate_w2", [128, D])
    pool_w = din("pool_w", [4, 128, 256])
    w_out = din("w_out", [D, D])
    w_ff1 = din("w_ff1", [D, 4 * D])
    w_ff2 = din("w_ff2", [4 * D, D])
    g_final = din("g_final", [1, D])
    b_ada_in = din("b_ada", [6, D])
    c_identb = din("c_identb", [128, 128])
    c_identf = din("c_identf", [128, 128])
    c_tri = din("c_tri", [2, 128, 384])
    c_mask = din("c_mask", [5, 128, 512])
    c_bd = din("c_bd", [128, 128])
    c_pm = din("c_pm", [2, 4, 128, 2, 256])

    yc = dout("yc", [NT, D])
    sto = dout("sto", [2, 2, 16, 64, 64])

    HT = dscr("HT", [128, 8, NT], BF16)
    AT = [dscr("AT%d" % d, [128, 8, NT], BF16) for d in range(2)]
    RT = [dscr("RT%d" % d, [128, 8, NT], BF16) for d in range(2)]
    BT = [dscr("BT%d" % d, [128, 8, NT], BF16) for d in range(2)]
    KT = [dscr("KT%d" % d, [128, 8, NT], BF16) for d in range(2)]
    BH = [dscr("BH%d" % d, [NT, D], F32) for d in range(2)]
    KH = [dscr("KH%d" % d, [NT, D], F32) for d in range(2)]
    DTO = [dscr("DTO%d" % d, [128, 8, NT // 64], F32) for d in range(2)]
    VH = dscr("VH", [NT, D], BF16)
    VHF = dscr("VHF", [NT, D], F32)
    BON = dscr("BON", [128, 8, NT], F32)
    GT = dscr("GT", [128, 8, NT], F32)
    YS = [dscr("YS%d" % d, [128, 8, NT], F32) for d in range(2)]
    X1 = dscr("X1", [NT, D], F32)
    H2T = dscr("H2T", [128, 8, NT], BF16)

    op = kb.op
    dma = kb.dma

    with ExitStack() as gs:
        def sb(name, shape, dt, stack=gs):
            return stack.enter_context(nc.sbuf_tensor("s_" + name, list(shape), dt))

        def ps(name, shape, dt, stack=gs):
            return stack.enter_context(nc.psum_tensor("p_" + name, list(shape), dt))

        PF = [ps("pf%d" % i, [128, 512], F32) for i in range(6)]
        PB = [ps("pb%d" % i, [128, 1024], BF16) for i in range(2)]

        identb = sb("identb", [128, 128], BF16)
        identf = sb("identf", [128, 128], F32)
        bdones = sb("bdones", [128, 128], F32)
        cols = sb("cols", [128, 8, NR], F32)
        A1 = sb("A1", [128, 8, 2], F32)
        B1 = sb("B1", [128, 8, 2], F32)
        A2 = sb("A2", [128, 8, 2], F32)
        B2 = sb("B2", [128, 8, 2], F32)
        GA1 = [sb("GA1_%d" % i, [128, D], F32) for i in range(2)]
        GA2 = [sb("GA2_%d" % i, [128, D], F32) for i in range(2)]
        GF = sb("GF", [128, D], F32)
        muh = sb("muh", [128, 8, 3], F32)
        omm = sb("omm", [128, 8, 3], F32)
        omka = sb("omka", [128, 8], F32)
        eps12 = sb("eps12", [128, 1], F32)

        def col(name):
            return cols[:, :, RI[name]]

        def colc(name, c):
            return cols[:, c, RI[name]:RI[name] + 1]

        dma("pool", "initp", identb[:], c_identb[:, :], w=["identb"])
        dma("sp", "init", identf[:], c_identf[:, :], w=["identf"])
        dma("sp", "init2", bdones[:], c_bd[:, :], w=["bdones"])
        dma("sp", "init3", GF[:], g_final[0:1, :].partition_broadcast(128), w=["GF"])

        if run("M"):
            with ExitStack() as ph:
                rows = sb("rows", [NR, D], F32, ph)
                silu = sb("silu", [128, 8, 2], F32, ph)
                silub = sb("silub", [128, 8, 2], BF16, ph)
                silubc = [sb("silubc%d" % i, [128, 8, 128], F32, ph) for i in range(2)]
                onesb = sb("onesb", [128, 128], F32, ph)
                slab = [sb("mslab%d" % i, [128, 8, 512], F32, ph) for i in range(2)]
                modT = sb("modT", [128, 8, 6, 2], F32, ph)
                gab = [sb("gab%d" % i, [128, D], F32, ph) for i in range(2)]

                dma("sp", "rows", rows[:], rows_in[:, :], w=["rows"])
                dma("sp", "gab0", gab[0][:], b_ada_in[2:3, :].partition_broadcast(128), w=["gab0"])
                dma("sp", "gab1", gab[1][:], b_ada_in[5:6, :].partition_broadcast(128), w=["gab1"])
                for c in range(8):
                    op("pe", lambda e, c=c: e.transpose(PF[c % 2][:, 0:NR], rows[:, c * 128:(c + 1) * 128], identf[0:NR, 0:NR]),
                       r=["rows", "identf"], w=["pf%d" % (c % 2)])
                    op("dve", lambda e, c=c: e.tensor_copy(out=cols[:, c, :], in_=PF[c % 2][:, 0:NR]),
                       r=["pf%d" % (c % 2)], w=["cols"])
                op("dve", lambda e: e.memset(eps12[:], 1e-12), w=["eps12"])
                op("dve", lambda e: e.memset(onesb[:], 1.0), w=["onesb"])
                op("dve", lambda e: e.tensor_scalar(out=muh[:], in0=cols[:, :, RI["mu_r"]:RI["mu_r"] + 3], scalar1=0.5, scalar2=None, op0=ALU.mult),
                   r=["cols"], w=["muh"])
                op("dve", lambda e: e.tensor_scalar(out=omm[:], in0=cols[:, :, RI["mu_r"]:RI["mu_r"] + 3], scalar1=-1.0, scalar2=1.0, op0=ALU.mult, op1=ALU.add),
                   r=["cols"], w=["omm"])
                op("dve", lambda e: e.tensor_scalar(out=omka[:], in0=col("k_a"), scalar1=-1.0, scalar2=1.0, op0=ALU.mult, op1=ALU.add),
                   r=["cols"], w=["omka"])
                op("act", lambda e: e.activation(out=silu[:], in_=cols[:, :, RI["c0"]:RI["c0"] + 2], func=AF.Sigmoid),
                   r=["cols"], w=["silu"])
                op("dve", lambda e: e.tensor_tensor(out=silu[:], in0=silu[:], in1=cols[:, :, RI["c0"]:RI["c0"] + 2], op=ALU.mult),
                   r=["silu", "cols"], w=["silu"])
                op("dve", lambda e: e.tensor_copy(out=silub[:], in_=silu[:]), r=["silu"], w=["silub"])
                for b in range(2):
                    for c in range(8):
                        op("dve", lambda e, b=b, c=c: e.tensor_scalar(out=silubc[b][:, c, :], in0=onesb[:], scalar1=silu[:, c, b:b + 1], scalar2=None, op0=ALU.mult),
                           r=["silu", "onesb"], w=["silubc%d" % b])
                for sl in range(12):
                    sbuf = slab[sl % 2]
                    skey = "mslab%d" % (sl % 2)
                    dma("sp", skey, sbuf[:], w_ada[:, sl * 512:(sl + 1) * 512].rearrange("(c p) n -> p c n", p=128), w=[skey])
                    which = sl // 2
                    pf = PF[sl % 2]
                    for j in range(4):
                        for kc in range(8):
                            op("pe", lambda e, j=j, kc=kc, sbuf=sbuf, pf=pf: e.matmul(pf[:, j * 2:j * 2 + 2], sbuf[:, kc, j * 128:(j + 1) * 128], silu[:, kc, :], start=(kc == 0), stop=(kc == 7)),
                               r=[skey, "silu"], w=["pf%d" % (sl % 2)], inc=(j == 3 and kc == 7))
                    cbase = (sl % 2) * 4
                    op("dve", lambda e, pf=pf, which=which, cbase=cbase: e.tensor_copy(out=modT[:, cbase:cbase + 4, which, :], in_=pf[:, 0:8].rearrange("p (j b) -> p j b", b=2)),
                       r=["pf%d" % (sl % 2)], w=["modT"])
                    if which in (2, 5):
                        gi = 0 if which == 2 else 1
                        for b in range(2):
                            pg = PF[2 + b]
                            for kc in range(8):
                                op("pe", lambda e, kc=kc, b=b, sbuf=sbuf, pg=pg: e.matmul(pg[:, :], silubc[b][:, kc, :], sbuf[:, kc, :], start=(kc == 0), stop=(kc == 7)),
                                   r=[skey, "silubc%d" % b], w=["pf%d" % (2 + b)], inc=(kc == 7))
                            dst = (GA1 if gi == 0 else GA2)[b]
                            half = sl % 2
                            op("dve", lambda e, dst=dst, pg=pg, gi=gi, half=half: e.tensor_tensor(out=dst[:, half * 512:(half + 1) * 512], in0=pg[:, :], in1=gab[gi][:, half * 512:(half + 1) * 512], op=ALU.add),
                               r=["pf%d" % (2 + b), "gab%d" % gi], w=["GA%d_%d" % (gi + 1, b)])
                for b in range(2):
                    for wi, nm in enumerate(["sh1", "sc1", "ga1", "sh2", "sc2", "ga2"]):
                        op("dve", lambda e, b=b, wi=wi, nm=nm: e.tensor_tensor(out=modT[:, :, wi, b], in0=modT[:, :, wi, b], in1=col(nm), op=ALU.add),
                           r=["modT", "cols"], w=["modT"])
                    op("dve", lambda e, b=b: e.scalar_tensor_tensor(out=A1[:, :, b], in0=modT[:, :, 1, b], scalar=1.0, in1=col("g_norm1"), op0=ALU.add, op1=ALU.mult),
                       r=["modT", "cols"], w=["A1"])
                    op("dve", lambda e, b=b: e.scalar_tensor_tensor(out=A2[:, :, b], in0=modT[:, :, 4, b], scalar=1.0, in1=col("g_norm2"), op0=ALU.add, op1=ALU.mult),
                       r=["modT", "cols"], w=["A2"])
                    op("dve", lambda e, b=b: e.tensor_copy(out=B1[:, :, b], in_=modT[:, :, 0, b]), r=["modT"], w=["B1"])
                    op("dve", lambda e, b=b: e.tensor_copy(out=B2[:, :, b], in_=modT[:, :, 3, b]), r=["modT"], w=["B2"])
                kb.barrier()

        def norm_transpose(ph, xt, xkey, Acol, Bcol, cv, out_fm, okey, scratch):
            ss, rstd, xn = scratch
            op("act", lambda e: e.activation(out=xn[:], in_=xt, func=AF.Square, accum_out=ss[:]),
               r=[xkey], w=["nt_xn", "nt_ss"])
            op("dve", lambda e: e.tensor_scalar(out=rstd[:], in0=ss[:], scalar1=1.0 / D, scalar2=1e-6, op0=ALU.mult, op1=ALU.add),
               r=["nt_ss"], w=["nt_rstd"])
            op("act", lambda e: e.activation(out=rstd[:], in_=rstd[:], func=AF.Sqrt), r=["nt_rstd"], w=["nt_rstd"])
            op("dve", lambda e: e.reciprocal(out=rstd[:], in_=rstd[:]), r=["nt_rstd"], w=["nt_rstd"])
            op("act", lambda e: e.activation(out=xn[:], in_=xt, func=AF.Copy, scale=rstd[:]),
               r=[xkey, "nt_rstd"], w=["nt_xn"])
            for c in range(8):
                op("pe", lambda e, c=c: e.transpose(PF[4 + c // 4][:, (c % 4) * 128:(c % 4) * 128 + 128], xn[:, c * 128:(c + 1) * 128], identf[:]),
                   r=["nt_xn", "identf"], w=["pf%d" % (4 + c // 4)], inc=(c % 4 == 3))
            for c in range(8):
                op("dve", lambda e, c=c: e.tensor_scalar(out=out_fm(c), in0=PF[4 + c // 4][:, (c % 4) * 128:(c % 4) * 128 + 128], scalar1=Acol[:, c, cv:cv + 1], scalar2=Bcol[:, c, cv:cv + 1], op0=ALU.mult, op1=ALU.add),
                   r=["pf%d" % (4 + c // 4), "A1", "A2", "B1", "B2"], w=[okey])

        def fm_dram(t, tok0, n):
            return t[:, :, tok0:tok0 + n]

        if run("A"):
            with ExitStack() as ph:
                xt = [sb("a_x%d" % i, [128, D], F32, ph) for i in range(2)]
                hT = [sb("a_h%d" % i, [128, 8, 128], BF16, ph) for i in range(2)]
                ss = sb("a_ss", [128, 1], F32, ph)
                rstd = sb("a_rstd", [128, 1], F32, ph)
                xn = sb("a_xn", [128, D], F32, ph)
                for i in range(NTILE):
                    cv = 0 if i < 4 else 1
                    b = i % 2
                    dma("sp", "a_x%d" % b, xt[b][:], xc[i * 128:(i + 1) * 128, :], w=["a_x%d" % b])
                    norm_transpose(ph, xt[b][:], "a_x%d" % b, A1, B1, cv, lambda c, b=b: hT[b][:, c, :], "a_h%d" % b, (ss, rstd, xn))
                    dma("sp", "a_h%d" % b, fm_dram(HT, i * 128, 128), hT[b][:], r=["a_h%d" % b], w=["HT"])
                kb.barrier()

        if run("B"):
            with ExitStack() as ph:
                w_rkv = sb("w_rkv", [128, 8, 3072], BF16, ph)
                wd1s = sb("wd1s", [128, 2, 8, 64], BF16, ph)
                a1s = sb("a1s", [128, 2, 8, 64], BF16, ph)
                wd2s = sb("wd2s", [64, 2, D], BF16, ph)
                a2s = sb("a2s", [64, 2, D], BF16, ph)
                gw1s = sb("gw1s", [128, 8, 128], BF16, ph)
                gw2s = sb("gw2s", [128, D], BF16, ph)
                WD0 = [sb("WD0_%d" % d, [128, D], F32, ph) for d in range(2)]
                tri = [sb("tri%d" % d, [128, 384], F32, ph) for d in range(2)]
                for q in range(3):
                    dma("pool", "w_rkv%d" % q, w_rkv[:, :, q * 1024:(q + 1) * 1024],
                        w_in[:, 512 + q * 1024:512 + (q + 1) * 1024].rearrange("(c p) n -> p c n", p=128), w=["w_rkv%d" % q])
                for d in range(2):
                    dma("pool", "wsm", wd1s[:, d, :, :], wd1[d].rearrange("(c p) n -> p c n", p=128), w=["wsm"])
                    dma("pool", "wsm", a1s[:, d, :, :], a1[d].rearrange("(c p) n -> p c n", p=128), w=["wsm"])
                    dma("pool", "wsm", wd2s[:, d, :], wd2[d], w=["wsm"])
                    dma("pool", "wsm", a2s[:, d, :], a2[d], w=["wsm"])
                    dma("sp", "wsm2", WD0[d][:], wd0[d:d + 1, :].partition_broadcast(128), w=["wsm2"])
                    dma("sp", "wsm2", tri[d][:], c_tri[d], w=["wsm2"])
                dma("pool", "wsm", gw1s[:], gw1.rearrange("(c p) n -> p c n", p=128), w=["wsm"])
                dma("pool", "wsm", gw2s[:], gw2[:, :], w=["wsm"])
                WK = ["w_rkv0", "w_rkv1", "w_rkv2"]

                hx = sb("b_hx", [128, 8, 130], BF16, ph)
                hd = sb("b_hd", [128, 8, 128], F32, ph)
                xw = sb("b_xw", [128, 8, 128], BF16, ph)
                xa = sb("b_xa", [128, 8, 128], BF16, ph)
                xg = sb("b_xg", [128, 8, 128], BF16, ph)
                tw = sb("b_tw", [64, 2, 128], BF16, ph)
                aw = sb("b_aw", [64, 2, 128], BF16, ph)
                gw = sb("b_gw", [128, 128], BF16, ph)
                zs = [sb("b_zs%d" % i, [128, 130], F32, ph) for i in range(2)]
                t2 = [sb("b_t2%d" % i, [128, 128], F32, ph) for i in range(2)]
                rkv = sb("b_rkv", [128, 24, 128], F32, ph)
                kraw = sb("b_kraw", [128, 8, 128], F32, ph)
                sq = sb("b_sq", [128, 8, 128], F32, ph)
                kk = sb("b_kk", [128, 8, 128], F32, ph)
                bon = sb("b_bon", [128, 8, 128], F32, ph)
                gt = sb("b_gt", [128, 8, 128], F32, ph)
                vb = sb("b_vb", [128, 8, 128], BF16, ph)
                vht = sb("b_vht", [128, D], BF16, ph)
                sig = sb("b_sig", [128, D], F32, ph)
                afm = sb("b_a", [128, 8, 128], F32, ph)
                beta = sb("b_beta", [128, 8, 128], F32, ph)
                kd = sb("b_kd", [128, 8, 128], F32, ph)
                ee = [sb("b_e%d" % i, [128, 128], F32, ph) for i in range(4)]
                o_at = sb("b_oat", [128, 8, 128], BF16, ph)
                o_rt = sb("b_ort", [128, 8, 128], BF16, ph)
                o_bt = sb("b_obt", [128, 8, 128], BF16, ph)
                o_kt = sb("b_okt", [128, 8, 128], BF16, ph)
                o_bh = sb("b_obh", [128, 8, 128], F32, ph)
                o_kh = sb("b_okh", [128, 8, 128], F32, ph)
                bht = sb("b_bht", [128, D], F32, ph)
                kht = sb("b_kht", [128, D], F32, ph)
                vhf = sb("b_vhf", [128, D], F32, ph)
                dtt = sb("b_dtt", [128, 8, 2], F32, ph)

                BSEC = int(os.environ.get("B_SEC", "99"))
                for i in range(int(os.environ.get("B_MAXT", NTILE))):
                    t0 = i * 128
                    s0, sl = SEQS[seq_of_tile(i)]
                    has_prev = t0 > s0
                    has_next = t0 + 128 < s0 + sl
                    lo = t0 - 1 if has_prev else t0
                    hi = t0 + 129 if has_next else t0 + 128
                    if not has_prev:
                        op("dve", lambda e: e.memset(hx[:, :, 0:1], 0.0), w=["hx"])
                    if not has_next:
                        op("dve", lambda e: e.memset(hx[:, :, 129:130], 0.0), w=["hx"])
                    dma("sp", "b_hx", hx[:, :, (lo - (t0 - 1)):(hi - (t0 - 1))], HT[:, :, lo:hi], r=["HT"], w=["hx"])
                    if BSEC < 1:
                        continue
                    op("dve", lambda e: e.tensor_tensor(out=hd[:], in0=hx[:, :, 0:128], in1=hx[:, :, 2:130], op=ALU.add), r=["hx"], w=["hd"])
                    op("dve", lambda e: e.scalar_tensor_tensor(out=hd[:], in0=hd[:], scalar=0.5, in1=hx[:, :, 1:129], op0=ALU.mult, op1=ALU.subtract),
                       r=["hd", "hx"], w=["hd"])
                    for c in range(8):
                        for nm, dst, key in (("mu_w", xw, "xw"), ("mu_a", xa, "xa"), ("mu_g", xg, "xg")):
                            op("dve", lambda e, c=c, nm=nm, dst=dst: e.scalar_tensor_tensor(out=dst[:, c, :], in0=hd[:, c, :], scalar=colc(nm, c), in1=hx[:, c, 1:129], op0=ALU.mult, op1=ALU.add),
                               r=["hd", "hx"], w=[key])
                    S1 = int(os.environ.get("S1", "9"))
                    if S1 < 1:
                        continue
                    for d in range(2):
                        for kc in range(8):
                            op("pe", lambda e, d=d, kc=kc: e.matmul(PF[0][0:64, d * 128:(d + 1) * 128], wd1s[:, d, kc, :], xw[:, kc, :], start=(kc == 0), stop=(kc == 7)),
                               r=["wsm", "xw"], w=["pf0"], inc=(kc == 7))
                        for kc in range(8):
                            op("pe", lambda e, d=d, kc=kc: e.matmul(PF[0][0:64, 256 + d * 128:256 + (d + 1) * 128], a1s[:, d, kc, :], xa[:, kc, :], start=(kc == 0), stop=(kc == 7)),
                               r=["wsm", "xa"], w=["pf0"], inc=(kc == 7))
                    if S1 < 2:
                        continue
                    op("act", lambda e: e.activation(out=tw[:].rearrange("p d t -> p (d t)"), in_=PF[0][0:64, 0:256], func=AF.Tanh), r=["pf0"], w=["tw"])
                    if True:
                        op("act", lambda e: e.activation(out=aw[:].rearrange("p d t -> p (d t)"), in_=PF[0][0:64, 256:512], func=AF.Copy), r=["pf0"], w=["aw"])
                    else:
                        op("dve", lambda e: e.tensor_copy(out=aw[:].rearrange("p d t -> p (d t)"), in_=PF[0][0:64, 256:512]), r=["pf0"], w=["aw"])
                    if S1 < 3:
                        continue
                    for kc in range(8):
                        op("pe", lambda e, kc=kc: e.matmul(PF[1][:, 0:128], gw1s[:, kc, :], xg[:, kc, :], start=(kc == 0), stop=(kc == 7)),
                           r=["wsm", "xg"], w=["pf1"], inc=(kc == 7))
                    op("act", lambda e: e.activation(out=gw[:], in_=PF[1][:, 0:128], func=AF.Sigmoid), r=["pf1"], w=["gw"])
                    if S1 < 4:
                        continue
                    for half in range(2):
                        for j in range(4):
                            fc = half * 4 + j
                            op("pe", lambda e, fc=fc, j=j: e.matmul(PF[1][:, j * 128:(j + 1) * 128], gw2s[:, fc * 128:(fc + 1) * 128], gw[:], start=True, stop=True),
                               r=["wsm", "gw"], w=["pf1"], inc=(j == 3))
                        op("act", lambda e, half=half: e.activation(out=gt[:, half * 4:half * 4 + 4, :], in_=PF[1][:, :].rearrange("p (j t) -> p j t", t=128), func=AF.Copy),
                           r=["pf1"], w=["gt"])
                    if S1 < 5:
                        continue
                    dma("sp", "b_gt", fm_dram(GT, t0, 128), gt[:], r=["gt"], w=["GT"])
                    if BSEC < 2:
                        continue
                    for fc in range(24):
                        pf = PF[2 + fc % 2]
                        pk = "pf%d" % (2 + fc % 2)
                        z = zs[fc % 2]
                        zk = "zs%d" % (fc % 2)
                        tt = t2[fc % 2]
                        tk = "t2%d" % (fc % 2)
                        for kc in range(8):
                            op("pe", lambda e, fc=fc, kc=kc, pf=pf: e.matmul(pf[:, 0:130], w_rkv[:, kc, fc * 128:(fc + 1) * 128], hx[:, kc, :], start=(kc == 0), stop=(kc == 7)),
                               r=[WK[fc // 8], "hx"], w=[pk], inc=(kc == 7))
                        op("act", lambda e, pf=pf, z=z: e.activation(out=z[:], in_=pf[:, 0:130], func=AF.Copy), r=[pk], w=[zk])
                        op("dve", lambda e, z=z, tt=tt: e.tensor_tensor(out=tt[:], in0=z[:, 0:128], in1=z[:, 2:130], op=ALU.add), r=[zk], w=[tk])
                        op("dve", lambda e, tt=tt, fc=fc: e.tensor_scalar(out=tt[:], in0=tt[:], scalar1=muh[:, fc % 8, fc // 8:fc // 8 + 1], scalar2=None, op0=ALU.mult),
                           r=[tk, "muh"], w=[tk])
                        op("dve", lambda e, z=z, tt=tt, fc=fc: e.scalar_tensor_tensor(out=rkv[:, fc, :], in0=z[:, 1:129], scalar=omm[:, fc % 8, fc // 8:fc // 8 + 1], in1=tt[:], op0=ALU.mult, op1=ALU.add),
                           r=[zk, tk, "omm"], w=["rkv"])
                    R = lambda c: rkv[:, c, :]
                    Kf = lambda c: rkv[:, 8 + c, :]
                    Vf = lambda c: rkv[:, 16 + c, :]
                    if BSEC < 3:
                        continue
                    for c in range(8):
                        op("dve", lambda e, c=c: e.tensor_scalar(out=kraw[:, c, :], in0=Kf(c), scalar1=colc("k_k", c), scalar2=None, op0=ALU.mult), r=["rkv"], w=["kraw"])
                    op("act", lambda e: e.activation(out=sq[:], in_=kraw[:], func=AF.Square), r=["kraw"], w=["sq"])
                    for half in range(2):
                        for j in range(4):
                            c = half * 4 + j
                            op("pe", lambda e, c=c, j=j: e.matmul(PF[4][:, j * 128:(j + 1) * 128], bdones[:], sq[:, c, :], start=True, stop=True),
                               r=["bdones", "sq"], w=["pf4"], inc=(j == 3))
                        op("act", lambda e, half=half: e.activation(out=kk[:, half * 4:half * 4 + 4, :], in_=PF[4][:, :].rearrange("p (j t) -> p j t", t=128), func=AF.Ln, bias=eps12[:]),
                           r=["pf4", "eps12"], w=["kk"])
                    op("act", lambda e: e.activation(out=kk[:], in_=kk[:], func=AF.Exp, scale=-0.5), r=["kk"], w=["kk"])
                    op("dve", lambda e: e.tensor_tensor(out=kk[:], in0=kk[:], in1=kraw[:], op=ALU.mult), r=["kk", "kraw"], w=["kk"])
                    for c in range(8):
                        op("dve", lambda e, c=c: e.scalar_tensor_tensor(out=sq[:, c, :], in0=R(c), scalar=colc("r_k", c), in1=Kf(c), op0=ALU.mult, op1=ALU.mult),
                           r=["rkv"], w=["sq"])
                    for half in range(2):
                        for j in range(4):
                            c = half * 4 + j
                            op("pe", lambda e, c=c, j=j: e.matmul(PF[4][:, j * 128:(j + 1) * 128], bdones[:], sq[:, c, :], start=True, stop=True),
                               r=["bdones", "sq"], w=["pf4"], inc=(j == 3))
                        op("dve", lambda e, half=half: e.tensor_tensor(out=bon[:, half * 4:half * 4 + 4, :], in0=PF[4][:, :].rearrange("p (j t) -> p j t", t=128), in1=rkv[:, 16 + half * 4:16 + half * 4 + 4, :], op=ALU.mult),
                           r=["pf4", "rkv"], w=["bon"])
                    dma("sp", "b_bon", fm_dram(BON, t0, 128), bon[:], r=["bon"], w=["BON"])
                    if BSEC < 4:
                        continue
                    op("act", lambda e: e.activation(out=vb[:], in_=rkv[:, 16:24, :], func=AF.Copy), r=["rkv"], w=["vb"])
                    for c in range(8):
                        op("pe", lambda e, c=c: e.transpose(PB[0][:, c * 128:(c + 1) * 128], vb[:, c, :], identb[:]), r=["vb", "identb"], w=["pb0"], inc=(c == 7))
                    op("dve", lambda e: e.tensor_copy(out=vht[:], in_=PB[0][:, :]), r=["pb0"], w=["vht"])
                    dma("sp", "b_vht", VH[t0:t0 + 128, :], vht[:], r=["vht"], w=["VH"])
                    for c in range(8):
                        op("pe", lambda e, c=c: e.transpose(PF[c // 4][:, (c % 4) * 128:(c % 4) * 128 + 128], rkv[:, 16 + c, :], identf[:]),
                           r=["rkv", "identf"], w=["pf%d" % (c // 4)], inc=(c % 4 == 3))
                    op("dve", lambda e: e.tensor_copy(out=vhf[:, 0:512], in_=PF[0][:, :]), r=["pf0"], w=["vhf"])
                    op("act", lambda e: e.activation(out=vhf[:, 512:1024], in_=PF[1][:, :], func=AF.Copy), r=["pf1"], w=["vhf"])
                    dma("sp", "b_vhf", VHF[t0:t0 + 128, :], vhf[:], r=["vhf"], w=["VHF"])
                    if BSEC < 5:
                        continue
                    for d in range(2):
                        for half in range(2):
                            op("pe", lambda e, d=d, half=half: e.matmul(PF[0][:, :], tw[:, d, :], wd2s[:, d, half * 512:(half + 1) * 512], start=True, stop=True),
                               r=["tw", "wsm"], w=["pf0"])
                            op("dve", lambda e, d=d, half=half: e.tensor_tensor(out=sig[:, half * 512:(half + 1) * 512], in0=PF[0][:, :], in1=WD0[d][:, half * 512:(half + 1) * 512], op=ALU.add),
                               r=["pf0", "wsm2"], w=["sig"])
                        op("act", lambda e: e.activation(out=sig[:], in_=sig[:], func=AF.Sigmoid), r=["sig"], w=["sig"])
                        a0n = "a0f" if d == 0 else "a0b"
                        for c in range(8):
                            pc = PF[2 + c % 2]
                            pck = "pf%d" % (2 + c % 2)
                            pa = PF[4 + c % 2]
                            pak = "pf%d" % (4 + c % 2)
                            op("pe", lambda e, c=c, d=d, pc=pc: e.matmul(pc[:, 0:384], sig[:, c * 128:(c + 1) * 128], tri[d][:], start=True, stop=True),
                               r=["sig", "wsm2"], w=[pck])
                            op("pe", lambda e, c=c, d=d, pa=pa: e.matmul(pa[:, 0:128], a2s[:, d, c * 128:(c + 1) * 128], aw[:, d, :], start=True, stop=True),
                               r=["wsm", "aw"], w=[pak])
                            op("act", lambda e, c=c, pa=pa, a0n=a0n: e.activation(out=afm[:, c, :], in_=pa[:, 0:128], func=AF.Sigmoid, bias=colc(a0n, c)),
                               r=[pak, "cols"], w=["afm"])
                            op("act", lambda e, pc=pc: e.activation(out=ee[0][:], in_=pc[:, 128:256], func=AF.Exp), r=[pck], w=["e0"])
                            op("act", lambda e, pc=pc: e.activation(out=ee[1][:], in_=pc[:, 0:128], func=AF.Exp), r=[pck], w=["e1"])
                            op("act", lambda e, pc=pc: e.activation(out=ee[2][:], in_=pc[:, 0:128], func=AF.Exp, scale=-1.0), r=[pck], w=["e2"])
                            op("act", lambda e, pc=pc: e.activation(out=ee[3][:], in_=pc[:, 256:384], func=AF.Exp), r=[pck], w=["e3"])
                            if d == 0:
                                src = pc[:, 63:128:64]
                            else:
                                src = pc[:, 0:128:64]
                            op("act", lambda e, c=c, src=src: e.activation(out=dtt[:, c, :], in_=src, func=AF.Exp), r=[pck], w=["dtt"])
                            op("dve", lambda e, c=c: e.tensor_tensor(out=beta[:, c, :], in0=kk[:, c, :], in1=afm[:, c, :], op=ALU.mult), r=["kk", "afm"], w=["beta"])
                            op("dve", lambda e, c=c: e.tensor_scalar(out=kd[:, c, :], in0=afm[:, c, :], scalar1=colc("k_a", c), scalar2=omka[:, c:c + 1], op0=ALU.mult, op1=ALU.add),
                               r=["afm", "cols", "omka"], w=["kd"])
                            op("dve", lambda e, c=c: e.tensor_tensor(out=kd[:, c, :], in0=kd[:, c, :], in1=Kf(c), op=ALU.mult), r=["kd", "rkv"], w=["kd"])
                            op("dve", lambda e, c=c: e.scalar_tensor_tensor(out=o_at[:, c, :], in0=kk[:, c, :], scalar=-1.0, in1=ee[0][:], op0=ALU.mult, op1=ALU.mult),
                               r=["kk", "e0"], w=["o_at"])
                            op("dve", lambda e, c=c: e.tensor_tensor(out=o_rt[:, c, :], in0=R(c), in1=ee[1][:], op=ALU.mult), r=["rkv", "e1"], w=["o_rt"])
                            op("dve", lambda e, c=c: e.tensor_tensor(out=o_bt[:, c, :], in0=beta[:, c, :], in1=ee[2][:], op=ALU.mult), r=["beta", "e2"], w=["o_bt"])
                            op("dve", lambda e, c=c: e.tensor_tensor(out=o_kt[:, c, :], in0=kd[:, c, :], in1=ee[2][:], op=ALU.mult), r=["kd", "e2"], w=["o_kt"])
                            op("dve", lambda e, c=c: e.tensor_tensor(out=o_bh[:, c, :], in0=beta[:, c, :], in1=ee[3][:], op=ALU.mult), r=["beta", "e3"], w=["o_bh"])
                            op("dve", lambda e, c=c: e.tensor_tensor(out=o_kh[:, c, :], in0=kd[:, c, :], in1=ee[3][:], op=ALU.mult), r=["kd", "e3"], w=["o_kh"])
                        for src_t, dst_t, sk, dk, pb0 in ((o_bh, bht, "o_bh", "bht", 0), (o_kh, kht, "o_kh", "kht", 2)):
                            for c in range(8):
                                op("pe", lambda e, c=c, src_t=src_t, pb0=pb0: e.transpose(PF[pb0 + c // 4][:, (c % 4) * 128:(c % 4) * 128 + 128], src_t[:, c, :], identf[:]),
                                   r=[sk, "identf"], w=["pf%d" % (pb0 + c // 4)], inc=(c % 4 == 3))
                            op("dve", lambda e, dst_t=dst_t, pb0=pb0: e.tensor_copy(out=dst_t[:, 0:512], in_=PF[pb0][:, :]), r=["pf%d" % pb0], w=[dk])
                            op("act", lambda e, dst_t=dst_t, pb0=pb0: e.activation(out=dst_t[:, 512:1024], in_=PF[pb0 + 1][:, :], func=AF.Copy), r=["pf%d" % (pb0 + 1)], w=[dk])
                        dma("sp", "b_o0", fm_dram(AT[d], t0, 128), o_at[:], r=["o_at"], w=["AT"])
                        dma("sp", "b_o1", fm_dram(RT[d], t0, 128), o_rt[:], r=["o_rt"], w=["RT"])
                        dma("sp", "b_o2", fm_dram(BT[d], t0, 128), o_bt[:], r=["o_bt"], w=["BT"])
                        dma("sp", "b_o3", fm_dram(KT[d], t0, 128), o_kt[:], r=["o_kt"], w=["KT"])
                        dma("sp", "b_o4", BH[d][t0:t0 + 128, :], bht[:], r=["bht"], w=["BH"])
                        dma("sp", "b_o5", KH[d][t0:t0 + 128, :], kht[:], r=["kht"], w=["KH"])
                        dma("sp", "b_o6", DTO[d][:, :, 2 * i:2 * i + 2], dtt[:], r=["dtt"], w=["DTO"])
                kb.barrier()

        if run("C"):
            with ExitStack() as ph:
                masks = sb("c_masks", [128, 5, 512], BF16, ph)
                dma("pool", "c_mask", masks[:], c_mask.rearrange("m p n -> p m n"), w=["masks"])
                ident64 = identf[0:64, 0:64]
                SU, SL, IU, IL, IDS = range(5)
                satz = [sb("c_atz%d" % i, [128, 2, 8, 128], BF16, ph) for i in range(2)]
                srtz = [sb("c_rtz%d" % i, [128, 2, 8, 128], BF16, ph) for i in range(2)]
                sbt = [sb("c_bt%d" % i, [128, 8, 128], BF16, ph) for i in range(2)]
                skt = [sb("c_kt%d" % i, [128, 8, 128], BF16, ph) for i in range(2)]
                sbhz = [sb("c_bhz%d" % i, [128, 2, D], F32, ph) for i in range(2)]
                skhz = [sb("c_khz%d" % i, [128, 2, D], F32, ph) for i in range(2)]
                svfz = [sb("c_vfz%d" % i, [128, 2, D], F32, ph) for i in range(2)]
                svhz = [sb("c_vhz%d" % i, [128, 2, D], BF16, ph) for i in range(2)]
                sdt = [sb("c_dt%d" % i, [128, 8, 2], F32, ph) for i in range(2)]
                MKB = sb("c_mkb", [128, 16, 64], BF16, ph)
                MBR = sb("c_mbr", [128, 16, 64], BF16, ph)
                MKR = sb("c_mkr", [128, 16, 64], BF16, ph)
                TT = sb("c_tt", [128, 16, 64], BF16, ph)
                Pm = [sb("c_pm%d" % i, [128, 8, 64], BF16, ph) for i in range(3)]
                Nm = [sb("c_nm%d" % i, [128, 8, 64], BF16, ph) for i in range(3)]
                RZ = sb("c_rz", [128, 16, 64], BF16, ph)
                rtmp = sb("c_rtmp", [128, 512], F32, ph)
                Tm = [sb("c_tm%d" % i, [128, 8, 64], BF16, ph) for i in range(2)]
                Wz = sb("c_wz", [128, 2, D], BF16, ph)
                Uz = sb("c_uz", [128, 2, D], BF16, ph)
                Ufz = sb("c_ufz", [128, 2, D], F32, ph)
                ST = sb("c_st", [128, 8, 64], F32, ph)
                Sbz = sb("c_sbz", [128, 8, 2, 64], BF16, ph)
                S0 = sb("c_s0", [64, 16, 64], F32, ph)
                SO = sb("c_so", [64, 8, 128], F32, ph)
                Yt = [sb("c_y%d" % i, [128, 8, 128], F32, ph) for i in range(2)]
                for i in range(2):
                    for tz, nm in ((satz[i], "atz"), (srtz[i], "rtz"), (sbhz[i], "bhz"), (skhz[i], "khz"), (svhz[i], "vhz"), (svfz[i], "vfz")):
                        op("dve", lambda e, tz=tz: e.memset(tz[:], 0.0), w=["ld%d" % i])
                op("dve", lambda e: e.memset(Wz[:], 0.0), w=["Wz"])
                op("dve", lambda e: e.memset(Uz[:], 0.0), w=["Uz"])
                op("dve", lambda e: e.memset(Ufz[:], 0.0), w=["Ufz"])
                op("dve", lambda e: e.memset(Sbz[:], 0.0), w=["Sbz"])
                H0 = slice(0, 64)
                H1 = slice(64, 128)
                HS = (H0, H1)

                def copy_state_bf16():
                    op("act", lambda e: e.activation(out=Sbz[H0, :, 0, :], in_=ST[H0, :, :], func=AF.Copy), r=["ST"], w=["Sbz"])
                    op("act", lambda e: e.activation(out=Sbz[H1, :, 1, :], in_=ST[H1, :, :], func=AF.Copy), r=["ST"], w=["Sbz"])

                lt = 0
                CSEC = int(os.environ.get("C_SEC", "99"))
                CSEQ = [int(v) for v in os.environ.get("C_SEQ", "0,1,2").split(",")]
                CDIR = [int(v) for v in os.environ.get("C_DIR", "0,1").split(",")]
                for si, (s0, sl) in enumerate(SEQS):
                    if si not in CSEQ:
                        continue
                    ntl = sl // 128
                    for d in CDIR:
                        if si < 2:
                            op("dve", lambda e: e.memset(ST[:], 0.0), w=["ST"])
                        else:
                            dma("sp", "c_s0", S0[:], st0[d].rearrange("h v k -> v h k"), w=["S0"])
                            for hp in range(8):
                                op("pe", lambda e, hp=hp: e.transpose(PF[0][:, hp * 64:(hp + 1) * 64], S0[:, 2 * hp:2 * hp + 2, :].rearrange("v h k -> v (h k)"), ident64),
                                   r=["S0", "identf"], w=["pf0"], inc=(hp == 7))
                            op("dve", lambda e: e.tensor_copy(out=ST[:], in_=PF[0][:, :].rearrange("p (h v) -> p h v", v=64)), r=["pf0"], w=["ST"])
                        copy_state_bf16()
                        order = range(ntl) if d == 0 else range(ntl - 1, -1, -1)
                        mM, mN, mI = (SU, SL, IU) if d == 0 else (SL, SU, IL)
                        for ti in order:
                            t0 = s0 + ti * 128
                            b = lt % 2
                            lt += 1
                            L = "ld%d" % b
                            loads = []
                            for par in range(2):
                                loads.append((satz[b][HS[par], par, :, :], AT[d][HS[par], :, t0:t0 + 128]))
                                loads.append((srtz[b][HS[par], par, :, :], RT[d][HS[par], :, t0:t0 + 128]))
                                loads.append((sbhz[b][HS[par], par, :], BH[d][t0 + 64 * par:t0 + 64 * par + 64, :]))
                                loads.append((skhz[b][HS[par], par, :], KH[d][t0 + 64 * par:t0 + 64 * par + 64, :]))
                                loads.append((svhz[b][HS[par], par, :], VH[t0 + 64 * par:t0 + 64 * par + 64, :]))
                                loads.append((svfz[b][HS[par], par, :], VHF[t0 + 64 * par:t0 + 64 * par + 64, :]))
                            loads.append((sbt[b][:], fm_dram(BT[d], t0, 128)))
                            loads.append((skt[b][:], fm_dram(KT[d], t0, 128)))
                            loads.append((sdt[b][:], DTO[d][:, :, t0 // 64:t0 // 64 + 2]))
                            for j, (dst, src) in enumerate(loads):
                                dma("sp", "c_ld%d_%d" % (b, j), dst, src, w=[L])
                            atz, rtz, bt_, kt_, bhz, khz, vhz, dt_ = satz[b], srtz[b], sbt[b], skt[b], sbhz[b], skhz[b], svhz[b], sdt[b]
                            vfz = svfz[b]
                            if CSEC < 1:
                                continue
                            for g in range(2):
                                def L_plain(t_):
                                    return lambda h, cs: t_[:, h // 2, cs]

                                def L_z(tz_):
                                    return lambda h, cs: tz_[:, h % 2, h // 2, cs]

                                specs = ((0, L_plain(bt_), L_z(atz)), (1, L_z(atz), L_plain(bt_)), (2, L_plain(kt_), L_z(atz)),
                                         (3, L_plain(bt_), L_z(rtz)), (4, L_plain(kt_), L_z(rtz)))
                                for pi, lf, rf in specs:
                                    for hh in range(8):
                                        h = g * 8 + hh
                                        for cp in range(2):
                                            cs = slice(cp * 64, cp * 64 + 64)
                                            op("pe", lambda e, pi=pi, lf=lf, rf=rf, h=h, hh=hh, cs=cs: e.matmul(
                                                PF[pi][cs, hh * 64:(hh + 1) * 64], lf(h, cs), rf(h, cs), start=True, stop=True, skip_group_check=True),
                                               r=[L], w=["pf%d" % pi], inc=(hh == 7 and cp == 1))
                                gsl = slice(g * 8, g * 8 + 8)
                                op("dve", lambda e: e.tensor_tensor(out=Pm[2][:].rearrange("p h t -> p (h t)"), in0=PF[0][:, :], in1=masks[:, mM, :], op=ALU.mult),
                                   r=["pf0", "masks"], w=["Pm2"])
                                op("dve", lambda e: e.tensor_tensor(out=Nm[2][:].rearrange("p h t -> p (h t)"), in0=PF[1][:, :], in1=masks[:, mN, :], op=ALU.mult),
                                   r=["pf1", "masks"], w=["Nm2"])
                                op("dve", lambda e, gsl=gsl: e.tensor_tensor(out=MKB[:, gsl, :].rearrange("p h t -> p (h t)"), in0=PF[2][:, :], in1=masks[:, mM, :], op=ALU.mult),
                                   r=["pf2", "masks"], w=["MKB"])
                                op("dve", lambda e, gsl=gsl: e.tensor_tensor(out=MBR[:, gsl, :].rearrange("p h t -> p (h t)"), in0=PF[3][:, :], in1=masks[:, mI, :], op=ALU.mult),
                                   r=["pf3", "masks"], w=["MBR"])
                                op("dve", lambda e, gsl=gsl: e.tensor_tensor(out=MKR[:, gsl, :].rearrange("p h t -> p (h t)"), in0=PF[4][:, :], in1=masks[:, mI, :], op=ALU.mult),
                                   r=["pf4", "masks"], w=["MKR"])
                                if CSEC < 2:
                                    continue
                                cur = 2
                                for lev in range(6):
                                    nx = 0 if cur == 2 else 1 - cur
                                    last = lev == 5
                                    first = lev == 0

                                    def blk(kind, cp, bank, cur=cur):
                                        cs = slice(cp * 64, cp * 64 + 64)
                                        for hh in range(8):
                                            if kind == "P":
                                                lt_, rh_ = Nm[cur], Pm[cur]
                                            elif kind == "N":
                                                lt_, rh_ = Pm[cur], Nm[cur]
                                            else:
                                                lt_, rh_ = Nm[cur], Tm[cur]
                                            op("pe", lambda e, hh=hh, cs=cs, lt_=lt_, rh_=rh_, bank=bank: e.matmul(PF[bank][cs, hh * 64:(hh + 1) * 64], lt_[cs, hh, :], rh_[cs, hh, :], start=True, stop=True, skip_group_check=True),
                                               r=["Nm%d" % cur, "Pm%d" % cur, "Tm%d" % cur], w=["pf%d" % bank], inc=(hh == 7))

                                    if first:
                                        op("dve", lambda e, nx=nx, cur=cur: e.tensor_tensor(out=Tm[nx][:].rearrange("p h t -> p (h t)"), in0=Pm[cur][:].rearrange("p h t -> p (h t)"), in1=masks[:, IDS, :], op=ALU.add),
                                           r=["Pm%d" % cur, "masks"], w=["Tm%d" % nx])
                                        blk("P", 0, 0); blk("N", 1, 1); blk("P", 1, 0); blk("N", 0, 1)
                                    elif last:
                                        blk("T", 0, 2); blk("T", 1, 3)
                                    else:
                                        blk("P", 0, 0); blk("N", 1, 1); blk("T", 0, 2); blk("P", 1, 0); blk("N", 0, 1); blk("T", 1, 2)
                                    if not first:
                                        if last:
                                            op("dve", lambda e, nx=nx, cur=cur: e.tensor_tensor(out=Tm[nx][H0].rearrange("p h t -> p (h t)"), in0=PF[2][H0, :], in1=Tm[cur][H0].rearrange("p h t -> p (h t)"), op=ALU.add),
                                               r=["pf2", "Tm%d" % cur], w=["Tm%d" % nx])
                                            op("dve", lambda e, nx=nx, cur=cur: e.tensor_tensor(out=Tm[nx][H1].rearrange("p h t -> p (h t)"), in0=PF[3][H1, :], in1=Tm[cur][H1].rearrange("p h t -> p (h t)"), op=ALU.add),
                                               r=["pf3", "Tm%d" % cur], w=["Tm%d" % nx])
                                        else:
                                            op("dve", lambda e, nx=nx, cur=cur: e.tensor_tensor(out=Tm[nx][:].rearrange("p h t -> p (h t)"), in0=PF[2][:, :], in1=Tm[cur][:].rearrange("p h t -> p (h t)"), op=ALU.add),
                                               r=["pf2", "Tm%d" % cur], w=["Tm%d" % nx])
                                    if not last:
                                        op("act", lambda e, nx=nx: e.activation(out=Pm[nx][:].rearrange("p h t -> p (h t)"), in_=PF[0][:, :], func=AF.Copy), r=["pf0"], w=["Pm%d" % nx])
                                        op("act", lambda e, nx=nx: e.activation(out=Nm[nx][:].rearrange("p h t -> p (h t)"), in_=PF[1][:, :], func=AF.Copy), r=["pf1"], w=["Nm%d" % nx])
                                    cur = nx
                                op("dve", lambda e, gsl=gsl, cur=cur: e.tensor_copy(out=TT[:, gsl, :], in_=Tm[cur][:]), r=["Tm%d" % cur], w=["TT"])
                                for cp in range(2):
                                    cs = slice(cp * 64, cp * 64 + 64)
                                    for hh in range(8):
                                        op("pe", lambda e, hh=hh, cs=cs, cp=cp, cur=cur: e.matmul(PF[cp][cs, hh * 64:(hh + 1) * 64], Nm[2][cs, hh, :], Tm[cur][cs, hh, :], start=True, stop=True, skip_group_check=True),
                                           r=["Nm2", "Tm%d" % cur], w=["pf%d" % cp], inc=(hh == 7))
                                for cp in range(2):
                                    cs = slice(cp * 64, cp * 64 + 64)
                                    op("dve", lambda e, cs=cs, cp=cp, cur=cur: e.scalar_tensor_tensor(out=rtmp[cs, :], in0=Tm[cur][cs].rearrange("p h t -> p (h t)"), scalar=-1.0, in1=PF[cp][cs, :], op0=ALU.mult, op1=ALU.add),
                                       r=["pf%d" % cp, "Tm%d" % cur], w=["rtmp"])
                                    op("dve", lambda e, cs=cs, gsl=gsl: e.tensor_tensor(out=RZ[cs, gsl, :].rearrange("p h t -> p (h t)"), in0=rtmp[cs, :], in1=masks[cs, IDS, :], op=ALU.add),
                                       r=["rtmp", "masks"], w=["RZ"])
                            if CSEC < 3:
                                continue
                            yb = Yt[b]
                            yk = "Y%d" % b
                            for cp in ((0, 1) if d == 0 else (1, 0)):
                                cs = slice(cp * 64, cp * 64 + 64)
                                for h in range(16):
                                    pw = PF[h // 8]
                                    o = pw[cs, (h % 8) * 64:(h % 8) * 64 + 64]
                                    op("pe", lambda e, o=o, h=h, cs=cs: e.matmul(o, atz[:, h % 2, h // 2, cs], Sbz[:, h // 2, h % 2, :], start=True, stop=False, skip_group_check=True),
                                       r=[L, "Sbz"], w=["pf%d" % (h // 8)], inc=False)
                                    op("pe", lambda e, o=o, h=h, cp=cp: e.matmul(o, MKB[:, h, :], vhz[:, cp, h * 64:(h + 1) * 64], start=False, stop=True, skip_group_check=True),
                                       r=[L, "MKB"], w=["pf%d" % (h // 8)], inc=(h % 8 == 7))
                                op("act", lambda e, cs=cs, cp=cp: e.activation(out=Wz[cs, cp, 0:512], in_=PF[0][cs, :], func=AF.Copy), r=["pf0"], w=["Wz"])
                                op("dve", lambda e, cs=cs, cp=cp: e.tensor_copy(out=Wz[cs, cp, 512:1024], in_=PF[1][cs, :]), r=["pf1"], w=["Wz"])
                                if CSEC < 4:
                                    continue
                                for h in range(16):
                                    pu = PF[2 + h // 8]
                                    op("pe", lambda e, pu=pu, h=h, cs=cs, cp=cp: e.matmul(pu[cs, (h % 8) * 64:(h % 8) * 64 + 64], TT[:, h, :], Wz[:, cp, h * 64:(h + 1) * 64], start=True, stop=True, skip_group_check=True),
                                       r=["TT", "Wz"], w=["pf%d" % (2 + h // 8)], inc=(h % 8 == 7))
                                op("act", lambda e, cs=cs, cp=cp: e.activation(out=Uz[cs, cp, 0:512], in_=PF[2][cs, :], func=AF.Copy), r=["pf2"], w=["Uz"])
                                op("dve", lambda e, cs=cs, cp=cp: e.tensor_copy(out=Uz[cs, cp, 512:1024], in_=PF[3][cs, :]), r=["pf3"], w=["Uz"])
                                op("act", lambda e, cs=cs, cp=cp: e.activation(out=Ufz[cs, cp, 0:512], in_=PF[2][cs, :], func=AF.Copy), r=["pf2"], w=["Ufz"])
                                op("dve", lambda e, cs=cs, cp=cp: e.tensor_copy(out=Ufz[cs, cp, 512:1024], in_=PF[3][cs, :]), r=["pf3"], w=["Ufz"])
                                for h in range(16):
                                    pu = PF[2 + h // 8]
                                    op("pe", lambda e, pu=pu, h=h, cs=cs, cp=cp: e.matmul(pu[cs, (h % 8) * 64:(h % 8) * 64 + 64], RZ[:, h, :], Uz[:, cp, h * 64:(h + 1) * 64], start=True, stop=True, skip_group_check=True),
                                       r=["RZ", "Uz"], w=["pf%d" % (2 + h // 8)], inc=(h % 8 == 7))
                                for half in range(2):
                                    hsl = slice(half * 512, half * 512 + 512)
                                    op("dve", lambda e, cs=cs, cp=cp, half=half, hsl=hsl: e.tensor_tensor(out=Ufz[cs, cp, hsl], in0=PF[2 + half][cs, :], in1=Ufz[cs, cp, hsl], op=ALU.add),
                                       r=["pf%d" % (2 + half), "Ufz"], w=["Ufz"])
                                    op("act", lambda e, cs=cs, cp=cp, hsl=hsl: e.activation(out=Uz[cs, cp, hsl], in_=Ufz[cs, cp, hsl], func=AF.Copy), r=["Ufz"], w=["Uz"])
                                if CSEC < 5:
                                    continue
                                for h in range(16):
                                    hs = HS[h % 2]
                                    hv = slice(h * 64, h * 64 + 64)
                                    oy = PF[4][hs, (h // 2) * 64:(h // 2) * 64 + 64]
                                    op("pe", lambda e, oy=oy, h=h, cs=cs: e.matmul(oy, Sbz[:, h // 2, h % 2, :], rtz[:, h % 2, h // 2, cs], start=True, stop=False, skip_group_check=True),
                                       r=["Sbz", L], w=["pf4"], inc=False)
                                    op("pe", lambda e, oy=oy, hv=hv, h=h, cp=cp: e.matmul(oy, Uz[:, cp, hv], MBR[:, h, :], start=False, stop=False, skip_group_check=True),
                                       r=["Uz", "MBR"], w=["pf4"], inc=False)
                                    op("pe", lambda e, oy=oy, hv=hv, h=h, cp=cp: e.matmul(oy, vhz[:, cp, hv], MKR[:, h, :], start=False, stop=True, skip_group_check=True),
                                       r=[L, "MKR"], w=["pf4"], inc=(h == 15))
                                for h in range(16):
                                    hs = HS[h % 2]
                                    hv = slice(h * 64, h * 64 + 64)
                                    osn = PF[5][hs, (h // 2) * 64:(h // 2) * 64 + 64]
                                    op("pe", lambda e, osn=osn, hv=hv, cp=cp: e.matmul(osn, bhz[:, cp, hv], Ufz[:, cp, hv], start=True, stop=False, skip_group_check=True),
                                       r=[L, "Ufz"], w=["pf5"], inc=False)
                                    op("pe", lambda e, osn=osn, hv=hv, cp=cp: e.matmul(osn, khz[:, cp, hv], vfz[:, cp, hv], start=False, stop=True, skip_group_check=True),
                                       r=[L], w=["pf5"], inc=(h == 15))
                                op("act", lambda e, yb=yb, cs=cs: e.activation(out=yb[:, :, cs], in_=PF[4][:, :].rearrange("p (c t) -> p c t", t=64), func=AF.Copy), r=["pf4"], w=[yk])
                                for hp in range(8):
                                    op("dve", lambda e, hp=hp, cp=cp: e.scalar_tensor_tensor(out=ST[:, hp, :], in0=ST[:, hp, :], scalar=dt_[:, hp, cp:cp + 1], in1=PF[5][:, hp * 64:(hp + 1) * 64], op0=ALU.mult, op1=ALU.add),
                                       r=["ST", L, "pf5"], w=["ST"])
                                copy_state_bf16()
                            if CSEC >= 5:
                                dma("sp", "c_y%d" % b, fm_dram(YS[d], t0, 128), yb[:], r=[yk], w=["YS"])
                        if si < 2 and CSEC >= 6:
                            for hp in range(8):
                                op("pe", lambda e, hp=hp: e.transpose(PF[hp // 4][0:64, (hp % 4) * 128:(hp % 4) * 128 + 128], ST[:, hp, :], identf[:]),
                                   r=["ST", "identf"], w=["pf%d" % (hp // 4)], inc=(hp % 4 == 3))
                            op("dve", lambda e: e.tensor_copy(out=SO[:, 0:4, :].rearrange("p a b -> p (a b)"), in_=PF[0][0:64, :]), r=["pf0"], w=["SO"])
                            op("dve", lambda e: e.tensor_copy(out=SO[:, 4:8, :].rearrange("p a b -> p (a b)"), in_=PF[1][0:64, :]), r=["pf1"], w=["SO"])
                            dma("sp", "c_so", sto[si, d].rearrange("h v k -> v h k"), SO[:].rearrange("v a (h k) -> v (a h) k", k=64), r=["SO"], w=["sto"])
                kb.barrier()

        if run("D"):
            with ExitStack() as ph:
                w_pg = sb("w_pg", [128, 8, 2560], BF16, ph)
                w_o = sb("w_o", [128, 8, D], BF16, ph)
                pws = sb("pws", [128, 4, 256], BF16, ph)
                pms = sb("pms", [128, 2, 4, 2, 256], BF16, ph)
                dma("pool", "d_w0", w_pg[:, :, 0:512], w_in[:, 0:512].rearrange("(c p) n -> p c n", p=128), w=["dw"])
                for q in range(2):
                    dma("pool", "d_w0", w_pg[:, :, 512 + q * 1024:512 + (q + 1) * 1024],
                        w_in[:, 3584 + q * 1024:3584 + (q + 1) * 1024].rearrange("(c p) n -> p c n", p=128), w=["dw"])
                dma("pool", "d_w0", w_o[:], w_out.rearrange("(c p) n -> p c n", p=128), w=["dw"])
                dma("pool", "d_w0", pws[:], pool_w.rearrange("g c n -> c g n"), w=["dw"])
                dma("pool", "d_w0", pms[:], c_pm.rearrange("k g p s t -> p k g s t"), w=["dw"])
                hT = sb("d_hT", [128, 8, 256], BF16, ph)
                yf = sb("d_yf", [128, 8, 256], F32, ph)
                yb2 = sb("d_yb", [128, 8, 256], F32, ph)
                bon = sb("d_bon", [128, 8, 256], F32, ph)
                gt = sb("d_gt", [128, 8, 256], F32, ph)
                xt = sb("d_x", [128, 2, D], F32, ph)
                yc2 = sb("d_yc", [128, 8, 256], F32, ph)
                gA = sb("d_gA", [128, 8, 256], F32, ph)
                gB = sb("d_gB", [128, 8, 256], F32, ph)
                zp = sb("d_zp", [128, 2, 512], BF16, ph)
                mixT = sb("d_mixT", [128, 4, 256], BF16, ph)
                t1 = sb("d_t1", [128, 256], F32, ph)
                mT = sb("d_mT", [128, 8, 256], BF16, ph)
                x1 = sb("d_x1", [128, 2, D], F32, ph)
                tmp = sb("d_tmp", [128, 512], F32, ph)
                ss = sb("d_ss", [128, 1], F32, ph)
                rstd = sb("d_rstd", [128, 1], F32, ph)
                xn = sb("d_xn", [128, D], F32, ph)
                h2 = sb("d_h2", [128, 8, 128], BF16, ph)
                eps_ln = sb("d_eps", [128, 1], F32, ph)
                op("dve", lambda e: e.memset(eps_ln[:], 64e-5), w=["eps_ln"])
                for blk in range(NT // 256):
                    t0 = blk * 256
                    cv = 0 if blk < 2 else 1
                    kind = 0 if blk < 2 else 1
                    dma("sp", "d_l0", hT[:], fm_dram(HT, t0, 256), r=[], w=["hT"])
                    dma("sp", "d_l1", yf[:], fm_dram(YS[0], t0, 256), w=["yf"])
                    dma("sp", "d_l2", yb2[:], fm_dram(YS[1], t0, 256), w=["yb"])
                    dma("sp", "d_l3", bon[:], fm_dram(BON, t0, 256), w=["bon"])
                    dma("sp", "d_l4", gt[:], fm_dram(GT, t0, 256), w=["gt"])
                    dma("sp", "d_l5", xt[:], xc[t0:t0 + 256, :].rearrange("(a p) n -> p a n", p=128), w=["xt"])
                    op("dve", lambda e: e.tensor_tensor(out=yf[:], in0=yf[:], in1=yb2[:], op=ALU.add), r=["yf", "yb"], w=["yf"])
                    for q in range(4):
                        for j in range(2):
                            c = q * 2 + j
                            op("pe", lambda e, c=c, j=j, q=q: e.matmul(PF[q % 2][:, j * 256:(j + 1) * 256], bdones[:], yf[:, c, :], start=True, stop=True),
                               r=["bdones", "yf"], w=["pf%d" % (q % 2)], inc=(j == 1))
                        op("dve", lambda e, q=q: e.scalar_tensor_tensor(out=yc2[:, 2 * q:2 * q + 2, :], in0=PF[q % 2][:, :].rearrange("p (j t) -> p j t", t=256), scalar=-1.0 / 64, in1=yf[:, 2 * q:2 * q + 2, :], op0=ALU.mult, op1=ALU.add),
                           r=["pf%d" % (q % 2), "yf"], w=["yc"])
                    op("act", lambda e: e.activation(out=yb2[:], in_=yc2[:], func=AF.Square), r=["yc"], w=["yb"])
                    for q in range(4):
                        for j in range(2):
                            c = q * 2 + j
                            op("pe", lambda e, c=c, j=j, q=q: e.matmul(PF[q % 2][:, j * 256:(j + 1) * 256], bdones[:], yb2[:, c, :], start=True, stop=True),
                               r=["bdones", "yb"], w=["pf%d" % (q % 2)], inc=(j == 1))
                        op("act", lambda e, q=q: e.activation(out=yf[:, 2 * q:2 * q + 2, :], in_=PF[q % 2][:, :].rearrange("p (j t) -> p j t", t=256), func=AF.Ln, scale=1.0 / 64, bias=eps_ln[:]),
                           r=["pf%d" % (q % 2), "eps_ln"], w=["yf"])
                    op("act", lambda e: e.activation(out=yf[:], in_=yf[:], func=AF.Exp, scale=-0.5), r=["yf"], w=["yf"])
                    op("dve", lambda e: e.tensor_tensor(out=yc2[:], in0=yc2[:], in1=yf[:], op=ALU.mult), r=["yc", "yf"], w=["yc"])
                    for c in range(8):
                        op("dve", lambda e, c=c: e.tensor_scalar(out=yc2[:, c, :], in0=yc2[:, c, :], scalar1=colc("ln_w", c), scalar2=colc("ln_b", c), op0=ALU.mult, op1=ALU.add),
                           r=["yc", "cols"], w=["yc"])
                    op("dve", lambda e: e.tensor_tensor(out=yc2[:], in0=yc2[:], in1=bon[:], op=ALU.add), r=["yc", "bon"], w=["yc"])
                    op("dve", lambda e: e.tensor_tensor(out=yc2[:], in0=yc2[:], in1=gt[:], op=ALU.mult), r=["yc", "gt"], w=["yc"])
                    for fc in range(16):
                        pf = PF[2 + fc % 2]
                        pk = "pf%d" % (2 + fc % 2)
                        for kc in range(8):
                            op("pe", lambda e, fc=fc, kc=kc, pf=pf: e.matmul(pf[:, 0:256], w_pg[:, kc, 512 + fc * 128:512 + (fc + 1) * 128], hT[:, kc, :], start=(kc == 0), stop=(kc == 7)),
                               r=["dw", "hT"], w=[pk], inc=(kc == 7))
                        dst = gA if fc < 8 else gB
                        op("act", lambda e, fc=fc, pf=pf, dst=dst: e.activation(out=dst[:, fc % 8, :], in_=pf[:, 0:256], func=AF.Sigmoid), r=[pk], w=["gA" if fc < 8 else "gB"])
                    op("dve", lambda e: e.tensor_tensor(out=yc2[:], in0=yc2[:], in1=gB[:], op=ALU.mult), r=["yc", "gB"], w=["yc"])
                    for a in range(2):
                        for kc in range(8):
                            op("pe", lambda e, a=a, kc=kc: e.matmul(PF[4][:, :], hT[:, kc, a * 128:(a + 1) * 128], w_pg[:, kc, 0:512], start=(kc == 0), stop=(kc == 7)),
                               r=["dw", "hT"], w=["pf4"], inc=(kc == 7))
                        op("act", lambda e, a=a: e.activation(out=zp[:, a, :], in_=PF[4][:, :], func=AF.Copy), r=["pf4"], w=["zp"])
                    for g in range(4):
                        for a in range(2):
                            op("pe", lambda e, g=g, a=a: e.matmul(PF[5][:, 0:256], zp[:, a, g * 128:(g + 1) * 128], pms[:, kind, g, a, :], start=(a == 0), stop=(a == 1)),
                               r=["zp", "dw"], w=["pf5"], inc=(a == 1))
                        op("act", lambda e, g=g: e.activation(out=mixT[:, g, :], in_=PF[5][:, 0:256], func=AF.Copy), r=["pf5"], w=["mixT"])
                    for dc in range(8):
                        g = dc // 2
                        pf = PF[dc % 2]
                        pk = "pf%d" % (dc % 2)
                        op("pe", lambda e, dc=dc, g=g, pf=pf: e.matmul(pf[:, 0:256], pws[:, g, (dc % 2) * 128:(dc % 2) * 128 + 128], mixT[:, g, :], start=True, stop=True),
                           r=["dw", "mixT"], w=[pk])
                        op("dve", lambda e, dc=dc, pf=pf: e.scalar_tensor_tensor(out=t1[:], in0=pf[:, 0:256], scalar=colc("pool_scale", dc), in1=gA[:, dc, :], op0=ALU.mult, op1=ALU.mult),
                           r=[pk, "gA", "cols"], w=["t1"])
                        op("dve", lambda e, dc=dc: e.tensor_tensor(out=mT[:, dc, :], in0=t1[:], in1=yc2[:, dc, :], op=ALU.add), r=["t1", "yc"], w=["mT"])
                    for a in range(2):
                        for half in range(2):
                            pf = PF[2 + half]
                            pk = "pf%d" % (2 + half)
                            for cc in range(8):
                                op("pe", lambda e, a=a, half=half, cc=cc, pf=pf: e.matmul(pf[:, :], mT[:, cc, a * 128:(a + 1) * 128], w_o[:, cc, half * 512:(half + 1) * 512], start=(cc == 0), stop=(cc == 7)),
                                   r=["mT", "dw"], w=[pk], inc=(cc == 7))
                            op("dve", lambda e, a=a, half=half, pf=pf: e.tensor_tensor(out=tmp[:], in0=pf[:, :], in1=GA1[cv][:, half * 512:(half + 1) * 512], op=ALU.mult),
                               r=[pk, "GA"], w=["tmp"])
                            op("dve", lambda e, a=a, half=half: e.tensor_tensor(out=x1[:, a, half * 512:(half + 1) * 512], in0=tmp[:], in1=xt[:, a, half * 512:(half + 1) * 512], op=ALU.add),
                               r=["tmp", "xt"], w=["x1"])
                        norm_transpose(ph, x1[:, a, :], "x1", A2, B2, cv, lambda c: h2[:, c, :], "h2", (ss, rstd, xn))
                        dma("sp", "d_s0", fm_dram(H2T, t0 + a * 128, 128), h2[:], r=["h2"], w=["H2T"])
                    dma("sp", "d_s1", X1[t0:t0 + 256, :].rearrange("(a p) n -> p a n", p=128), x1[:], r=["x1"], w=["X1"])
                kb.barrier()

        if run("E"):
            with ExitStack() as ph:
                wf1 = sb("wf1", [128, 8, 4096], BF16, ph)
                wf2 = sb("wf2", [128, 32, D], BF16, ph)
                for q in range(8):
                    dma("pool", "e_w1_%d" % q, wf1[:, :, q * 512:(q + 1) * 512], w_ff1[:, q * 512:(q + 1) * 512].rearrange("(c p) n -> p c n", p=128), w=["wf1_%d" % q])
                for q in range(8):
                    dma("pool", "e_w2_%d" % q, wf2[:, q * 4:(q + 1) * 4, :], w_ff2[q * 512:(q + 1) * 512, :].rearrange("(c p) n -> p c n", p=128), w=["wf2_%d" % q])
                h2 = sb("e_h2", [128, 8, 256], BF16, ph)
                x1 = sb("e_x1", [128, 2, D], F32, ph)
                u0 = [sb("e_u0%d" % i, [128, 512], BF16, ph) for i in range(2)]
                uT = sb("e_uT", [128, 32, 256], BF16, ph)
                tmp = sb("e_tmp", [128, 512], F32, ph)
                x2 = sb("e_x2", [128, D], F32, ph)
                junk = sb("e_junk", [128, D], BF16, ph)
                ss = sb("e_ss", [128, 1], F32, ph)
                rstd = sb("e_rstd", [128, 1], F32, ph)
                ot = [sb("e_o%d" % i, [128, D], F32, ph) for i in range(2)]
                for blk in range(NT // 256):
                    t0 = blk * 256
                    cv = 0 if blk < 2 else 1
                    dma("sp", "e_l0", h2[:], fm_dram(H2T, t0, 256), w=["h2"])
                    dma("sp", "e_l1", x1[:], X1[t0:t0 + 256, :].rearrange("(a p) n -> p a n", p=128), w=["x1"])
                    for fp in range(16):
                        pf = PF[fp % 2]
                        pk = "pf%d" % (fp % 2)
                        for j in range(2):
                            fc = fp * 2 + j
                            for kc in range(8):
                                op("pe", lambda e, fc=fc, kc=kc, j=j, pf=pf: e.matmul(pf[:, j * 256:(j + 1) * 256], wf1[:, kc, fc * 128:(fc + 1) * 128], h2[:, kc, :], start=(kc == 0), stop=(kc == 7)),
                                   r=["wf1_%d" % (fc // 4), "h2"], w=[pk], inc=(kc == 7 and j == 1))
                        u = u0[fp % 2]
                        uk = "u0%d" % (fp % 2)
                        op("act", lambda e, pf=pf, u=u: e.activation(out=u[:], in_=pf[:, :], func=AF.Relu), r=[pk], w=[uk])
                        op("dve", lambda e, fp=fp, u=u: e.tensor_tensor(out=uT[:, 2 * fp:2 * fp + 2, :].rearrange("p j t -> p (j t)"), in0=u[:], in1=u[:], op=ALU.mult), r=[uk], w=["uT"])
                    for a in range(2):
                        for half in range(2):
                            pf = PF[2 + half]
                            pk = "pf%d" % (2 + half)
                            for fc in range(32):
                                op("pe", lambda e, a=a, half=half, fc=fc, pf=pf: e.matmul(pf[:, :], uT[:, fc, a * 128:(a + 1) * 128], wf2[:, fc, half * 512:(half + 1) * 512], start=(fc == 0), stop=(fc == 31)),
                                   r=["uT", "wf2_%d" % (fc // 4)], w=[pk], inc=(fc == 31))
                            op("dve", lambda e, half=half, pf=pf: e.tensor_tensor(out=tmp[:], in0=pf[:, :], in1=GA2[cv][:, half * 512:(half + 1) * 512], op=ALU.mult),
                               r=[pk, "GA"], w=["tmp"])
                            op("dve", lambda e, a=a, half=half: e.tensor_tensor(out=x2[:, half * 512:(half + 1) * 512], in0=tmp[:], in1=x1[:, a, half * 512:(half + 1) * 512], op=ALU.add),
                               r=["tmp", "x1"], w=["x2"])
                        op("act", lambda e: e.activation(out=junk[:], in_=x2[:], func=AF.Square, accum_out=ss[:]), r=["x2"], w=["junk", "ss"])
                        op("dve", lambda e: e.tensor_scalar(out=rstd[:], in0=ss[:], scalar1=1.0 / D, scalar2=1e-6, op0=ALU.mult, op1=ALU.add), r=["ss"], w=["rstd"])
                        op("act", lambda e: e.activation(out=rstd[:], in_=rstd[:], func=AF.Sqrt), r=["rstd"], w=["rstd"])
                        op("dve", lambda e: e.reciprocal(out=rstd[:], in_=rstd[:]), r=["rstd"], w=["rstd"])
                        o = ot[a]
                        okey = "ot%d" % a
                        op("act", lambda e, o=o: e.activation(out=o[:], in_=x2[:], func=AF.Copy, scale=rstd[:]), r=["x2", "rstd"], w=[okey])
                        op("dve", lambda e, o=o: e.tensor_tensor(out=o[:], in0=o[:], in1=GF[:], op=ALU.mult), r=[okey, "GF"], w=[okey])
                        dma("sp", "e_o%d" % a, yc[t0 + a * 128:t0 + (a + 1) * 128, :], o[:], r=[okey], w=["yc"])
        kb.finish()
    kb.es.close()
    return nc, kb


def _constants():
    identf = np.eye(128, dtype=np.float32)
    tri = np.zeros((2, 128, 384), np.float32)
    C = -float(np.exp(-0.5))
    for t in range(128):
        for u in range(128):
            if t // 64 != u // 64:
                continue
            tri[0, t, u] = C if t <= u else 0.0
            tri[0, t, 128 + u] = C if t < u else 0.0
            tri[0, t, 256 + u] = C if t > u else 0.0
            tri[1, t, u] = C if t >= u else 0.0
            tri[1, t, 128 + u] = C if t > u else 0.0
            tri[1, t, 256 + u] = C if t < u else 0.0
    s = np.arange(64)[:, None]
    t = np.arange(64)[None, :]
    base = [(s < t), (s > t), (s <= t), (s >= t), (s == t)]
    mask = np.zeros((5, 128, 512), np.float32)
    for m in range(5):
        blk = base[m].astype(np.float32)
        mask[m] = np.tile(np.tile(blk, (2, 1)), (1, 8))
    bd = np.zeros((128, 128), np.float32)
    bd[:64, :64] = 1.0
    bd[64:, 64:] = 1.0
    pm = np.zeros((2, 4, 256, 256), np.float32)
    for kind, lr in ((0, 256), (1, 64)):
        for g, win in enumerate(WINS):
            for tt in range(256):
                r0 = (tt // lr) * lr
                tl = tt - r0
                lo = min(max(tl - win // 2, 0), lr)
                hi = min(max(tl + win - win // 2, 0), lr)
                pm[kind, g, r0 + lo:r0 + hi, tt] += 1.0 / (hi - lo)
                pm[kind, g, tt, tt] -= 1.0
    pm = pm.reshape(2, 4, 2, 128, 256).transpose(0, 1, 3, 2, 4)
    return identf, tri, mask, bd, np.ascontiguousarray(pm)


_CACHE = {}


def kernel(x_prompt, x_sample, state_rwkv, c, c_ctx, w_ada, b_ada, g_norm1, g_norm2, w_in,
           mu_rkv, mu_wag, w_dec0, w_dec1, w_dec2, a0, a1, a2, gate_w1, gate_w2, k_k, k_a,
           r_k, ln_x_w, ln_x_b, pool_w, pool_scale, w_out, w_ff1, w_ff2, g_final):
    in_maps = _prep(x_prompt, x_sample, state_rwkv, c, c_ctx, w_ada, b_ada, g_norm1, g_norm2, w_in,
                    mu_rkv, mu_wag, w_dec0, w_dec1, w_dec2, a0, a1, a2, gate_w1, gate_w2, k_k, k_a,
                    r_k, ln_x_w, ln_x_b, pool_w, pool_scale, w_out, w_ff1, w_ff2, g_final)
    if "nc" not in _CACHE:
        _CACHE["nc"] = build()[0]
    nc = _CACHE["nc"]
    res = run_bass_kernel_spmd(nc, in_maps, core_ids=list(range(8)))
    outs = res.results
    y_prompt = np.zeros((16, 256, D), np.float32)
    y_sample = np.zeros((4, 2048, D), np.float32)
    st_new = np.zeros((16, 1, 2, 16, 64, 64), np.float32)
    for core in range(8):
        yc_ = np.asarray(outs[core]["yc"], dtype=np.float32)
        y_prompt[2 * core] = yc_[0:256]
        y_prompt[2 * core + 1] = yc_[256:512]
        if core < 4:
            y_sample[core] = yc_[512:]
        so = np.asarray(outs[core]["sto"], dtype=np.float32)
        st_new[2 * core, 0] = so[0]
        st_new[2 * core + 1, 0] = so[1]
    return y_prompt, y_sample, st_new


def _prep(x_prompt, x_sample, state_rwkv, c, c_ctx, w_ada, b_ada, g_norm1, g_norm2, w_in,
          mu_rkv, mu_wag, w_dec0, w_dec1, w_dec2, a0, a1, a2, gate_w1, gate_w2, k_k, k_a,
          r_k, ln_x_w, ln_x_b, pool_w, pool_scale, w_out, w_ff1, w_ff2, g_final):
    f = lambda a: np.ascontiguousarray(np.asarray(a, dtype=np.float32))
    x_prompt, x_sample, state_rwkv = f(x_prompt), f(x_sample), f(state_rwkv)
    identf, tri, mask, bd, pm = _constants()
    b_ada6 = f(b_ada).reshape(6, D)
    shared = {
        "w_ada": f(w_ada)[0], "w_in": f(w_in)[0], "w_dec0": f(w_dec0)[0], "w_dec1": f(w_dec1)[0],
        "w_dec2": f(w_dec2)[0], "a1": f(a1)[0], "a2": f(a2)[0], "gate_w1": f(gate_w1)[0],
        "gate_w2": f(gate_w2)[0], "pool_w": f(pool_w)[0], "w_out": f(w_out)[0], "w_ff1": f(w_ff1)[0],
        "w_ff2": f(w_ff2)[0], "g_final": f(g_final).reshape(1, D), "b_ada": b_ada6,
        "c_identb": identf, "c_identf": identf, "c_tri": tri, "c_mask": mask, "c_bd": bd, "c_pm": pm,
    }
    in_maps = []
    for core in range(8):
        li = core % 4
        rows = np.zeros((NR, D), np.float32)
        vals = {"g_norm1": f(g_norm1)[0], "g_norm2": f(g_norm2)[0], "mu_r": f(mu_rkv)[0, 0], "mu_k": f(mu_rkv)[0, 1],
                "mu_v": f(mu_rkv)[0, 2], "mu_w": f(mu_wag)[0, 0], "mu_a": f(mu_wag)[0, 1], "mu_g": f(mu_wag)[0, 2],
                "a0f": f(a0)[0, 0], "a0b": f(a0)[0, 1], "k_k": f(k_k)[0], "k_a": f(k_a)[0], "r_k": f(r_k)[0].reshape(-1),
                "ln_w": f(ln_x_w)[0], "ln_b": f(ln_x_b)[0], "pool_scale": f(pool_scale)[0],
                "sh1": b_ada6[0], "sc1": b_ada6[1], "ga1": b_ada6[2], "sh2": b_ada6[3], "sc2": b_ada6[4], "ga2": b_ada6[5],
                "c0": f(c_ctx), "c1": f(c)[li]}
        for n, v in vals.items():
            rows[RI[n]] = v
        xcore = np.concatenate([x_prompt[2 * core], x_prompt[2 * core + 1], x_sample[li]], axis=0)
        m = dict(shared)
        m["xc"] = np.ascontiguousarray(xcore)
        m["rows"] = rows
        m["st0"] = np.ascontiguousarray(state_rwkv[li, 0])
        in_maps.append(m)
    return in_maps
```

```python
import os
import numpy as np
from contextlib import ExitStack
import concourse.bass as bass
import concourse.mybir as mybir
from concourse.bass_utils import run_bass_kernel_spmd

F32 = mybir.dt.float32
BF16 = mybir.dt.bfloat16
AF = mybir.ActivationFunctionType
ALU = mybir.AluOpType

NT = 2560
NTILE = NT // 128
D = 1024
SEQS = [(0, 256), (256, 256), (512, 2048)]
WINS = (2, 4, 8, 16)
ROWNAMES = ["g_norm1", "g_norm2", "mu_r", "mu_k", "mu_v", "mu_w", "mu_a", "mu_g", "a0f", "a0b",
            "k_k", "k_a", "r_k", "ln_w", "ln_b", "pool_scale", "sh1", "sc1", "ga1", "sh2", "sc2",
            "ga2", "c0", "c1"]
RI = {n: i for i, n in enumerate(ROWNAMES)}
NR = 32


def seq_of_tile(i):
    t = i * 128
    for si, (s0, ln) in enumerate(SEQS):
        if s0 <= t < s0 + ln:
            return si
    raise ValueError


SAME_ENGINE_WAITS = int(os.environ.get("SEW", "1"))


class KB:
    def __init__(self, nc):
        self.nc = nc
        self.es = ExitStack()
        self.E = {"pe": nc.tensor, "act": nc.scalar, "dve": nc.vector, "pool": nc.gpsimd, "sp": nc.sync}
        self.sems = {}
        self.cnt = {}
        self.seen = {e: {} for e in self.E}
        self.lastw = {}
        self.readers = {}
        self.pending = {e: {} for e in self.E}
        self.epoch = {e: 0 for e in self.E}
        self.pe_r = set()
        self.pe_w = set()
        self.n_ins = 0

    def sem(self, name):
        if name not in self.sems:
            self.sems[name] = self.es.enter_context(self.nc.semaphore(name))
            self.cnt[name] = 0
        return self.sems[name]

    def _deps(self, r, w):
        d = {}

        def add(s, v):
            if d.get(s, 0) < v:
                d[s] = v

        for k in r:
            if k in self.lastw:
                add(*self.lastw[k])
        for k in w:
            if k in self.lastw:
                add(*self.lastw[k])
            for s, v in self.readers.get(k, {}).items():
                add(s, v)
        return d

    def _emit_waits(self, eng, d):
        for s, v in self.pending[eng].items():
            if d.get(s, 0) < v:
                d[s] = v
        self.pending[eng] = {}
        for s, v in d.items():
            if eng == "pe" and s.startswith("c_pe"):
                continue
            if SAME_ENGINE_WAITS == 0 and s.startswith("c_" + eng):
                continue
            if self.seen[eng].get(s, 0) >= v:
                continue
            self.E[eng].wait_ge(self.sems[s], v)
            self.seen[eng][s] = v
            self.n_ins += 1

    def _record(self, tok, r, w):
        s, v = tok
        for k in r:
            rd = self.readers.setdefault(k, {})
            if rd.get(s, 0) < v:
                rd[s] = v
        for k in w:
            self.lastw[k] = tok
            self.readers[k] = {}

    def op(self, eng, fn, r=(), w=(), inc=True):
        d = self._deps(r, w)
        self._emit_waits(eng, d)
        ins = fn(self.E[eng])
        self.n_ins += 1
        if eng == "pe" and not inc:
            self.pe_r.update(r)
            self.pe_w.update(w)
            return
        name = "c_%s%d" % (eng, self.epoch[eng])
        sem = self.sem(name)
        self.cnt[name] += 1
        ins.then_inc(sem, 1)
        tok = (name, self.cnt[name])
        if eng == "pe":
            r = set(r) | self.pe_r
            w = set(w) | self.pe_w
            self.pe_r = set()
            self.pe_w = set()
        self._record(tok, r, w)
        if self.cnt[name] >= 30000:
            self.epoch[eng] += 1

    def dma(self, q, slot, out, in_, r=(), w=(), **kw):
        d = self._deps(r, w)
        self._emit_waits(q, d)
        name = "d_" + slot
        sem = self.sem(name)
        self.cnt[name] += 16
        self.E[q].dma_start(out=out, in_=in_, **kw).then_inc(sem, 16)
        self.n_ins += 1
        self._record((name, self.cnt[name]), r, w)

    def barrier(self):
        assert not self.pe_r and not self.pe_w
        toks = {n: c for n, c in self.cnt.items() if c > 0}
        for e in self.E:
            self.pending[e] = dict(toks)
        self.lastw = {}
        self.readers = {}

    def finish(self):
        self.barrier()
        for e in self.E:
            self._emit_waits(e, {})


def build(debug=False, stop_after="E"):
    run = lambda p: "MABCDE".index(p) <= "MABCDE".index(stop_after)
    nc = bass.Bass("TRN2", target_bir_lowering=False)
    kb = KB(nc)
    dram = {}

    def din(name, shape, dt=F32):
        dram[name] = nc.dram_tensor(name, list(shape), dt, kind="ExternalInput").ap()
        return dram[name]

    def dout(name, shape, dt=F32):
        dram[name] = nc.dram_tensor(name, list(shape), dt, kind="ExternalOutput").ap()
        return dram[name]

    def dscr(name, shape, dt):
        kind = "ExternalOutput" if debug else "Internal"
        dram[name] = nc.dram_tensor(name, list(shape), dt, kind=kind).ap()
        return dram[name]

    xc = din("xc", [NT, D])
    rows_in = din("rows", [NR, D])
    st0 = din("st0", [2, 16, 64, 64])
    w_ada = din("w_ada", [D, 6 * D])
    w_in = din("w_in", [D, 5632])
    wd0 = din("w_dec0", [2, D])
    wd1 = din("w_dec1", [2, D, 64])
    wd2 = din("w_dec2", [2, 64, D])
    a1 = din("a1", [2, D, 64])
    a2 = din("a2", [2, 64, D])
    gw1 = din("gate_w1", [D, 128])
    gw2 = din("gate_w2", [128, D])
    pool_w = din("pool_w", [4, 128, 256])
    w_out = din("w_out", [D, D])
    w_ff1 = din("w_ff1", [D, 4 * D])
    w_ff2 = din("w_ff2", [4 * D, D])
    g_final = din("g_final", [1, D])
    b_ada_in = din("b_ada", [6, D])
    c_identb = din("c_identb", [128, 128])
    c_identf = din("c_identf", [128, 128])
    c_tri = din("c_tri", [2, 128, 384])
    c_mask = din("c_mask", [5, 128, 512])
    c_bd = din("c_bd", [128, 128])
    c_pm = din("c_pm", [2, 4, 128, 2, 256])

    yc = dout("yc", [NT, D])
    sto = dout("sto", [2, 2, 16, 64, 64])

    HT = dscr("HT", [128, 8, NT], BF16)
    AT = [dscr("AT%d" % d, [128, 8, NT], BF16) for d in range(2)]
    RT = [dscr("RT%d" % d, [128, 8, NT], BF16) for d in range(2)]
    BT = [dscr("BT%d" % d, [128, 8, NT], BF16) for d in range(2)]
    KT = [dscr("KT%d" % d, [128, 8, NT], BF16) for d in range(2)]
    BH = [dscr("BH%d" % d, [NT, D], F32) for d in range(2)]
    KH = [dscr("KH%d" % d, [NT, D], F32) for d in range(2)]
    DTO = [dscr("DTO%d" % d, [128, 8, NT // 64], F32) for d in range(2)]
    VH = dscr("VH", [NT, D], BF16)
    VHF = dscr("VHF", [NT, D], F32)
    BON = dscr("BON", [128, 8, NT], F32)
    GT = dscr("GT", [128, 8, NT], F32)
    YS = [dscr("YS%d" % d, [128, 8, NT], F32) for d in range(2)]
    X1 = dscr("X1", [NT, D], F32)
    H2T = dscr("H2T", [128, 8, NT], BF16)

    op = kb.op
    dma = kb.dma

    with ExitStack() as gs:
        def sb(name, shape, dt, stack=gs):
            return stack.enter_context(nc.sbuf_tensor("s_" + name, list(shape), dt))

        def ps(name, shape, dt, stack=gs):
            return stack.enter_context(nc.psum_tensor("p_" + name, list(shape), dt))

        PF = [ps("pf%d" % i, [128, 512], F32) for i in range(8)]

        identb = sb("identb", [128, 128], BF16)
        identf = sb("identf", [128, 128], F32)
        bdones = sb("bdones", [128, 128], F32)
        cols = sb("cols", [128, 8, NR], F32)
        A1 = sb("A1", [128, 8, 2], F32)
        B1 = sb("B1", [128, 8, 2], F32)
        A2 = sb("A2", [128, 8, 2], F32)
        B2 = sb("B2", [128, 8, 2], F32)
        GA1 = [sb("GA1_%d" % i, [128, D], F32) for i in range(2)]
        GA2 = [sb("GA2_%d" % i, [128, D], F32) for i in range(2)]
        GF = sb("GF", [128, D], F32)
        muh = sb("muh", [128, 8, 3], F32)
        omm = sb("omm", [128, 8, 3], F32)
        omka = sb("omka", [128, 8], F32)
        eps12 = sb("eps12", [128, 1], F32)

        def col(name):
            return cols[:, :, RI[name]]

        def colc(name, c):
            return cols[:, c, RI[name]:RI[name] + 1]

        dma("pool", "initp", identb[:], c_identb[:, :], w=["identb"])
        dma("sp", "init", identf[:], c_identf[:, :], w=["identf"])
        dma("sp", "init2", bdones[:], c_bd[:, :], w=["bdones"])
        dma("sp", "init3", GF[:], g_final[0:1, :].partition_broadcast(128), w=["GF"])

        if run("M"):
            with ExitStack() as ph:
                rows = sb("rows", [NR, D], F32, ph)
                silu = sb("silu", [128, 8, 2], F32, ph)
                silub = sb("silub", [128, 8, 2], BF16, ph)
                silubc = [sb("silubc%d" % i, [128, 8, 128], F32, ph) for i in range(2)]
                onesb = sb("onesb", [128, 128], F32, ph)
                slab = [sb("mslab%d" % i, [128, 8, 512], F32, ph) for i in range(2)]
                modT = sb("modT", [128, 8, 6, 2], F32, ph)
                gab = [sb("gab%d" % i, [128, D], F32, ph) for i in range(2)]

                dma("sp", "rows", rows[:], rows_in[:, :], w=["rows"])
                dma("sp", "gab0", gab[0][:], b_ada_in[2:3, :].partition_broadcast(128), w=["gab0"])
                dma("sp", "gab1", gab[1][:], b_ada_in[5:6, :].partition_broadcast(128), w=["gab1"])
                for c in range(8):
                    op("pe", lambda e, c=c: e.transpose(PF[c % 2][:, 0:NR], rows[:, c * 128:(c + 1) * 128], identf[0:NR, 0:NR]),
                       r=["rows", "identf"], w=["pf%d" % (c % 2)])
                    op("dve", lambda e, c=c: e.tensor_copy(out=cols[:, c, :], in_=PF[c % 2][:, 0:NR]),
                       r=["pf%d" % (c % 2)], w=["cols"])
                op("dve", lambda e: e.memset(eps12[:], 1e-12), w=["eps12"])
                op("dve", lambda e: e.memset(onesb[:], 1.0), w=["onesb"])
                op("dve", lambda e: e.tensor_scalar(out=muh[:], in0=cols[:, :, RI["mu_r"]:RI["mu_r"] + 3], scalar1=0.5, scalar2=None, op0=ALU.mult),
                   r=["cols"], w=["muh"])
                op("dve", lambda e: e.tensor_scalar(out=omm[:], in0=cols[:, :, RI["mu_r"]:RI["mu_r"] + 3], scalar1=-1.0, scalar2=1.0, op0=ALU.mult, op1=ALU.add),
                   r=["cols"], w=["omm"])
                op("dve", lambda e: e.tensor_scalar(out=omka[:], in0=col("k_a"), scalar1=-1.0, scalar2=1.0, op0=ALU.mult, op1=ALU.add),
                   r=["cols"], w=["omka"])
                op("act", lambda e: e.activation(out=silu[:], in_=cols[:, :, RI["c0"]:RI["c0"] + 2], func=AF.Sigmoid),
                   r=["cols"], w=["silu"])
                op("dve", lambda e: e.tensor_tensor(out=silu[:], in0=silu[:], in1=cols[:, :, RI["c0"]:RI["c0"] + 2], op=ALU.mult),
                   r=["silu", "cols"], w=["silu"])
                op("dve", lambda e: e.tensor_copy(out=silub[:], in_=silu[:]), r=["silu"], w=["silub"])
                for b in range(2):
                    for c in range(8):
                        op("dve", lambda e, b=b, c=c: e.tensor_scalar(out=silubc[b][:, c, :], in0=onesb[:], scalar1=silu[:, c, b:b + 1], scalar2=None, op0=ALU.mult),
                           r=["silu", "onesb"], w=["silubc%d" % b])
                for sl in range(12):
                    sbuf = slab[sl % 2]
                    skey = "mslab%d" % (sl % 2)
                    dma("sp", skey, sbuf[:], w_ada[:, sl * 512:(sl + 1) * 512].rearrange("(c p) n -> p c n", p=128), w=[skey])
                    which = sl // 2
                    pf = PF[sl % 2]
                    for j in range(4):
                        for kc in range(8):
                            op("pe", lambda e, j=j, kc=kc, sbuf=sbuf, pf=pf: e.matmul(pf[:, j * 2:j * 2 + 2], sbuf[:, kc, j * 128:(j + 1) * 128], silu[:, kc, :], start=(kc == 0), stop=(kc == 7)),
                               r=[skey, "silu"], w=["pf%d" % (sl % 2)], inc=(j == 3 and kc == 7))
                    cbase = (sl % 2) * 4
                    op("dve", lambda e, pf=pf, which=which, cbase=cbase: e.tensor_copy(out=modT[:, cbase:cbase + 4, which, :], in_=pf[:, 0:8].rearrange("p (j b) -> p j b", b=2)),
                       r=["pf%d" % (sl % 2)], w=["modT"])
                    if which in (2, 5):
                        gi = 0 if which == 2 else 1
                        for b in range(2):
                            pg = PF[2 + b]
                            for kc in range(8):
                                op("pe", lambda e, kc=kc, b=b, sbuf=sbuf, pg=pg: e.matmul(pg[:, :], silubc[b][:, kc, :], sbuf[:, kc, :], start=(kc == 0), stop=(kc == 7)),
                                   r=[skey, "silubc%d" % b], w=["pf%d" % (2 + b)], inc=(kc == 7))
                            dst = (GA1 if gi == 0 else GA2)[b]
                            half = sl % 2
                            op("dve", lambda e, dst=dst, pg=pg, gi=gi, half=half: e.tensor_tensor(out=dst[:, half * 512:(half + 1) * 512], in0=pg[:, :], in1=gab[gi][:, half * 512:(half + 1) * 512], op=ALU.add),
                               r=["pf%d" % (2 + b), "gab%d" % gi], w=["GA%d_%d" % (gi + 1, b)])
                for b in range(2):
                    for wi, nm in enumerate(["sh1", "sc1", "ga1", "sh2", "sc2", "ga2"]):
                        op("dve", lambda e, b=b, wi=wi, nm=nm: e.tensor_tensor(out=modT[:, :, wi, b], in0=modT[:, :, wi, b], in1=col(nm), op=ALU.add),
                           r=["modT", "cols"], w=["modT"])
                    op("dve", lambda e, b=b: e.scalar_tensor_tensor(out=A1[:, :, b], in0=modT[:, :, 1, b], scalar=1.0, in1=col("g_norm1"), op0=ALU.add, op1=ALU.mult),
                       r=["modT", "cols"], w=["A1"])
                    op("dve", lambda e, b=b: e.scalar_tensor_tensor(out=A2[:, :, b], in0=modT[:, :, 4, b], scalar=1.0, in1=col("g_norm2"), op0=ALU.add, op1=ALU.mult),
                       r=["modT", "cols"], w=["A2"])
                    op("dve", lambda e, b=b: e.tensor_copy(out=B1[:, :, b], in_=modT[:, :, 0, b]), r=["modT"], w=["B1"])
                    op("dve", lambda e, b=b: e.tensor_copy(out=B2[:, :, b], in_=modT[:, :, 3, b]), r=["modT"], w=["B2"])
                kb.barrier()

        def norm_transpose(ph, xt, xkey, Acol, Bcol, cv, out_fm, okey, scratch):
            ss, rstd, xn = scratch
            op("act", lambda e: e.activation(out=xn[:], in_=xt, func=AF.Square, accum_out=ss[:]),
               r=[xkey], w=["nt_xn", "nt_ss"])
            op("dve", lambda e: e.tensor_scalar(out=rstd[:], in0=ss[:], scalar1=1.0 / D, scalar2=1e-6, op0=ALU.mult, op1=ALU.add),
               r=["nt_ss"], w=["nt_rstd"])
            op("act", lambda e: e.activation(out=rstd[:], in_=rstd[:], func=AF.Sqrt), r=["nt_rstd"], w=["nt_rstd"])
            op("dve", lambda e: e.reciprocal(out=rstd[:], in_=rstd[:]), r=["nt_rstd"], w=["nt_rstd"])
            op("act", lambda e: e.activation(out=xn[:], in_=xt, func=AF.Copy, scale=rstd[:]),
               r=[xkey, "nt_rstd"], w=["nt_xn"])
            for c in range(8):
                op("pe", lambda e, c=c: e.transpose(PF[4 + c // 4][:, (c % 4) * 128:(c % 4) * 128 + 128], xn[:, c * 128:(c + 1) * 128], identf[:]),
                   r=["nt_xn", "identf"], w=["pf%d" % (4 + c // 4)], inc=(c % 4 == 3))
            for c in range(8):
                op("dve", lambda e, c=c: e.tensor_scalar(out=out_fm(c), in0=PF[4 + c // 4][:, (c % 4) * 128:(c % 4) * 128 + 128], scalar1=Acol[:, c, cv:cv + 1], scalar2=Bcol[:, c, cv:cv + 1], op0=ALU.mult, op1=ALU.add),
                   r=["pf%d" % (4 + c // 4), "A1", "A2", "B1", "B2"], w=[okey])

        def fm_dram(t, tok0, n):
            return t[:, :, tok0:tok0 + n]

        if run("A"):
            with ExitStack() as ph:
                xt = [sb("a_x%d" % i, [128, D], F32, ph) for i in range(2)]
                hT = [sb("a_h%d" % i, [128, 8, 128], BF16, ph) for i in range(2)]
                ss = sb("a_ss", [128, 1], F32, ph)
                rstd = sb("a_rstd", [128, 1], F32, ph)
                xn = sb("a_xn", [128, D], F32, ph)
                for i in range(NTILE):
                    cv = 0 if i < 4 else 1
                    b = i % 2
                    dma("sp", "a_x%d" % b, xt[b][:], xc[i * 128:(i + 1) * 128, :], w=["a_x%d" % b])
                    norm_transpose(ph, xt[b][:], "a_x%d" % b, A1, B1, cv, lambda c, b=b: hT[b][:, c, :], "a_h%d" % b, (ss, rstd, xn))
                    dma("sp", "a_h%d" % b, fm_dram(HT, i * 128, 128), hT[b][:], r=["a_h%d" % b], w=["HT"])
                kb.barrier()

        if run("B"):
            with ExitStack() as ph:
                w_rkv = sb("w_rkv", [128, 8, 3072], BF16, ph)
                wd1s = sb("wd1s", [128, 2, 8, 64], BF16, ph)
                a1s = sb("a1s", [128, 2, 8, 64], BF16, ph)
                wd2s = sb("wd2s", [64, 2, D], BF16, ph)
                a2s = sb("a2s", [64, 2, D], BF16, ph)
                gw1s = sb("gw1s", [128, 8, 128], BF16, ph)
                gw2s = sb("gw2s", [128, D], BF16, ph)
                WD0 = [sb("WD0_%d" % d, [128, D], F32, ph) for d in range(2)]
                tri = [sb("tri%d" % d, [128, 384], F32, ph) for d in range(2)]
                for q in range(3):
                    dma("pool", "w_rkv%d" % q, w_rkv[:, :, q * 1024:(q + 1) * 1024],
                        w_in[:, 512 + q * 1024:512 + (q + 1) * 1024].rearrange("(c p) n -> p c n", p=128), w=["w_rkv%d" % q])
                for d in range(2):
                    dma("pool", "wsm", wd1s[:, d, :, :], wd1[d].rearrange("(c p) n -> p c n", p=128), w=["wsm"])
                    dma("pool", "wsm", a1s[:, d, :, :], a1[d].rearrange("(c p) n -> p c n", p=128), w=["wsm"])
                    dma("pool", "wsm", wd2s[:, d, :], wd2[d], w=["wsm"])
                    dma("pool", "wsm", a2s[:, d, :], a2[d], w=["wsm"])
                    dma("sp", "wsm2", WD0[d][:], wd0[d:d + 1, :].partition_broadcast(128), w=["wsm2"])
                    dma("sp", "wsm2", tri[d][:], c_tri[d], w=["wsm2"])
                dma("pool", "wsm", gw1s[:], gw1.rearrange("(c p) n -> p c n", p=128), w=["wsm"])
                dma("pool", "wsm", gw2s[:], gw2[:, :], w=["wsm"])
                WK = ["w_rkv0", "w_rkv1", "w_rkv2"]

                hx = sb("b_hx", [128, 8, 130], BF16, ph)
                hd = sb("b_hd", [128, 8, 128], F32, ph)
                xw = sb("b_xw", [128, 8, 128], BF16, ph)
                xa = sb("b_xa", [128, 8, 128], BF16, ph)
                xg = sb("b_xg", [128, 8, 128], BF16, ph)
                tw = sb("b_tw", [64, 2, 128], BF16, ph)
                aw = sb("b_aw", [64, 2, 128], BF16, ph)
                gw = sb("b_gw", [128, 128], BF16, ph)
                zs = [sb("b_zs%d" % i, [128, 130], F32, ph) for i in range(2)]
                t2 = [sb("b_t2%d" % i, [128, 128], F32, ph) for i in range(2)]
                rkv = sb("b_rkv", [128, 24, 128], F32, ph)
                kraw = sb("b_kraw", [128, 8, 128], F32, ph)
                sq = sb("b_sq", [128, 8, 128], F32, ph)
                kk = sb("b_kk", [128, 8, 128], F32, ph)
                bon = sb("b_bon", [128, 8, 128], F32, ph)
                gt = sb("b_gt", [128, 8, 128], F32, ph)
                vb = sb("b_vb", [128, 8, 128], BF16, ph)
                vht = sb("b_vht", [128, D], BF16, ph)
                sig = sb("b_sig", [128, D], F32, ph)
                afm = sb("b_a", [128, 8, 128], F32, ph)
                beta = sb("b_beta", [128, 8, 128], F32, ph)
                kd = sb("b_kd", [128, 8, 128], F32, ph)
                ee = [sb("b_e%d" % i, [128, 128], F32, ph) for i in range(4)]
                o_at = sb("b_oat", [128, 8, 128], BF16, ph)
                o_rt = sb("b_ort", [128, 8, 128], BF16, ph)
                o_bt = sb("b_obt", [128, 8, 128], BF16, ph)
                o_kt = sb("b_okt", [128, 8, 128], BF16, ph)
                o_bh = sb("b_obh", [128, 8, 128], F32, ph)
                o_kh = sb("b_okh", [128, 8, 128], F32, ph)
                bht = sb("b_bht", [128, D], F32, ph)
                kht = sb("b_kht", [128, D], F32, ph)
                vhf = sb("b_vhf", [128, D], F32, ph)
                dtt = sb("b_dtt", [128, 8, 2], F32, ph)

                BSEC = int(os.environ.get("B_SEC", "99"))
                for i in range(int(os.environ.get("B_MAXT", NTILE))):
                    t0 = i * 128
                    s0, sl = SEQS[seq_of_tile(i)]
                    has_prev = t0 > s0
                    has_next = t0 + 128 < s0 + sl
                    lo = t0 - 1 if has_prev else t0
                    hi = t0 + 129 if has_next else t0 + 128
                    if not has_prev:
                        op("dve", lambda e: e.memset(hx[:, :, 0:1], 0.0), w=["hx"])
                    if not has_next:
                        op("dve", lambda e: e.memset(hx[:, :, 129:130], 0.0), w=["hx"])
                    dma("sp", "b_hx", hx[:, :, (lo - (t0 - 1)):(hi - (t0 - 1))], HT[:, :, lo:hi], r=["HT"], w=["hx"])
                    if BSEC < 1:
                        continue
                    op("dve", lambda e: e.tensor_tensor(out=hd[:], in0=hx[:, :, 0:128], in1=hx[:, :, 2:130], op=ALU.add), r=["hx"], w=["hd"])
                    op("dve", lambda e: e.scalar_tensor_tensor(out=hd[:], in0=hd[:], scalar=0.5, in1=hx[:, :, 1:129], op0=ALU.mult, op1=ALU.subtract),
                       r=["hd", "hx"], w=["hd"])
                    for c in range(8):
                        for nm, dst, key in (("mu_w", xw, "xw"), ("mu_a", xa, "xa"), ("mu_g", xg, "xg")):
                            op("dve", lambda e, c=c, nm=nm, dst=dst: e.scalar_tensor_tensor(out=dst[:, c, :], in0=hd[:, c, :], scalar=colc(nm, c), in1=hx[:, c, 1:129], op0=ALU.mult, op1=ALU.add),
                               r=["hd", "hx"], w=[key])
                    S1 = int(os.environ.get("S1", "9"))
                    if S1 < 1:
                        continue
                    for d in range(2):
                        for kc in range(8):
                            op("pe", lambda e, d=d, kc=kc: e.matmul(PF[0][0:64, d * 128:(d + 1) * 128], wd1s[:, d, kc, :], xw[:, kc, :], start=(kc == 0), stop=(kc == 7)),
                               r=["wsm", "xw"], w=["pf0"], inc=(kc == 7))
                        for kc in range(8):
                            op("pe", lambda e, d=d, kc=kc: e.matmul(PF[0][0:64, 256 + d * 128:256 + (d + 1) * 128], a1s[:, d, kc, :], xa[:, kc, :], start=(kc == 0), stop=(kc == 7)),
                               r=["wsm", "xa"], w=["pf0"], inc=(kc == 7))
                    if S1 < 2:
                        continue
                    op("act", lambda e: e.activation(out=tw[:].rearrange("p d t -> p (d t)"), in_=PF[0][0:64, 0:256], func=AF.Tanh), r=["pf0"], w=["tw"])
                    if True:
                        op("act", lambda e: e.activation(out=aw[:].rearrange("p d t -> p (d t)"), in_=PF[0][0:64, 256:512], func=AF.Copy), r=["pf0"], w=["aw"])
                    else:
                        op("dve", lambda e: e.tensor_copy(out=aw[:].rearrange("p d t -> p (d t)"), in_=PF[0][0:64, 256:512]), r=["pf0"], w=["aw"])
                    if S1 < 3:
                        continue
                    for kc in range(8):
                        op("pe", lambda e, kc=kc: e.matmul(PF[1][:, 0:128], gw1s[:, kc, :], xg[:, kc, :], start=(kc == 0), stop=(kc == 7)),
                           r=["wsm", "xg"], w=["pf1"], inc=(kc == 7))
                    op("act", lambda e: e.activation(out=gw[:], in_=PF[1][:, 0:128], func=AF.Sigmoid), r=["pf1"], w=["gw"])
                    if S1 < 4:
                        continue
                    for half in range(2):
                        for j in range(4):
                            fc = half * 4 + j
                            op("pe", lambda e, fc=fc, j=j: e.matmul(PF[1][:, j * 128:(j + 1) * 128], gw2s[:, fc * 128:(fc + 1) * 128], gw[:], start=True, stop=True),
                               r=["wsm", "gw"], w=["pf1"], inc=(j == 3))
                        op("act", lambda e, half=half: e.activation(out=gt[:, half * 4:half * 4 + 4, :], in_=PF[1][:, :].rearrange("p (j t) -> p j t", t=128), func=AF.Copy),
                           r=["pf1"], w=["gt"])
                    if S1 < 5:
                        continue
                    dma("sp", "b_gt", fm_dram(GT, t0, 128), gt[:], r=["gt"], w=["GT"])
                    if BSEC < 2:
                        continue
                    for fc in range(24):
                        pf = PF[2 + fc % 2]
                        pk = "pf%d" % (2 + fc % 2)
                        z = zs[fc % 2]
                        zk = "zs%d" % (fc % 2)
                        tt = t2[fc % 2]
                        tk = "t2%d" % (fc % 2)
                        for kc in range(8):
                            op("pe", lambda e, fc=fc, kc=kc, pf=pf: e.matmul(pf[:, 0:130], w_rkv[:, kc, fc * 128:(fc + 1) * 128], hx[:, kc, :], start=(kc == 0), stop=(kc == 7)),
                               r=[WK[fc // 8], "hx"], w=[pk], inc=(kc == 7))
                        op("act", lambda e, pf=pf, z=z: e.activation(out=z[:], in_=pf[:, 0:130], func=AF.Copy), r=[pk], w=[zk])
                        op("dve", lambda e, z=z, tt=tt: e.tensor_tensor(out=tt[:], in0=z[:, 0:128], in1=z[:, 2:130], op=ALU.add), r=[zk], w=[tk])
                        op("dve", lambda e, tt=tt, fc=fc: e.tensor_scalar(out=tt[:], in0=tt[:], scalar1=muh[:, fc % 8, fc // 8:fc // 8 + 1], scalar2=None, op0=ALU.mult),
                           r=[tk, "muh"], w=[tk])
                        op("dve", lambda e, z=z, tt=tt, fc=fc: e.scalar_tensor_tensor(out=rkv[:, fc, :], in0=z[:, 1:129], scalar=omm[:, fc % 8, fc // 8:fc // 8 + 1], in1=tt[:], op0=ALU.mult, op1=ALU.add),
                           r=[zk, tk, "omm"], w=["rkv"])
                    R = lambda c: rkv[:, c, :]
                    Kf = lambda c: rkv[:, 8 + c, :]
                    Vf = lambda c: rkv[:, 16 + c, :]
                    if BSEC < 3:
                        continue
                    for c in range(8):
                        op("dve", lambda e, c=c: e.tensor_scalar(out=kraw[:, c, :], in0=Kf(c), scalar1=colc("k_k", c), scalar2=None, op0=ALU.mult), r=["rkv"], w=["kraw"])
                    op("act", lambda e: e.activation(out=sq[:], in_=kraw[:], func=AF.Square), r=["kraw"], w=["sq"])
                    for half in range(2):
                        for j in range(4):
                            c = half * 4 + j
                            op("pe", lambda e, c=c, j=j: e.matmul(PF[4][:, j * 128:(j + 1) * 128], bdones[:], sq[:, c, :], start=True, stop=True),
                               r=["bdones", "sq"], w=["pf4"], inc=(j == 3))
                        op("act", lambda e, half=half: e.activation(out=kk[:, half * 4:half * 4 + 4, :], in_=PF[4][:, :].rearrange("p (j t) -> p j t", t=128), func=AF.Ln, bias=eps12[:]),
                           r=["pf4", "eps12"], w=["kk"])
                    op("act", lambda e: e.activation(out=kk[:], in_=kk[:], func=AF.Exp, scale=-0.5), r=["kk"], w=["kk"])
                    op("dve", lambda e: e.tensor_tensor(out=kk[:], in0=kk[:], in1=kraw[:], op=ALU.mult), r=["kk", "kraw"], w=["kk"])
                    for c in range(8):
                        op("dve", lambda e, c=c: e.scalar_tensor_tensor(out=sq[:, c, :], in0=R(c), scalar=colc("r_k", c), in1=Kf(c), op0=ALU.mult, op1=ALU.mult),
                           r=["rkv"], w=["sq"])
                    for half in range(2):
                        for j in range(4):
                            c = half * 4 + j
                            op("pe", lambda e, c=c, j=j: e.matmul(PF[4][:, j * 128:(j + 1) * 128], bdones[:], sq[:, c, :], start=True, stop=True),
                               r=["bdones", "sq"], w=["pf4"], inc=(j == 3))
                        op("dve", lambda e, half=half: e.tensor_tensor(out=bon[:, half * 4:half * 4 + 4, :], in0=PF[4][:, :].rearrange("p (j t) -> p j t", t=128), in1=rkv[:, 16 + half * 4:16 + half * 4 + 4, :], op=ALU.mult),
                           r=["pf4", "rkv"], w=["bon"])
                    dma("sp", "b_bon", fm_dram(BON, t0, 128), bon[:], r=["bon"], w=["BON"])
                    if BSEC < 4:
                        continue
                    for c in range(8):
                        op("pe", lambda e, c=c: e.transpose(PF[c // 4][:, (c % 4) * 128:(c % 4) * 128 + 128], rkv[:, 16 + c, :], identf[:]),
                           r=["rkv", "identf"], w=["pf%d" % (c // 4)], inc=(c % 4 == 3))
                    op("dve", lambda e: e.tensor_copy(out=vhf[:, 0:512], in_=PF[0][:, :]), r=["pf0"], w=["vhf"])
                    op("act", lambda e: e.activation(out=vhf[:, 512:1024], in_=PF[1][:, :], func=AF.Copy), r=["pf1"], w=["vhf"])
                    dma("sp", "b_vhf", VHF[t0:t0 + 128, :], vhf[:], r=["vhf"], w=["VHF"])
                    op("act", lambda e: e.activation(out=vht[:], in_=vhf[:], func=AF.Copy), r=["vhf"], w=["vht"])
                    dma("sp", "b_vht", VH[t0:t0 + 128, :], vht[:], r=["vht"], w=["VH"])
                    if BSEC < 5:
                        continue
                    for d in range(2):
                        for half in range(2):
                            op("pe", lambda e, d=d, half=half: e.matmul(PF[0][:, :], tw[:, d, :], wd2s[:, d, half * 512:(half + 1) * 512], start=True, stop=True),
                               r=["tw", "wsm"], w=["pf0"])
                            op("dve", lambda e, d=d, half=half: e.tensor_tensor(out=sig[:, half * 512:(half + 1) * 512], in0=PF[0][:, :], in1=WD0[d][:, half * 512:(half + 1) * 512], op=ALU.add),
                               r=["pf0", "wsm2"], w=["sig"])
                        op("act", lambda e: e.activation(out=sig[:], in_=sig[:], func=AF.Sigmoid), r=["sig"], w=["sig"])
                        a0n = "a0f" if d == 0 else "a0b"
                        for c in range(8):
                            pc = PF[2 + c % 2]
                            pck = "pf%d" % (2 + c % 2)
                            pa = PF[4 + c % 2]
                            pak = "pf%d" % (4 + c % 2)
                            op("pe", lambda e, c=c, d=d, pc=pc: e.matmul(pc[:, 0:384], sig[:, c * 128:(c + 1) * 128], tri[d][:], start=True, stop=True),
                               r=["sig", "wsm2"], w=[pck])
                            op("pe", lambda e, c=c, d=d, pa=pa: e.matmul(pa[:, 0:128], a2s[:, d, c * 128:(c + 1) * 128], aw[:, d, :], start=True, stop=True),
                               r=["wsm", "aw"], w=[pak])
                            op("act", lambda e, c=c, pa=pa, a0n=a0n: e.activation(out=afm[:, c, :], in_=pa[:, 0:128], func=AF.Sigmoid, bias=colc(a0n, c)),
                               r=[pak, "cols"], w=["afm"])
                            op("act", lambda e, pc=pc: e.activation(out=ee[0][:], in_=pc[:, 128:256], func=AF.Exp), r=[pck], w=["e0"])
                            op("act", lambda e, pc=pc: e.activation(out=ee[1][:], in_=pc[:, 0:128], func=AF.Exp), r=[pck], w=["e1"])
                            op("act", lambda e, pc=pc: e.activation(out=ee[2][:], in_=pc[:, 0:128], func=AF.Exp, scale=-1.0), r=[pck], w=["e2"])
                            op("act", lambda e, pc=pc: e.activation(out=ee[3][:], in_=pc[:, 256:384], func=AF.Exp), r=[pck], w=["e3"])
                            if d == 0:
                                src = pc[:, 63:128:64]
                            else:
                                src = pc[:, 0:128:64]
                            op("act", lambda e, c=c, src=src: e.activation(out=dtt[:, c, :], in_=src, func=AF.Exp), r=[pck], w=["dtt"])
                            op("dve", lambda e, c=c: e.tensor_tensor(out=beta[:, c, :], in0=kk[:, c, :], in1=afm[:, c, :], op=ALU.mult), r=["kk", "afm"], w=["beta"])
                            op("dve", lambda e, c=c: e.tensor_scalar(out=kd[:, c, :], in0=afm[:, c, :], scalar1=colc("k_a", c), scalar2=omka[:, c:c + 1], op0=ALU.mult, op1=ALU.add),
                               r=["afm", "cols", "omka"], w=["kd"])
                            op("dve", lambda e, c=c: e.tensor_tensor(out=kd[:, c, :], in0=kd[:, c, :], in1=Kf(c), op=ALU.mult), r=["kd", "rkv"], w=["kd"])
                            op("dve", lambda e, c=c: e.scalar_tensor_tensor(out=o_at[:, c, :], in0=kk[:, c, :], scalar=-1.0, in1=ee[0][:], op0=ALU.mult, op1=ALU.mult),
                               r=["kk", "e0"], w=["o_at"])
                            op("dve", lambda e, c=c: e.tensor_tensor(out=o_rt[:, c, :], in0=R(c), in1=ee[1][:], op=ALU.mult), r=["rkv", "e1"], w=["o_rt"])
                            op("dve", lambda e, c=c: e.tensor_tensor(out=o_bt[:, c, :], in0=beta[:, c, :], in1=ee[2][:], op=ALU.mult), r=["beta", "e2"], w=["o_bt"])
                            op("dve", lambda e, c=c: e.tensor_tensor(out=o_kt[:, c, :], in0=kd[:, c, :], in1=ee[2][:], op=ALU.mult), r=["kd", "e2"], w=["o_kt"])
                            op("dve", lambda e, c=c: e.tensor_tensor(out=o_bh[:, c, :], in0=beta[:, c, :], in1=ee[3][:], op=ALU.mult), r=["beta", "e3"], w=["o_bh"])
                            op("dve", lambda e, c=c: e.tensor_tensor(out=o_kh[:, c, :], in0=kd[:, c, :], in1=ee[3][:], op=ALU.mult), r=["kd", "e3"], w=["o_kh"])
                        for src_t, dst_t, sk, dk, pb0 in ((o_bh, bht, "o_bh", "bht", 0), (o_kh, kht, "o_kh", "kht", 2)):
                            for c in range(8):
                                op("pe", lambda e, c=c, src_t=src_t, pb0=pb0: e.transpose(PF[pb0 + c // 4][:, (c % 4) * 128:(c % 4) * 128 + 128], src_t[:, c, :], identf[:]),
                                   r=[sk, "identf"], w=["pf%d" % (pb0 + c // 4)], inc=(c % 4 == 3))
                            op("dve", lambda e, dst_t=dst_t, pb0=pb0: e.tensor_copy(out=dst_t[:, 0:512], in_=PF[pb0][:, :]), r=["pf%d" % pb0], w=[dk])
                            op("act", lambda e, dst_t=dst_t, pb0=pb0: e.activation(out=dst_t[:, 512:1024], in_=PF[pb0 + 1][:, :], func=AF.Copy), r=["pf%d" % (pb0 + 1)], w=[dk])
                        dma("sp", "b_o0", fm_dram(AT[d], t0, 128), o_at[:], r=["o_at"], w=["AT"])
                        dma("sp", "b_o1", fm_dram(RT[d], t0, 128), o_rt[:], r=["o_rt"], w=["RT"])
                        dma("sp", "b_o2", fm_dram(BT[d], t0, 128), o_bt[:], r=["o_bt"], w=["BT"])
                        dma("sp", "b_o3", fm_dram(KT[d], t0, 128), o_kt[:], r=["o_kt"], w=["KT"])
                        dma("sp", "b_o4", BH[d][t0:t0 + 128, :], bht[:], r=["bht"], w=["BH"])
                        dma("sp", "b_o5", KH[d][t0:t0 + 128, :], kht[:], r=["kht"], w=["KH"])
                        dma("sp", "b_o6", DTO[d][:, :, 2 * i:2 * i + 2], dtt[:], r=["dtt"], w=["DTO"])
                kb.barrier()

        if run("C"):
            with ExitStack() as ph:
                masks = sb("c_masks", [128, 5, 512], BF16, ph)
                dma("pool", "c_mask", masks[:], c_mask.rearrange("m p n -> p m n"), w=["masks"])
                ident64 = identf[0:64, 0:64]
                SU, SL, IU, IL, IDS = range(5)
                satz = [sb("c_atz%d" % i, [128, 2, 8, 128], BF16, ph) for i in range(2)]
                srtz = [sb("c_rtz%d" % i, [128, 2, 8, 128], BF16, ph) for i in range(2)]
                sbt = [sb("c_bt%d" % i, [128, 8, 128], BF16, ph) for i in range(2)]
                skt = [sb("c_kt%d" % i, [128, 8, 128], BF16, ph) for i in range(2)]
                sbhz = [sb("c_bhz%d" % i, [128, 2, D], F32, ph) for i in range(2)]
                skhz = [sb("c_khz%d" % i, [128, 2, D], F32, ph) for i in range(2)]
                svfz = [sb("c_vfz%d" % i, [128, 2, D], F32, ph) for i in range(2)]
                svhz = [sb("c_vhz%d" % i, [128, 2, D], BF16, ph) for i in range(2)]
                sdt = [sb("c_dt%d" % i, [128, 8, 2], F32, ph) for i in range(2)]
                MKB = [sb("c_mkb%d" % i, [128, 16, 64], BF16, ph) for i in range(2)]
                MBR = [sb("c_mbr%d" % i, [128, 16, 64], BF16, ph) for i in range(2)]
                MKR = [sb("c_mkr%d" % i, [128, 16, 64], BF16, ph) for i in range(2)]
                TT = [sb("c_tt%d" % i, [128, 16, 64], BF16, ph) for i in range(2)]
                RZ = [sb("c_rz%d" % i, [128, 16, 64], BF16, ph) for i in range(2)]
                Pm = [sb("c_pm%d" % i, [128, 8, 64], BF16, ph) for i in range(3)]
                Nm = [sb("c_nm%d" % i, [128, 8, 64], BF16, ph) for i in range(3)]
                rtmp = sb("c_rtmp", [128, 512], F32, ph)
                Tm = [sb("c_tm%d" % i, [128, 8, 64], BF16, ph) for i in range(2)]
                Wz = sb("c_wz", [128, 2, D], BF16, ph)
                Uz = sb("c_uz", [128, 2, D], BF16, ph)
                Ufz = sb("c_ufz", [128, 2, D], F32, ph)
                ST = sb("c_st", [128, 8, 64], F32, ph)
                Sbz = sb("c_sbz", [128, 8, 2, 64], BF16, ph)
                S0 = sb("c_s0", [64, 16, 64], F32, ph)
                SO = sb("c_so", [64, 8, 128], F32, ph)
                Yt = [sb("c_y%d" % i, [128, 8, 128], F32, ph) for i in range(2)]
                for i in range(2):
                    for tz in (satz[i], srtz[i], sbhz[i], skhz[i], svhz[i], svfz[i]):
                        op("dve", lambda e, tz=tz: e.memset(tz[:], 0.0), w=["ld%d" % i])
                op("dve", lambda e: e.memset(Wz[:], 0.0), w=["Wz"])
                op("dve", lambda e: e.memset(Uz[:], 0.0), w=["Uz"])
                op("dve", lambda e: e.memset(Ufz[:], 0.0), w=["Ufz"])
                op("dve", lambda e: e.memset(Sbz[:], 0.0), w=["Sbz"])
                H0 = slice(0, 64)
                H1 = slice(64, 128)
                HS = (H0, H1)

                def copy_state_bf16():
                    op("act", lambda e: e.activation(out=Sbz[H0, :, 0, :], in_=ST[H0, :, :], func=AF.Copy), r=["ST"], w=["Sbz"])
                    op("act", lambda e: e.activation(out=Sbz[H1, :, 1, :], in_=ST[H1, :, :], func=AF.Copy), r=["ST"], w=["Sbz"])

                def prep(si, d, ti, b):
                    s0 = SEQS[si][0]
                    t0 = s0 + ti * 128
                    L = "ld%d" % b
                    mM, mN, mI = (SU, SL, IU) if d == 0 else (SL, SU, IL)
                    loads = []
                    for par in range(2):
                        loads.append((satz[b][HS[par], par, :, :], AT[d][HS[par], :, t0:t0 + 128]))
                        loads.append((srtz[b][HS[par], par, :, :], RT[d][HS[par], :, t0:t0 + 128]))
                        loads.append((sbhz[b][HS[par], par, :], BH[d][t0 + 64 * par:t0 + 64 * par + 64, :]))
                        loads.append((skhz[b][HS[par], par, :], KH[d][t0 + 64 * par:t0 + 64 * par + 64, :]))
                        loads.append((svhz[b][HS[par], par, :], VH[t0 + 64 * par:t0 + 64 * par + 64, :]))
                        loads.append((svfz[b][HS[par], par, :], VHF[t0 + 64 * par:t0 + 64 * par + 64, :]))
                    loads.append((sbt[b][:], fm_dram(BT[d], t0, 128)))
                    loads.append((skt[b][:], fm_dram(KT[d], t0, 128)))
                    loads.append((sdt[b][:], DTO[d][:, :, t0 // 64:t0 // 64 + 2]))
                    for j, (dst, src) in enumerate(loads):
                        dma("sp", "c_ld%d_%d" % (b, j), dst, src, w=[L])
                    yield
                    atz, rtz, bt_, kt_ = satz[b], srtz[b], sbt[b], skt[b]
                    for g in range(2):
                        def L_plain(t_):
                            return lambda h, cs: t_[:, h // 2, cs]

                        def L_z(tz_):
                            return lambda h, cs: tz_[:, h % 2, h // 2, cs]

                        gsl = slice(g * 8, g * 8 + 8)

                        def mtype(bank, lf, rf):
                            for hh in range(8):
                                h = g * 8 + hh
                                for cp in range(2):
                                    cs = slice(cp * 64, cp * 64 + 64)
                                    op("pe", lambda e, lf=lf, rf=rf, h=h, hh=hh, cs=cs, bank=bank: e.matmul(
                                        PF[bank][cs, hh * 64:(hh + 1) * 64], lf(h, cs), rf(h, cs), start=True, stop=True, skip_group_check=True),
                                       r=[L], w=["pf%d" % bank], inc=(hh == 7 and cp == 1))

                        mtype(0, L_plain(bt_), L_z(atz))
                        mtype(1, L_z(atz), L_plain(bt_))
                        mtype(2, L_plain(kt_), L_z(atz))
                        mtype(3, L_plain(bt_), L_z(rtz))
                        op("dve", lambda e: e.tensor_tensor(out=Pm[2][:].rearrange("p h t -> p (h t)"), in0=PF[0][:, :], in1=masks[:, mM, :], op=ALU.mult),
                           r=["pf0", "masks"], w=["Pm2"])
                        op("dve", lambda e: e.tensor_tensor(out=Nm[2][:].rearrange("p h t -> p (h t)"), in0=PF[1][:, :], in1=masks[:, mN, :], op=ALU.mult),
                           r=["pf1", "masks"], w=["Nm2"])
                        op("dve", lambda e: e.tensor_tensor(out=MKB[b][:, gsl, :].rearrange("p h t -> p (h t)"), in0=PF[2][:, :], in1=masks[:, mM, :], op=ALU.mult),
                           r=["pf2", "masks"], w=["MKB%d" % b])
                        op("dve", lambda e: e.tensor_tensor(out=MBR[b][:, gsl, :].rearrange("p h t -> p (h t)"), in0=PF[3][:, :], in1=masks[:, mI, :], op=ALU.mult),
                           r=["pf3", "masks"], w=["MBR%d" % b])
                        yield
                        mtype(3, L_plain(kt_), L_z(rtz))
                        op("dve", lambda e: e.tensor_tensor(out=MKR[b][:, gsl, :].rearrange("p h t -> p (h t)"), in0=PF[3][:, :], in1=masks[:, mI, :], op=ALU.mult),
                           r=["pf3", "masks"], w=["MKR%d" % b])
                        cur = 2
                        for lev in range(6):
                            nx = 0 if cur == 2 else 1 - cur
                            last = lev == 5
                            first = lev == 0

                            def blk(kind, cp, bank, cur=cur):
                                cs = slice(cp * 64, cp * 64 + 64)
                                for hh in range(8):
                                    if kind == "P":
                                        lt_, rh_ = Nm[cur], Pm[cur]
                                    elif kind == "N":
                                        lt_, rh_ = Pm[cur], Nm[cur]
                                    else:
                                        lt_, rh_ = Nm[cur], Tm[cur]
                                    rk = ["Nm%d" % cur, "Pm%d" % cur] + (["Tm%d" % cur] if kind == "T" else [])
                                    op("pe", lambda e, hh=hh, cs=cs, lt_=lt_, rh_=rh_, bank=bank: e.matmul(PF[bank][cs, hh * 64:(hh + 1) * 64], lt_[cs, hh, :], rh_[cs, hh, :], start=True, stop=True, skip_group_check=True),
                                       r=rk, w=["pf%d" % bank], inc=(hh == 7))

                            if first:
                                op("dve", lambda e, nx=nx, cur=cur: e.tensor_tensor(out=Tm[nx][:].rearrange("p h t -> p (h t)"), in0=Pm[cur][:].rearrange("p h t -> p (h t)"), in1=masks[:, IDS, :], op=ALU.add),
                                   r=["Pm%d" % cur, "masks"], w=["Tm%d" % nx])
                                blk("P", 0, 0); blk("N", 1, 1); blk("P", 1, 0); blk("N", 0, 1)
                            elif last:
                                blk("T", 0, 2); blk("T", 1, 3)
                            else:
                                blk("P", 0, 0); blk("N", 1, 1); blk("T", 0, 2); blk("P", 1, 0); blk("N", 0, 1); blk("T", 1, 2)
                            if not first:
                                if last:
                                    op("dve", lambda e, nx=nx, cur=cur: e.tensor_tensor(out=Tm[nx][H0].rearrange("p h t -> p (h t)"), in0=PF[2][H0, :], in1=Tm[cur][H0].rearrange("p h t -> p (h t)"), op=ALU.add),
                                       r=["pf2", "Tm%d" % cur], w=["Tm%d" % nx])
                                    op("dve", lambda e, nx=nx, cur=cur: e.tensor_tensor(out=Tm[nx][H1].rearrange("p h t -> p (h t)"), in0=PF[3][H1, :], in1=Tm[cur][H1].rearrange("p h t -> p (h t)"), op=ALU.add),
                                       r=["pf3", "Tm%d" % cur], w=["Tm%d" % nx])
                                else:
                                    op("dve", lambda e, nx=nx, cur=cur: e.tensor_tensor(out=Tm[nx][:].rearrange("p h t -> p (h t)"), in0=PF[2][:, :], in1=Tm[cur][:].rearrange("p h t -> p (h t)"), op=ALU.add),
                                       r=["pf2", "Tm%d" % cur], w=["Tm%d" % nx])
                            if not last:
                                op("act", lambda e, nx=nx: e.activation(out=Pm[nx][:].rearrange("p h t -> p (h t)"), in_=PF[0][:, :], func=AF.Copy), r=["pf0"], w=["Pm%d" % nx])
                                op("act", lambda e, nx=nx: e.activation(out=Nm[nx][:].rearrange("p h t -> p (h t)"), in_=PF[1][:, :], func=AF.Copy), r=["pf1"], w=["Nm%d" % nx])
                            cur = nx
                            yield
                        op("dve", lambda e, cur=cur: e.tensor_copy(out=TT[b][:, gsl, :], in_=Tm[cur][:]), r=["Tm%d" % cur], w=["TT%d" % b])
                        for cp in range(2):
                            cs = slice(cp * 64, cp * 64 + 64)
                            for hh in range(8):
                                op("pe", lambda e, hh=hh, cs=cs, cp=cp, cur=cur: e.matmul(PF[cp][cs, hh * 64:(hh + 1) * 64], Nm[2][cs, hh, :], Tm[cur][cs, hh, :], start=True, stop=True, skip_group_check=True),
                                   r=["Nm2", "Tm%d" % cur], w=["pf%d" % cp], inc=(hh == 7))
                        for cp in range(2):
                            cs = slice(cp * 64, cp * 64 + 64)
                            op("dve", lambda e, cs=cs, cp=cp, cur=cur: e.scalar_tensor_tensor(out=rtmp[cs, :], in0=Tm[cur][cs].rearrange("p h t -> p (h t)"), scalar=-1.0, in1=PF[cp][cs, :], op0=ALU.mult, op1=ALU.add),
                               r=["pf%d" % cp, "Tm%d" % cur], w=["rtmp"])
                            op("dve", lambda e, cs=cs: e.tensor_tensor(out=RZ[b][cs, gsl, :].rearrange("p h t -> p (h t)"), in0=rtmp[cs, :], in1=masks[cs, IDS, :], op=ALU.add),
                               r=["rtmp", "masks"], w=["RZ%d" % b])
                        yield

                def chain(si, d, ti, b, first_tile, last_tile):
                    s0 = SEQS[si][0]
                    t0 = s0 + ti * 128
                    L = "ld%d" % b
                    atz, rtz, bhz, khz, vhz, vfz, dt_ = satz[b], srtz[b], sbhz[b], skhz[b], svhz[b], svfz[b], sdt[b]
                    mkb, mbr, mkr, tt, rz = MKB[b], MBR[b], MKR[b], TT[b], RZ[b]
                    KM = ["MKB%d" % b, "MBR%d" % b, "MKR%d" % b, "TT%d" % b, "RZ%d" % b]
                    if first_tile:
                        if si < 2:
                            op("dve", lambda e: e.memset(ST[:], 0.0), w=["ST"])
                        else:
                            dma("sp", "c_s0", S0[:], st0[d].rearrange("h v k -> v h k"), w=["S0"])
                            for hp in range(8):
                                op("pe", lambda e, hp=hp: e.transpose(PF[4][:, hp * 64:(hp + 1) * 64], S0[:, 2 * hp:2 * hp + 2, :].rearrange("v h k -> v (h k)"), ident64),
                                   r=["S0", "identf"], w=["pf4"], inc=(hp == 7))
                            op("dve", lambda e: e.tensor_copy(out=ST[:], in_=PF[4][:, :].rearrange("p (h v) -> p h v", v=64)), r=["pf4"], w=["ST"])
                        copy_state_bf16()
                        yield
                    yb = Yt[b]
                    yk = "Y%d" % b
                    for cp in ((0, 1) if d == 0 else (1, 0)):
                        cs = slice(cp * 64, cp * 64 + 64)
                        for h in range(16):
                            pw = PF[4 + h // 8]
                            o = pw[cs, (h % 8) * 64:(h % 8) * 64 + 64]
                            op("pe", lambda e, o=o, h=h, cs=cs: e.matmul(o, atz[:, h % 2, h // 2, cs], Sbz[:, h // 2, h % 2, :], start=True, stop=False, skip_group_check=True),
                               r=[L, "Sbz"], w=["pf%d" % (4 + h // 8)], inc=False)
                            op("pe", lambda e, o=o, h=h, cp=cp: e.matmul(o, mkb[:, h, :], vhz[:, cp, h * 64:(h + 1) * 64], start=False, stop=True, skip_group_check=True),
                               r=[L, KM[0]], w=["pf%d" % (4 + h // 8)], inc=(h % 8 == 7))
                        op("act", lambda e, cs=cs, cp=cp: e.activation(out=Wz[cs, cp, 0:512], in_=PF[4][cs, :], func=AF.Copy), r=["pf4"], w=["Wz"])
                        op("dve", lambda e, cs=cs, cp=cp: e.tensor_copy(out=Wz[cs, cp, 512:1024], in_=PF[5][cs, :]), r=["pf5"], w=["Wz"])
                        yield
                        for h in range(16):
                            pu = PF[6 + h // 8]
                            op("pe", lambda e, pu=pu, h=h, cs=cs, cp=cp: e.matmul(pu[cs, (h % 8) * 64:(h % 8) * 64 + 64], tt[:, h, :], Wz[:, cp, h * 64:(h + 1) * 64], start=True, stop=True, skip_group_check=True),
                               r=[KM[3], "Wz"], w=["pf%d" % (6 + h // 8)], inc=(h % 8 == 7))
                        op("act", lambda e, cs=cs, cp=cp: e.activation(out=Uz[cs, cp, 0:512], in_=PF[6][cs, :], func=AF.Copy), r=["pf6"], w=["Uz"])
                        op("dve", lambda e, cs=cs, cp=cp: e.tensor_copy(out=Uz[cs, cp, 512:1024], in_=PF[7][cs, :]), r=["pf7"], w=["Uz"])
                        op("act", lambda e, cs=cs, cp=cp: e.activation(out=Ufz[cs, cp, 0:512], in_=PF[6][cs, :], func=AF.Copy), r=["pf6"], w=["Ufz"])
                        op("dve", lambda e, cs=cs, cp=cp: e.tensor_copy(out=Ufz[cs, cp, 512:1024], in_=PF[7][cs, :]), r=["pf7"], w=["Ufz"])
                        yield
                        for h in range(16):
                            pu = PF[6 + h // 8]
                            op("pe", lambda e, pu=pu, h=h, cs=cs, cp=cp: e.matmul(pu[cs, (h % 8) * 64:(h % 8) * 64 + 64], rz[:, h, :], Uz[:, cp, h * 64:(h + 1) * 64], start=True, stop=True, skip_group_check=True),
                               r=[KM[4], "Uz"], w=["pf%d" % (6 + h // 8)], inc=(h % 8 == 7))
                        for half in range(2):
                            hsl = slice(half * 512, half * 512 + 512)
                            op("dve", lambda e, cs=cs, cp=cp, half=half, hsl=hsl: e.tensor_tensor(out=Ufz[cs, cp, hsl], in0=PF[6 + half][cs, :], in1=Ufz[cs, cp, hsl], op=ALU.add),
                               r=["pf%d" % (6 + half), "Ufz"], w=["Ufz"])
                            op("act", lambda e, cs=cs, cp=cp, hsl=hsl: e.activation(out=Uz[cs, cp, hsl], in_=Ufz[cs, cp, hsl], func=AF.Copy), r=["Ufz"], w=["Uz"])
                        yield
                        for h in range(16):
                            hs = HS[h % 2]
                            hv = slice(h * 64, h * 64 + 64)
                            oy = PF[4][hs, (h // 2) * 64:(h // 2) * 64 + 64]
                            op("pe", lambda e, oy=oy, h=h, cs=cs: e.matmul(oy, Sbz[:, h // 2, h % 2, :], rtz[:, h % 2, h // 2, cs], start=True, stop=False, skip_group_check=True),
                               r=["Sbz", L], w=["pf4"], inc=False)
                            op("pe", lambda e, oy=oy, hv=hv, h=h, cp=cp: e.matmul(oy, Uz[:, cp, hv], mbr[:, h, :], start=False, stop=False, skip_group_check=True),
                               r=["Uz", KM[1]], w=["pf4"], inc=False)
                            op("pe", lambda e, oy=oy, hv=hv, h=h, cp=cp: e.matmul(oy, vhz[:, cp, hv], mkr[:, h, :], start=False, stop=True, skip_group_check=True),
                               r=[L, KM[2]], w=["pf4"], inc=(h == 15))
                        for h in range(16):
                            hs = HS[h % 2]
                            hv = slice(h * 64, h * 64 + 64)
                            osn = PF[5][hs, (h // 2) * 64:(h // 2) * 64 + 64]
                            op("pe", lambda e, osn=osn, hv=hv, cp=cp: e.matmul(osn, bhz[:, cp, hv], Ufz[:, cp, hv], start=True, stop=False, skip_group_check=True),
                               r=[L, "Ufz"], w=["pf5"], inc=False)
                            op("pe", lambda e, osn=osn, hv=hv, cp=cp: e.matmul(osn, khz[:, cp, hv], vfz[:, cp, hv], start=False, stop=True, skip_group_check=True),
                               r=[L], w=["pf5"], inc=(h == 15))
                        op("act", lambda e, yb=yb, cs=cs: e.activation(out=yb[:, :, cs], in_=PF[4][:, :].rearrange("p (c t) -> p c t", t=64), func=AF.Copy), r=["pf4"], w=[yk])
                        for hp in range(8):
                            op("dve", lambda e, hp=hp, cp=cp: e.scalar_tensor_tensor(out=ST[:, hp, :], in0=ST[:, hp, :], scalar=dt_[:, hp, cp:cp + 1], in1=PF[5][:, hp * 64:(hp + 1) * 64], op0=ALU.mult, op1=ALU.add),
                               r=["ST", L, "pf5"], w=["ST"])
                        copy_state_bf16()
                        yield
                    dma("sp", "c_y%d" % b, fm_dram(YS[d], t0, 128), yb[:], r=[yk], w=["YS"])
                    if last_tile and si < 2:
                        for hp in range(8):
                            op("pe", lambda e, hp=hp: e.transpose(PF[4 + hp // 4][0:64, (hp % 4) * 128:(hp % 4) * 128 + 128], ST[:, hp, :], identf[:]),
                               r=["ST", "identf"], w=["pf%d" % (4 + hp // 4)], inc=(hp % 4 == 3))
                        op("dve", lambda e: e.tensor_copy(out=SO[:, 0:4, :].rearrange("p a b -> p (a b)"), in_=PF[4][0:64, :]), r=["pf4"], w=["SO"])
                        op("dve", lambda e: e.tensor_copy(out=SO[:, 4:8, :].rearrange("p a b -> p (a b)"), in_=PF[5][0:64, :]), r=["pf5"], w=["SO"])
                        dma("sp", "c_so", sto[si, d].rearrange("h v k -> v h k"), SO[:].rearrange("v a (h k) -> v (a h) k", k=64), r=["SO"], w=["sto"])
                        yield

                units = []
                for si, (s0_, sl_) in enumerate(SEQS):
                    ntl = sl_ // 128
                    for d in range(2):
                        order = list(range(ntl)) if d == 0 else list(range(ntl - 1, -1, -1))
                        for j, ti in enumerate(order):
                            units.append((si, d, ti, j == 0, j == ntl - 1))
                PIPE = int(os.environ.get("C_PIPE", "1"))
                for _ in prep(units[0][0], units[0][1], units[0][2], 0):
                    pass
                for j, (si, d, ti, ft, ltile) in enumerate(units):
                    b = j % 2
                    g1 = chain(si, d, ti, b, ft, ltile)
                    g2 = prep(units[j + 1][0], units[j + 1][1], units[j + 1][2], 1 - b) if j + 1 < len(units) else iter(())
                    if not PIPE:
                        for _ in g1:
                            pass
                        for _ in g2:
                            pass
                        continue
                    a_done = b_done = False
                    while not (a_done and b_done):
                        if not b_done:
                            try:
                                next(g2)
                            except StopIteration:
                                b_done = True
                        if not a_done:
                            try:
                                next(g1)
                            except StopIteration:
                                a_done = True
                kb.barrier()

        if run("D"):
            with ExitStack() as ph:
                w_pg = sb("w_pg", [128, 8, 2560], BF16, ph)
                w_o = sb("w_o", [128, 8, D], BF16, ph)
                pws = sb("pws", [128, 4, 256], BF16, ph)
                pms = sb("pms", [128, 2, 4, 2, 256], BF16, ph)
                dma("pool", "d_w0", w_pg[:, :, 0:512], w_in[:, 0:512].rearrange("(c p) n -> p c n", p=128), w=["dw"])
                for q in range(2):
                    dma("pool", "d_w0", w_pg[:, :, 512 + q * 1024:512 + (q + 1) * 1024],
                        w_in[:, 3584 + q * 1024:3584 + (q + 1) * 1024].rearrange("(c p) n -> p c n", p=128), w=["dw"])
                dma("pool", "d_w0", w_o[:], w_out.rearrange("(c p) n -> p c n", p=128), w=["dw"])
                dma("pool", "d_w0", pws[:], pool_w.rearrange("g c n -> c g n"), w=["dw"])
                dma("pool", "d_w0", pms[:], c_pm.rearrange("k g p s t -> p k g s t"), w=["dw"])
                hT = sb("d_hT", [128, 8, 256], BF16, ph)
                yf = sb("d_yf", [128, 8, 256], F32, ph)
                yb2 = sb("d_yb", [128, 8, 256], F32, ph)
                bon = sb("d_bon", [128, 8, 256], F32, ph)
                gt = sb("d_gt", [128, 8, 256], F32, ph)
                xt = sb("d_x", [128, 2, D], F32, ph)
                yc2 = sb("d_yc", [128, 8, 256], F32, ph)
                gA = sb("d_gA", [128, 8, 256], F32, ph)
                gB = sb("d_gB", [128, 8, 256], F32, ph)
                zp = sb("d_zp", [128, 2, 512], BF16, ph)
                mixT = sb("d_mixT", [128, 4, 256], BF16, ph)
                t1 = sb("d_t1", [128, 256], F32, ph)
                mT = sb("d_mT", [128, 8, 256], BF16, ph)
                x1 = sb("d_x1", [128, 2, D], F32, ph)
                tmp = sb("d_tmp", [128, 512], F32, ph)
                ss = sb("d_ss", [128, 1], F32, ph)
                rstd = sb("d_rstd", [128, 1], F32, ph)
                xn = sb("d_xn", [128, D], F32, ph)
                h2 = sb("d_h2", [128, 8, 128], BF16, ph)
                eps_ln = sb("d_eps", [128, 1], F32, ph)
                op("dve", lambda e: e.memset(eps_ln[:], 64e-5), w=["eps_ln"])
                for blk in range(NT // 256):
                    t0 = blk * 256
                    cv = 0 if blk < 2 else 1
                    kind = 0 if blk < 2 else 1
                    dma("sp", "d_l0", hT[:], fm_dram(HT, t0, 256), r=[], w=["hT"])
                    dma("sp", "d_l1", yf[:], fm_dram(YS[0], t0, 256), w=["yf"])
                    dma("sp", "d_l2", yb2[:], fm_dram(YS[1], t0, 256), w=["yb"])
                    dma("sp", "d_l3", bon[:], fm_dram(BON, t0, 256), w=["bon"])
                    dma("sp", "d_l4", gt[:], fm_dram(GT, t0, 256), w=["gt"])
                    dma("sp", "d_l5", xt[:], xc[t0:t0 + 256, :].rearrange("(a p) n -> p a n", p=128), w=["xt"])
                    op("dve", lambda e: e.tensor_tensor(out=yf[:], in0=yf[:], in1=yb2[:], op=ALU.add), r=["yf", "yb"], w=["yf"])
                    for q in range(4):
                        for j in range(2):
                            c = q * 2 + j
                            op("pe", lambda e, c=c, j=j, q=q: e.matmul(PF[q % 2][:, j * 256:(j + 1) * 256], bdones[:], yf[:, c, :], start=True, stop=True),
                               r=["bdones", "yf"], w=["pf%d" % (q % 2)], inc=(j == 1))
                        op("dve", lambda e, q=q: e.scalar_tensor_tensor(out=yc2[:, 2 * q:2 * q + 2, :], in0=PF[q % 2][:, :].rearrange("p (j t) -> p j t", t=256), scalar=-1.0 / 64, in1=yf[:, 2 * q:2 * q + 2, :], op0=ALU.mult, op1=ALU.add),
                           r=["pf%d" % (q % 2), "yf"], w=["yc"])
                    op("act", lambda e: e.activation(out=yb2[:], in_=yc2[:], func=AF.Square), r=["yc"], w=["yb"])
                    for q in range(4):
                        for j in range(2):
                            c = q * 2 + j
                            op("pe", lambda e, c=c, j=j, q=q: e.matmul(PF[q % 2][:, j * 256:(j + 1) * 256], bdones[:], yb2[:, c, :], start=True, stop=True),
                               r=["bdones", "yb"], w=["pf%d" % (q % 2)], inc=(j == 1))
                        op("act", lambda e, q=q: e.activation(out=yf[:, 2 * q:2 * q + 2, :], in_=PF[q % 2][:, :].rearrange("p (j t) -> p j t", t=256), func=AF.Ln, scale=1.0 / 64, bias=eps_ln[:]),
                           r=["pf%d" % (q % 2), "eps_ln"], w=["yf"])
                    op("act", lambda e: e.activation(out=yf[:], in_=yf[:], func=AF.Exp, scale=-0.5), r=["yf"], w=["yf"])
                    op("dve", lambda e: e.tensor_tensor(out=yc2[:], in0=yc2[:], in1=yf[:], op=ALU.mult), r=["yc", "yf"], w=["yc"])
                    for c in range(8):
                        op("dve", lambda e, c=c: e.tensor_scalar(out=yc2[:, c, :], in0=yc2[:, c, :], scalar1=colc("ln_w", c), scalar2=colc("ln_b", c), op0=ALU.mult, op1=ALU.add),
                           r=["yc", "cols"], w=["yc"])
                    op("dve", lambda e: e.tensor_tensor(out=yc2[:], in0=yc2[:], in1=bon[:], op=ALU.add), r=["yc", "bon"], w=["yc"])
                    op("dve", lambda e: e.tensor_tensor(out=yc2[:], in0=yc2[:], in1=gt[:], op=ALU.mult), r=["yc", "gt"], w=["yc"])
                    for fc in range(16):
                        pf = PF[2 + fc % 2]
                        pk = "pf%d" % (2 + fc % 2)
                        for kc in range(8):
                            op("pe", lambda e, fc=fc, kc=kc, pf=pf: e.matmul(pf[:, 0:256], w_pg[:, kc, 512 + fc * 128:512 + (fc + 1) * 128], hT[:, kc, :], start=(kc == 0), stop=(kc == 7)),
                               r=["dw", "hT"], w=[pk], inc=(kc == 7))
                        dst = gA if fc < 8 else gB
                        op("act", lambda e, fc=fc, pf=pf, dst=dst: e.activation(out=dst[:, fc % 8, :], in_=pf[:, 0:256], func=AF.Sigmoid), r=[pk], w=["gA" if fc < 8 else "gB"])
                    op("dve", lambda e: e.tensor_tensor(out=yc2[:], in0=yc2[:], in1=gB[:], op=ALU.mult), r=["yc", "gB"], w=["yc"])
                    for a in range(2):
                        for kc in range(8):
                            op("pe", lambda e, a=a, kc=kc: e.matmul(PF[4][:, :], hT[:, kc, a * 128:(a + 1) * 128], w_pg[:, kc, 0:512], start=(kc == 0), stop=(kc == 7)),
                               r=["dw", "hT"], w=["pf4"], inc=(kc == 7))
                        op("act", lambda e, a=a: e.activation(out=zp[:, a, :], in_=PF[4][:, :], func=AF.Copy), r=["pf4"], w=["zp"])
                    for g in range(4):
                        for a in range(2):
                            op("pe", lambda e, g=g, a=a: e.matmul(PF[5][:, 0:256], zp[:, a, g * 128:(g + 1) * 128], pms[:, kind, g, a, :], start=(a == 0), stop=(a == 1)),
                               r=["zp", "dw"], w=["pf5"], inc=(a == 1))
                        op("act", lambda e, g=g: e.activation(out=mixT[:, g, :], in_=PF[5][:, 0:256], func=AF.Copy), r=["pf5"], w=["mixT"])
                    for dc in range(8):
                        g = dc // 2
                        pf = PF[dc % 2]
                        pk = "pf%d" % (dc % 2)
                        op("pe", lambda e, dc=dc, g=g, pf=pf: e.matmul(pf[:, 0:256], pws[:, g, (dc % 2) * 128:(dc % 2) * 128 + 128], mixT[:, g, :], start=True, stop=True),
                           r=["dw", "mixT"], w=[pk])
                        op("dve", lambda e, dc=dc, pf=pf: e.scalar_tensor_tensor(out=t1[:], in0=pf[:, 0:256], scalar=colc("pool_scale", dc), in1=gA[:, dc, :], op0=ALU.mult, op1=ALU.mult),
                           r=[pk, "gA", "cols"], w=["t1"])
                        op("dve", lambda e, dc=dc: e.tensor_tensor(out=mT[:, dc, :], in0=t1[:], in1=yc2[:, dc, :], op=ALU.add), r=["t1", "yc"], w=["mT"])
                    for a in range(2):
                        for half in range(2):
                            pf = PF[2 + half]
                            pk = "pf%d" % (2 + half)
                            for cc in range(8):
                                op("pe", lambda e, a=a, half=half, cc=cc, pf=pf: e.matmul(pf[:, :], mT[:, cc, a * 128:(a + 1) * 128], w_o[:, cc, half * 512:(half + 1) * 512], start=(cc == 0), stop=(cc == 7)),
                                   r=["mT", "dw"], w=[pk], inc=(cc == 7))
                            op("dve", lambda e, a=a, half=half, pf=pf: e.tensor_tensor(out=tmp[:], in0=pf[:, :], in1=GA1[cv][:, half * 512:(half + 1) * 512], op=ALU.mult),
                               r=[pk, "GA"], w=["tmp"])
                            op("dve", lambda e, a=a, half=half: e.tensor_tensor(out=x1[:, a, half * 512:(half + 1) * 512], in0=tmp[:], in1=xt[:, a, half * 512:(half + 1) * 512], op=ALU.add),
                               r=["tmp", "xt"], w=["x1"])
                        norm_transpose(ph, x1[:, a, :], "x1", A2, B2, cv, lambda c: h2[:, c, :], "h2", (ss, rstd, xn))
                        dma("sp", "d_s0", fm_dram(H2T, t0 + a * 128, 128), h2[:], r=["h2"], w=["H2T"])
                    dma("sp", "d_s1", X1[t0:t0 + 256, :].rearrange("(a p) n -> p a n", p=128), x1[:], r=["x1"], w=["X1"])
                kb.barrier()

        if run("E"):
            with ExitStack() as ph:
                wf1 = sb("wf1", [128, 8, 4096], BF16, ph)
                wf2 = sb("wf2", [128, 32, D], BF16, ph)
                for q in range(8):
                    dma("pool", "e_w1_%d" % q, wf1[:, :, q * 512:(q + 1) * 512], w_ff1[:, q * 512:(q + 1) * 512].rearrange("(c p) n -> p c n", p=128), w=["wf1_%d" % q])
                for q in range(8):
                    dma("pool", "e_w2_%d" % q, wf2[:, q * 4:(q + 1) * 4, :], w_ff2[q * 512:(q + 1) * 512, :].rearrange("(c p) n -> p c n", p=128), w=["wf2_%d" % q])
                h2 = sb("e_h2", [128, 8, 256], BF16, ph)
                x1 = sb("e_x1", [128, 2, D], F32, ph)
                u0 = [sb("e_u0%d" % i, [128, 512], BF16, ph) for i in range(2)]
                uT = sb("e_uT", [128, 32, 256], BF16, ph)
                tmp = sb("e_tmp", [128, 512], F32, ph)
                x2 = sb("e_x2", [128, D], F32, ph)
                junk = sb("e_junk", [128, D], BF16, ph)
                ss = sb("e_ss", [128, 1], F32, ph)
                rstd = sb("e_rstd", [128, 1], F32, ph)
                ot = [sb("e_o%d" % i, [128, D], F32, ph) for i in range(2)]
                for blk in range(NT // 256):
                    t0 = blk * 256
                    cv = 0 if blk < 2 else 1
                    dma("sp", "e_l0", h2[:], fm_dram(H2T, t0, 256), w=["h2"])
                    dma("sp", "e_l1", x1[:], X1[t0:t0 + 256, :].rearrange("(a p) n -> p a n", p=128), w=["x1"])
                    for fp in range(16):
                        pf = PF[fp % 2]
                        pk = "pf%d" % (fp % 2)
                        for j in range(2):
                            fc = fp * 2 + j
                            for kc in range(8):
                                op("pe", lambda e, fc=fc, kc=kc, j=j, pf=pf: e.matmul(pf[:, j * 256:(j + 1) * 256], wf1[:, kc, fc * 128:(fc + 1) * 128], h2[:, kc, :], start=(kc == 0), stop=(kc == 7)),
                                   r=["wf1_%d" % (fc // 4), "h2"], w=[pk], inc=(kc == 7 and j == 1))
                        u = u0[fp % 2]
                        uk = "u0%d" % (fp % 2)
                        op("act", lambda e, pf=pf, u=u: e.activation(out=u[:], in_=pf[:, :], func=AF.Relu), r=[pk], w=[uk])
                        op("dve", lambda e, fp=fp, u=u: e.tensor_tensor(out=uT[:, 2 * fp:2 * fp + 2, :].rearrange("p j t -> p (j t)"), in0=u[:], in1=u[:], op=ALU.mult), r=[uk], w=["uT"])
                    for a in range(2):
                        for half in range(2):
                            pf = PF[2 + half]
                            pk = "pf%d" % (2 + half)
                            for fc in range(32):
                                op("pe", lambda e, a=a, half=half, fc=fc, pf=pf: e.matmul(pf[:, :], uT[:, fc, a * 128:(a + 1) * 128], wf2[:, fc, half * 512:(half + 1) * 512], start=(fc == 0), stop=(fc == 31)),
                                   r=["uT", "wf2_%d" % (fc // 4)], w=[pk], inc=(fc == 31))
                            op("dve", lambda e, half=half, pf=pf: e.tensor_tensor(out=tmp[:], in0=pf[:, :], in1=GA2[cv][:, half * 512:(half + 1) * 512], op=ALU.mult),
                               r=[pk, "GA"], w=["tmp"])
                            op("dve", lambda e, a=a, half=half: e.tensor_tensor(out=x2[:, half * 512:(half + 1) * 512], in0=tmp[:], in1=x1[:, a, half * 512:(half + 1) * 512], op=ALU.add),
                               r=["tmp", "x1"], w=["x2"])
                        op("act", lambda e: e.activation(out=junk[:], in_=x2[:], func=AF.Square, accum_out=ss[:]), r=["x2"], w=["junk", "ss"])
                        op("dve", lambda e: e.tensor_scalar(out=rstd[:], in0=ss[:], scalar1=1.0 / D, scalar2=1e-6, op0=ALU.mult, op1=ALU.add), r=["ss"], w=["rstd"])
                        op("act", lambda e: e.activation(out=rstd[:], in_=rstd[:], func=AF.Sqrt), r=["rstd"], w=["rstd"])
                        op("dve", lambda e: e.reciprocal(out=rstd[:], in_=rstd[:]), r=["rstd"], w=["rstd"])
                        o = ot[a]
                        okey = "ot%d" % a
                        op("act", lambda e, o=o: e.activation(out=o[:], in_=x2[:], func=AF.Copy, scale=rstd[:]), r=["x2", "rstd"], w=[okey])
                        op("dve", lambda e, o=o: e.tensor_tensor(out=o[:], in0=o[:], in1=GF[:], op=ALU.mult), r=[okey, "GF"], w=[okey])
                        dma("sp", "e_o%d" % a, yc[t0 + a * 128:t0 + (a + 1) * 128, :], o[:], r=[okey], w=["yc"])
        kb.finish()
    kb.es.close()
    return nc, kb


def _constants():
    identf = np.eye(128, dtype=np.float32)
    tri = np.zeros((2, 128, 384), np.float32)
    C = -float(np.exp(-0.5))
    for t in range(128):
        for u in range(128):
            if t // 64 != u // 64:
                continue
            tri[0, t, u] = C if t <= u else 0.0
            tri[0, t, 128 + u] = C if t < u else 0.0
            tri[0, t, 256 + u] = C if t > u else 0.0
            tri[1, t, u] = C if t >= u else 0.0
            tri[1, t, 128 + u] = C if t > u else 0.0
            tri[1, t, 256 + u] = C if t < u else 0.0
    s = np.arange(64)[:, None]
    t = np.arange(64)[None, :]
    base = [(s < t), (s > t), (s <= t), (s >= t), (s == t)]
    mask = np.zeros((5, 128, 512), np.float32)
    for m in range(5):
        blk = base[m].astype(np.float32)
        mask[m] = np.tile(np.tile(blk, (2, 1)), (1, 8))
    bd = np.zeros((128, 128), np.float32)
    bd[:64, :64] = 1.0
    bd[64:, 64:] = 1.0
    pm = np.zeros((2, 4, 256, 256), np.float32)
    for kind, lr in ((0, 256), (1, 64)):
        for g, win in enumerate(WINS):
            for tt in range(256):
                r0 = (tt // lr) * lr
                tl = tt - r0
                lo = min(max(tl - win // 2, 0), lr)
                hi = min(max(tl + win - win // 2, 0), lr)
                pm[kind, g, r0 + lo:r0 + hi, tt] += 1.0 / (hi - lo)
                pm[kind, g, tt, tt] -= 1.0
    pm = pm.reshape(2, 4, 2, 128, 256).transpose(0, 1, 3, 2, 4)
    return identf, tri, mask, bd, np.ascontiguousarray(pm)


_CACHE = {}


def kernel(x_prompt, x_sample, state_rwkv, c, c_ctx, w_ada, b_ada, g_norm1, g_norm2, w_in,
           mu_rkv, mu_wag, w_dec0, w_dec1, w_dec2, a0, a1, a2, gate_w1, gate_w2, k_k, k_a,
           r_k, ln_x_w, ln_x_b, pool_w, pool_scale, w_out, w_ff1, w_ff2, g_final):
    in_maps = _prep(x_prompt, x_sample, state_rwkv, c, c_ctx, w_ada, b_ada, g_norm1, g_norm2, w_in,
                    mu_rkv, mu_wag, w_dec0, w_dec1, w_dec2, a0, a1, a2, gate_w1, gate_w2, k_k, k_a,
                    r_k, ln_x_w, ln_x_b, pool_w, pool_scale, w_out, w_ff1, w_ff2, g_final)
    if "nc" not in _CACHE:
        _CACHE["nc"] = build()[0]
    nc = _CACHE["nc"]
    res = run_bass_kernel_spmd(nc, in_maps, core_ids=list(range(8)))
    outs = res.results
    y_prompt = np.zeros((16, 256, D), np.float32)
    y_sample = np.zeros((4, 2048, D), np.float32)
    st_new = np.zeros((16, 1, 2, 16, 64, 64), np.float32)
    for core in range(8):
        yc_ = np.asarray(outs[core]["yc"], dtype=np.float32)
        y_prompt[2 * core] = yc_[0:256]
        y_prompt[2 * core + 1] = yc_[256:512]
        if core < 4:
            y_sample[core] = yc_[512:]
        so = np.asarray(outs[core]["sto"], dtype=np.float32)
        st_new[2 * core, 0] = so[0]
        st_new[2 * core + 1, 0] = so[1]
    return y_prompt, y_sample, st_new


def _prep(x_prompt, x_sample, state_rwkv, c, c_ctx, w_ada, b_ada, g_norm1, g_norm2, w_in,
          mu_rkv, mu_wag, w_dec0, w_dec1, w_dec2, a0, a1, a2, gate_w1, gate_w2, k_k, k_a,
          r_k, ln_x_w, ln_x_b, pool_w, pool_scale, w_out, w_ff1, w_ff2, g_final):
    f = lambda a: np.ascontiguousarray(np.asarray(a, dtype=np.float32))
    x_prompt, x_sample, state_rwkv = f(x_prompt), f(x_sample), f(state_rwkv)
    identf, tri, mask, bd, pm = _constants()
    b_ada6 = f(b_ada).reshape(6, D)
    shared = {
        "w_ada": f(w_ada)[0], "w_in": f(w_in)[0], "w_dec0": f(w_dec0)[0], "w_dec1": f(w_dec1)[0],
        "w_dec2": f(w_dec2)[0], "a1": f(a1)[0], "a2": f(a2)[0], "gate_w1": f(gate_w1)[0],
        "gate_w2": f(gate_w2)[0], "pool_w": f(pool_w)[0], "w_out": f(w_out)[0], "w_ff1": f(w_ff1)[0],
        "w_ff2": f(w_ff2)[0], "g_final": f(g_final).reshape(1, D), "b_ada": b_ada6,
        "c_identb": identf, "c_identf": identf, "c_tri": tri, "c_mask": mask, "c_bd": bd, "c_pm": pm,
    }
    in_maps = []
    for core in range(8):
        li = core % 4
        rows = np.zeros((NR, D), np.float32)
        vals = {"g_norm1": f(g_norm1)[0], "g_norm2": f(g_norm2)[0], "mu_r": f(mu_rkv)[0, 0], "mu_k": f(mu_rkv)[0, 1],
                "mu_v": f(mu_rkv)[0, 2], "mu_w": f(mu_wag)[0, 0], "mu_a": f(mu_wag)[0, 1], "mu_g": f(mu_wag)[0, 2],
                "a0f": f(a0)[0, 0], "a0b": f(a0)[0, 1], "k_k": f(k_k)[0], "k_a": f(k_a)[0], "r_k": f(r_k)[0].reshape(-1),
                "ln_w": f(ln_x_w)[0], "ln_b": f(ln_x_b)[0], "pool_scale": f(pool_scale)[0],
                "sh1": b_ada6[0], "sc1": b_ada6[1], "ga1": b_ada6[2], "sh2": b_ada6[3], "sc2": b_ada6[4], "ga2": b_ada6[5],
                "c0": f(c_ctx), "c1": f(c)[li]}
        for n, v in vals.items():
            rows[RI[n]] = v
        xcore = np.concatenate([x_prompt[2 * core], x_prompt[2 * core + 1], x_sample[li]], axis=0)
        m = dict(shared)
        m["xc"] = np.ascontiguousarray(xcore)
        m["rows"] = rows
        m["st0"] = np.ascontiguousarray(state_rwkv[li, 0])
        in_maps.append(m)
    return in_maps
```

```python
import os
import numpy as np
from contextlib import ExitStack
import concourse.bass as bass
import concourse.mybir as mybir
from concourse.bass_utils import run_bass_kernel_spmd

F32 = mybir.dt.float32
BF16 = mybir.dt.bfloat16
AF = mybir.ActivationFunctionType
ALU = mybir.AluOpType

NT = 2560
NTILE = NT // 128
D = 1024
SEQS = [(0, 256), (256, 256), (512, 2048)]
WINS = (2, 4, 8, 16)
ROWNAMES = ["g_norm1", "g_norm2", "mu_r", "mu_k", "mu_v", "mu_w", "mu_a", "mu_g", "a0f", "a0b",
            "k_k", "k_a", "r_k", "ln_w", "ln_b", "pool_scale", "sh1", "sc1", "ga1", "sh2", "sc2",
            "ga2", "c0", "c1"]
RI = {n: i for i, n in enumerate(ROWNAMES)}
NR = 32


def seq_of_tile(i):
    t = i * 128
    for si, (s0, ln) in enumerate(SEQS):
        if s0 <= t < s0 + ln:
            return si
    raise ValueError


SAME_ENGINE_WAITS = int(os.environ.get("SEW", "1"))


class KB:
    def __init__(self, nc):
        self.nc = nc
        self.es = ExitStack()
        self.E = {"pe": nc.tensor, "act": nc.scalar, "dve": nc.vector, "pool": nc.gpsimd, "sp": nc.sync}
        self.sems = {}
        self.cnt = {}
        self.seen = {e: {} for e in self.E}
        self.lastw = {}
        self.readers = {}
        self.pending = {e: {} for e in self.E}
        self.epoch = {e: 0 for e in self.E}
        self.pe_r = set()
        self.pe_w = set()
        self.n_ins = 0

    def sem(self, name):
        if name not in self.sems:
            self.sems[name] = self.es.enter_context(self.nc.semaphore(name))
            self.cnt[name] = 0
        return self.sems[name]

    def _deps(self, r, w):
        d = {}

        def add(s, v):
            if d.get(s, 0) < v:
                d[s] = v

        for k in r:
            if k in self.lastw:
                add(*self.lastw[k])
        for k in w:
            if k in self.lastw:
                add(*self.lastw[k])
            for s, v in self.readers.get(k, {}).items():
                add(s, v)
        return d

    def _emit_waits(self, eng, d):
        for s, v in self.pending[eng].items():
            if d.get(s, 0) < v:
                d[s] = v
        self.pending[eng] = {}
        for s, v in d.items():
            if eng == "pe" and s.startswith("c_pe"):
                continue
            if SAME_ENGINE_WAITS == 0 and s.startswith("c_" + eng):
                continue
            if self.seen[eng].get(s, 0) >= v:
                continue
            self.E[eng].wait_ge(self.sems[s], v)
            self.seen[eng][s] = v
            self.n_ins += 1

    def _record(self, tok, r, w):
        s, v = tok
        for k in r:
            rd = self.readers.setdefault(k, {})
            if rd.get(s, 0) < v:
                rd[s] = v
        for k in w:
            self.lastw[k] = tok
            self.readers[k] = {}

    def op(self, eng, fn, r=(), w=(), inc=True):
        d = self._deps(r, w)
        self._emit_waits(eng, d)
        ins = fn(self.E[eng])
        self.n_ins += 1
        if eng == "pe" and not inc:
            self.pe_r.update(r)
            self.pe_w.update(w)
            return
        name = "c_%s%d" % (eng, self.epoch[eng])
        sem = self.sem(name)
        self.cnt[name] += 1
        ins.then_inc(sem, 1)
        tok = (name, self.cnt[name])
        if eng == "pe":
            r = set(r) | self.pe_r
            w = set(w) | self.pe_w
            self.pe_r = set()
            self.pe_w = set()
        self._record(tok, r, w)
        if self.cnt[name] >= 30000:
            self.epoch[eng] += 1

    def dma(self, q, slot, out, in_, r=(), w=(), **kw):
        d = self._deps(r, w)
        self._emit_waits(q, d)
        name = "d_" + slot
        sem = self.sem(name)
        self.cnt[name] += 16
        self.E[q].dma_start(out=out, in_=in_, **kw).then_inc(sem, 16)
        self.n_ins += 1
        self._record((name, self.cnt[name]), r, w)

    def barrier(self):
        assert not self.pe_r and not self.pe_w
        toks = {n: c for n, c in self.cnt.items() if c > 0}
        for e in self.E:
            self.pending[e] = dict(toks)
        self.lastw = {}
        self.readers = {}

    def finish(self):
        self.barrier()
        for e in self.E:
            self._emit_waits(e, {})


def build(debug=False, stop_after="E"):
    run = lambda p: "MABCDE".index(p) <= "MABCDE".index(stop_after)
    nc = bass.Bass("TRN2", target_bir_lowering=False)
    kb = KB(nc)
    dram = {}

    def din(name, shape, dt=F32):
        dram[name] = nc.dram_tensor(name, list(shape), dt, kind="ExternalInput").ap()
        return dram[name]

    def dout(name, shape, dt=F32):
        dram[name] = nc.dram_tensor(name, list(shape), dt, kind="ExternalOutput").ap()
        return dram[name]

    def dscr(name, shape, dt):
        kind = "ExternalOutput" if debug else "Internal"
        dram[name] = nc.dram_tensor(name, list(shape), dt, kind=kind).ap()
        return dram[name]

    xc = din("xc", [NT, D])
    rows_in = din("rows", [NR, D])
    st0 = din("st0", [2, 16, 64, 64])
    w_ada = din("w_ada", [D, 6 * D])
    w_in = din("w_in", [D, 5632])
    wd0 = din("w_dec0", [2, D])
    wd1 = din("w_dec1", [2, D, 64])
    wd2 = din("w_dec2", [2, 64, D])
    a1 = din("a1", [2, D, 64])
    a2 = din("a2", [2, 64, D])
    gw1 = din("gate_w1", [D, 128])
    gw2 = din("gate_w2", [128, D])
    pool_w = din("pool_w", [4, 128, 256])
    w_out = din("w_out", [D, D])
    w_ff1 = din("w_ff1", [D, 4 * D])
    w_ff2 = din("w_ff2", [4 * D, D])
    g_final = din("g_final", [1, D])
    b_ada_in = din("b_ada", [6, D])
    c_identb = din("c_identb", [128, 128])
    c_identf = din("c_identf", [128, 128])
    c_tri = din("c_tri", [2, 128, 384])
    c_mask = din("c_mask", [5, 128, 512])
    c_bd = din("c_bd", [128, 128])
    c_pm = din("c_pm", [2, 4, 128, 2, 256])

    yc = dout("yc", [NT, D])
    sto = dout("sto", [2, 2, 16, 64, 64])

    HT = dscr("HT", [128, 8, NT], BF16)
    AT = [dscr("AT%d" % d, [128, 8, NT], BF16) for d in range(2)]
    RT = [dscr("RT%d" % d, [128, 8, NT], BF16) for d in range(2)]
    BT = [dscr("BT%d" % d, [128, 8, NT], BF16) for d in range(2)]
    KT = [dscr("KT%d" % d, [128, 8, NT], BF16) for d in range(2)]
    BH = [dscr("BH%d" % d, [NT, D], F32) for d in range(2)]
    KH = [dscr("KH%d" % d, [NT, D], F32) for d in range(2)]
    DTO = [dscr("DTO%d" % d, [128, 8, NT // 64], F32) for d in range(2)]
    VH = dscr("VH", [NT, D], BF16)
    VHF = dscr("VHF", [NT, D], F32)
    BON = dscr("BON", [128, 8, NT], F32)
    GT = dscr("GT", [128, 8, NT], F32)
    YS = [dscr("YS%d" % d, [128, 8, NT], F32) for d in range(2)]
    X1 = dscr("X1", [NT, D], F32)
    H2T = dscr("H2T", [128, 8, NT], BF16)

    op = kb.op
    dma = kb.dma

    with ExitStack() as gs:
        def sb(name, shape, dt, stack=gs):
            return stack.enter_context(nc.sbuf_tensor("s_" + name, list(shape), dt))

        def ps(name, shape, dt, stack=gs):
            return stack.enter_context(nc.psum_tensor("p_" + name, list(shape), dt))

        PF = [ps("pf%d" % i, [128, 512], F32) for i in range(8)]

        identb = sb("identb", [128, 128], BF16)
        identf = sb("identf", [128, 128], F32)
        bdones = sb("bdones", [128, 128], F32)
        cols = sb("cols", [128, 8, NR], F32)
        A1 = sb("A1", [128, 8, 2], F32)
        B1 = sb("B1", [128, 8, 2], F32)
        A2 = sb("A2", [128, 8, 2], F32)
        B2 = sb("B2", [128, 8, 2], F32)
        GA1 = [sb("GA1_%d" % i, [128, D], F32) for i in range(2)]
        GA2 = [sb("GA2_%d" % i, [128, D], F32) for i in range(2)]
        GF = sb("GF", [128, D], F32)
        muh = sb("muh", [128, 8, 3], F32)
        omm = sb("omm", [128, 8, 3], F32)
        omka = sb("omka", [128, 8], F32)
        eps12 = sb("eps12", [128, 1], F32)

        def col(name):
            return cols[:, :, RI[name]]

        def colc(name, c):
            return cols[:, c, RI[name]:RI[name] + 1]

        dma("pool", "initp", identb[:], c_identb[:, :], w=["identb"])
        dma("sp", "init", identf[:], c_identf[:, :], w=["identf"])
        dma("sp", "init2", bdones[:], c_bd[:, :], w=["bdones"])
        dma("sp", "init3", GF[:], g_final[0:1, :].partition_broadcast(128), w=["GF"])

        if run("M"):
            with ExitStack() as ph:
                rows = sb("rows", [NR, D], F32, ph)
                silu = sb("silu", [128, 8, 2], F32, ph)
                silub = sb("silub", [128, 8, 2], BF16, ph)
                silubc = [sb("silubc%d" % i, [128, 8, 128], F32, ph) for i in range(2)]
                onesb = sb("onesb", [128, 128], F32, ph)
                slab = [sb("mslab%d" % i, [128, 8, 512], F32, ph) for i in range(2)]
                modT = sb("modT", [128, 8, 6, 2], F32, ph)
                gab = [sb("gab%d" % i, [128, D], F32, ph) for i in range(2)]

                dma("sp", "rows", rows[:], rows_in[:, :], w=["rows"])
                dma("sp", "gab0", gab[0][:], b_ada_in[2:3, :].partition_broadcast(128), w=["gab0"])
                dma("sp", "gab1", gab[1][:], b_ada_in[5:6, :].partition_broadcast(128), w=["gab1"])
                for c in range(8):
                    op("pe", lambda e, c=c: e.transpose(PF[c % 2][:, 0:NR], rows[:, c * 128:(c + 1) * 128], identf[0:NR, 0:NR]),
                       r=["rows", "identf"], w=["pf%d" % (c % 2)])
                    op("dve", lambda e, c=c: e.tensor_copy(out=cols[:, c, :], in_=PF[c % 2][:, 0:NR]),
                       r=["pf%d" % (c % 2)], w=["cols"])
                op("dve", lambda e: e.memset(eps12[:], 1e-12), w=["eps12"])
                op("dve", lambda e: e.memset(onesb[:], 1.0), w=["onesb"])
                op("dve", lambda e: e.tensor_scalar(out=muh[:], in0=cols[:, :, RI["mu_r"]:RI["mu_r"] + 3], scalar1=0.5, scalar2=None, op0=ALU.mult),
                   r=["cols"], w=["muh"])
                op("dve", lambda e: e.tensor_scalar(out=omm[:], in0=cols[:, :, RI["mu_r"]:RI["mu_r"] + 3], scalar1=-1.0, scalar2=1.0, op0=ALU.mult, op1=ALU.add),
                   r=["cols"], w=["omm"])
                op("dve", lambda e: e.tensor_scalar(out=omka[:], in0=col("k_a"), scalar1=-1.0, scalar2=1.0, op0=ALU.mult, op1=ALU.add),
                   r=["cols"], w=["omka"])
                op("act", lambda e: e.activation(out=silu[:], in_=cols[:, :, RI["c0"]:RI["c0"] + 2], func=AF.Sigmoid),
                   r=["cols"], w=["silu"])
                op("dve", lambda e: e.tensor_tensor(out=silu[:], in0=silu[:], in1=cols[:, :, RI["c0"]:RI["c0"] + 2], op=ALU.mult),
                   r=["silu", "cols"], w=["silu"])
                op("dve", lambda e: e.tensor_copy(out=silub[:], in_=silu[:]), r=["silu"], w=["silub"])
                for b in range(2):
                    for c in range(8):
                        op("dve", lambda e, b=b, c=c: e.tensor_scalar(out=silubc[b][:, c, :], in0=onesb[:], scalar1=silu[:, c, b:b + 1], scalar2=None, op0=ALU.mult),
                           r=["silu", "onesb"], w=["silubc%d" % b])
                for sl in range(12):
                    sbuf = slab[sl % 2]
                    skey = "mslab%d" % (sl % 2)
                    dma("sp", skey, sbuf[:], w_ada[:, sl * 512:(sl + 1) * 512].rearrange("(c p) n -> p c n", p=128), w=[skey])
                    which = sl // 2
                    pf = PF[sl % 2]
                    for j in range(4):
                        for kc in range(8):
                            op("pe", lambda e, j=j, kc=kc, sbuf=sbuf, pf=pf: e.matmul(pf[:, j * 2:j * 2 + 2], sbuf[:, kc, j * 128:(j + 1) * 128], silu[:, kc, :], start=(kc == 0), stop=(kc == 7)),
                               r=[skey, "silu"], w=["pf%d" % (sl % 2)], inc=(j == 3 and kc == 7))
                    cbase = (sl % 2) * 4
                    op("dve", lambda e, pf=pf, which=which, cbase=cbase: e.tensor_copy(out=modT[:, cbase:cbase + 4, which, :], in_=pf[:, 0:8].rearrange("p (j b) -> p j b", b=2)),
                       r=["pf%d" % (sl % 2)], w=["modT"])
                    if which in (2, 5):
                        gi = 0 if which == 2 else 1
                        for b in range(2):
                            pg = PF[2 + b]
                            for kc in range(8):
                                op("pe", lambda e, kc=kc, b=b, sbuf=sbuf, pg=pg: e.matmul(pg[:, :], silubc[b][:, kc, :], sbuf[:, kc, :], start=(kc == 0), stop=(kc == 7)),
                                   r=[skey, "silubc%d" % b], w=["pf%d" % (2 + b)], inc=(kc == 7))
                            dst = (GA1 if gi == 0 else GA2)[b]
                            half = sl % 2
                            op("dve", lambda e, dst=dst, pg=pg, gi=gi, half=half: e.tensor_tensor(out=dst[:, half * 512:(half + 1) * 512], in0=pg[:, :], in1=gab[gi][:, half * 512:(half + 1) * 512], op=ALU.add),
                               r=["pf%d" % (2 + b), "gab%d" % gi], w=["GA%d_%d" % (gi + 1, b)])
                for b in range(2):
                    for wi, nm in enumerate(["sh1", "sc1", "ga1", "sh2", "sc2", "ga2"]):
                        op("dve", lambda e, b=b, wi=wi, nm=nm: e.tensor_tensor(out=modT[:, :, wi, b], in0=modT[:, :, wi, b], in1=col(nm), op=ALU.add),
                           r=["modT", "cols"], w=["modT"])
                    op("dve", lambda e, b=b: e.scalar_tensor_tensor(out=A1[:, :, b], in0=modT[:, :, 1, b], scalar=1.0, in1=col("g_norm1"), op0=ALU.add, op1=ALU.mult),
                       r=["modT", "cols"], w=["A1"])
                    op("dve", lambda e, b=b: e.scalar_tensor_tensor(out=A2[:, :, b], in0=modT[:, :, 4, b], scalar=1.0, in1=col("g_norm2"), op0=ALU.add, op1=ALU.mult),
                       r=["modT", "cols"], w=["A2"])
                    op("dve", lambda e, b=b: e.tensor_copy(out=B1[:, :, b], in_=modT[:, :, 0, b]), r=["modT"], w=["B1"])
                    op("dve", lambda e, b=b: e.tensor_copy(out=B2[:, :, b], in_=modT[:, :, 3, b]), r=["modT"], w=["B2"])
                kb.barrier()

        def norm_transpose(ph, xt, xkey, Acol, Bcol, cv, out_fm, okey, scratch):
            ss, rstd, xn = scratch
            op("act", lambda e: e.activation(out=xn[:], in_=xt, func=AF.Square, accum_out=ss[:]),
               r=[xkey], w=["nt_xn", "nt_ss"])
            op("dve", lambda e: e.tensor_scalar(out=rstd[:], in0=ss[:], scalar1=1.0 / D, scalar2=1e-6, op0=ALU.mult, op1=ALU.add),
               r=["nt_ss"], w=["nt_rstd"])
            op("act", lambda e: e.activation(out=rstd[:], in_=rstd[:], func=AF.Sqrt), r=["nt_rstd"], w=["nt_rstd"])
            op("dve", lambda e: e.reciprocal(out=rstd[:], in_=rstd[:]), r=["nt_rstd"], w=["nt_rstd"])
            op("act", lambda e: e.activation(out=xn[:], in_=xt, func=AF.Copy, scale=rstd[:]),
               r=[xkey, "nt_rstd"], w=["nt_xn"])
            for c in range(8):
                op("pe", lambda e, c=c: e.transpose(PF[4 + c // 4][:, (c % 4) * 128:(c % 4) * 128 + 128], xn[:, c * 128:(c + 1) * 128], identf[:]),
                   r=["nt_xn", "identf"], w=["pf%d" % (4 + c // 4)], inc=(c % 4 == 3))
            for c in range(8):
                op("dve", lambda e, c=c: e.tensor_scalar(out=out_fm(c), in0=PF[4 + c // 4][:, (c % 4) * 128:(c % 4) * 128 + 128], scalar1=Acol[:, c, cv:cv + 1], scalar2=Bcol[:, c, cv:cv + 1], op0=ALU.mult, op1=ALU.add),
                   r=["pf%d" % (4 + c // 4), "A1", "A2", "B1", "B2"], w=[okey])

        def fm_dram(t, tok0, n):
            return t[:, :, tok0:tok0 + n]

        if run("A"):
            with ExitStack() as ph:
                xt = [sb("a_x%d" % i, [128, D], F32, ph) for i in range(2)]
                hT = [sb("a_h%d" % i, [128, 8, 128], BF16, ph) for i in range(2)]
                ss = sb("a_ss", [128, 1], F32, ph)
                rstd = sb("a_rstd", [128, 1], F32, ph)
                xn = sb("a_xn", [128, D], F32, ph)
                for i in range(NTILE):
                    cv = 0 if i < 4 else 1
                    b = i % 2
                    dma("sp", "a_x%d" % b, xt[b][:], xc[i * 128:(i + 1) * 128, :], w=["a_x%d" % b])
                    norm_transpose(ph, xt[b][:], "a_x%d" % b, A1, B1, cv, lambda c, b=b: hT[b][:, c, :], "a_h%d" % b, (ss, rstd, xn))
                    dma("sp", "a_h%d" % b, fm_dram(HT, i * 128, 128), hT[b][:], r=["a_h%d" % b], w=["HT"])
                kb.barrier()

        if run("B"):
            with ExitStack() as ph:
                w_rkv = sb("w_rkv", [128, 8, 3072], BF16, ph)
                wd1s = sb("wd1s", [128, 2, 8, 64], BF16, ph)
                a1s = sb("a1s", [128, 2, 8, 64], BF16, ph)
                wd2s = sb("wd2s", [64, 2, D], BF16, ph)
                a2s = sb("a2s", [64, 2, D], BF16, ph)
                gw1s = sb("gw1s", [128, 8, 128], BF16, ph)
                gw2s = sb("gw2s", [128, D], BF16, ph)
                WD0 = [sb("WD0_%d" % d, [128, D], F32, ph) for d in range(2)]
                tri = [sb("tri%d" % d, [128, 384], F32, ph) for d in range(2)]
                for q in range(3):
                    dma("pool", "w_rkv%d" % q, w_rkv[:, :, q * 1024:(q + 1) * 1024],
                        w_in[:, 512 + q * 1024:512 + (q + 1) * 1024].rearrange("(c p) n -> p c n", p=128), w=["w_rkv%d" % q])
                for d in range(2):
                    dma("pool", "wsm", wd1s[:, d, :, :], wd1[d].rearrange("(c p) n -> p c n", p=128), w=["wsm"])
                    dma("pool", "wsm", a1s[:, d, :, :], a1[d].rearrange("(c p) n -> p c n", p=128), w=["wsm"])
                    dma("pool", "wsm", wd2s[:, d, :], wd2[d], w=["wsm"])
                    dma("pool", "wsm", a2s[:, d, :], a2[d], w=["wsm"])
                    dma("sp", "wsm2", WD0[d][:], wd0[d:d + 1, :].partition_broadcast(128), w=["wsm2"])
                    dma("sp", "wsm2", tri[d][:], c_tri[d], w=["wsm2"])
                dma("pool", "wsm", gw1s[:], gw1.rearrange("(c p) n -> p c n", p=128), w=["wsm"])
                dma("pool", "wsm", gw2s[:], gw2[:, :], w=["wsm"])
                WK = ["w_rkv0", "w_rkv1", "w_rkv2"]

                hx = sb("b_hx", [128, 8, 130], BF16, ph)
                hd = sb("b_hd", [128, 8, 128], F32, ph)
                xw = sb("b_xw", [128, 8, 128], BF16, ph)
                xa = sb("b_xa", [128, 8, 128], BF16, ph)
                xg = sb("b_xg", [128, 8, 128], BF16, ph)
                tw = sb("b_tw", [64, 2, 128], BF16, ph)
                aw = sb("b_aw", [64, 2, 128], BF16, ph)
                gw = sb("b_gw", [128, 128], BF16, ph)
                zs = [sb("b_zs%d" % i, [128, 130], F32, ph) for i in range(2)]
                t2 = [sb("b_t2%d" % i, [128, 128], F32, ph) for i in range(2)]
                rkv = sb("b_rkv", [128, 24, 128], F32, ph)
                kraw = sb("b_kraw", [128, 8, 128], F32, ph)
                sq = sb("b_sq", [128, 8, 128], F32, ph)
                kk = sb("b_kk", [128, 8, 128], F32, ph)
                bon = sb("b_bon", [128, 8, 128], F32, ph)
                gt = sb("b_gt", [128, 8, 128], F32, ph)
                vb = sb("b_vb", [128, 8, 128], BF16, ph)
                vht = sb("b_vht", [128, D], BF16, ph)
                sig = sb("b_sig", [128, D], F32, ph)
                afm = sb("b_a", [128, 8, 128], F32, ph)
                beta = sb("b_beta", [128, 8, 128], F32, ph)
                kd = sb("b_kd", [128, 8, 128], F32, ph)
                ee2 = [[sb("b_e%d_%d" % (j, i), [128, 128], F32, ph) for i in range(4)] for j in range(2)]
                o_at = sb("b_oat", [128, 8, 128], BF16, ph)
                o_rt = sb("b_ort", [128, 8, 128], BF16, ph)
                o_bt = sb("b_obt", [128, 8, 128], BF16, ph)
                o_kt = sb("b_okt", [128, 8, 128], BF16, ph)
                o_bh = sb("b_obh", [128, 8, 128], F32, ph)
                o_kh = sb("b_okh", [128, 8, 128], F32, ph)
                bht = sb("b_bht", [128, D], F32, ph)
                kht = sb("b_kht", [128, D], F32, ph)
                vhf = sb("b_vhf", [128, D], F32, ph)
                dtt = sb("b_dtt", [128, 8, 2], F32, ph)

                BSEC = int(os.environ.get("B_SEC", "99"))
                for i in range(int(os.environ.get("B_MAXT", NTILE))):
                    t0 = i * 128
                    s0, sl = SEQS[seq_of_tile(i)]
                    has_prev = t0 > s0
                    has_next = t0 + 128 < s0 + sl
                    lo = t0 - 1 if has_prev else t0
                    hi = t0 + 129 if has_next else t0 + 128
                    if not has_prev:
                        op("dve", lambda e: e.memset(hx[:, :, 0:1], 0.0), w=["hx"])
                    if not has_next:
                        op("dve", lambda e: e.memset(hx[:, :, 129:130], 0.0), w=["hx"])
                    dma("sp", "b_hx", hx[:, :, (lo - (t0 - 1)):(hi - (t0 - 1))], HT[:, :, lo:hi], r=["HT"], w=["hx"])
                    if BSEC < 1:
                        continue
                    op("dve", lambda e: e.tensor_tensor(out=hd[:], in0=hx[:, :, 0:128], in1=hx[:, :, 2:130], op=ALU.add), r=["hx"], w=["hd"])
                    op("dve", lambda e: e.scalar_tensor_tensor(out=hd[:], in0=hd[:], scalar=0.5, in1=hx[:, :, 1:129], op0=ALU.mult, op1=ALU.subtract),
                       r=["hd", "hx"], w=["hd"])
                    for c in range(8):
                        for nm, dst, key in (("mu_w", xw, "xw"), ("mu_a", xa, "xa"), ("mu_g", xg, "xg")):
                            op("dve", lambda e, c=c, nm=nm, dst=dst: e.scalar_tensor_tensor(out=dst[:, c, :], in0=hd[:, c, :], scalar=colc(nm, c), in1=hx[:, c, 1:129], op0=ALU.mult, op1=ALU.add),
                               r=["hd", "hx"], w=[key])
                    S1 = int(os.environ.get("S1", "9"))
                    if S1 < 1:
                        continue
                    for d in range(2):
                        for kc in range(8):
                            op("pe", lambda e, d=d, kc=kc: e.matmul(PF[0][0:64, d * 128:(d + 1) * 128], wd1s[:, d, kc, :], xw[:, kc, :], start=(kc == 0), stop=(kc == 7)),
                               r=["wsm", "xw"], w=["pf0"], inc=(kc == 7))
                        for kc in range(8):
                            op("pe", lambda e, d=d, kc=kc: e.matmul(PF[0][0:64, 256 + d * 128:256 + (d + 1) * 128], a1s[:, d, kc, :], xa[:, kc, :], start=(kc == 0), stop=(kc == 7)),
                               r=["wsm", "xa"], w=["pf0"], inc=(kc == 7))
                    if S1 < 2:
                        continue
                    op("act", lambda e: e.activation(out=tw[:].rearrange("p d t -> p (d t)"), in_=PF[0][0:64, 0:256], func=AF.Tanh), r=["pf0"], w=["tw"])
                    if True:
                        op("act", lambda e: e.activation(out=aw[:].rearrange("p d t -> p (d t)"), in_=PF[0][0:64, 256:512], func=AF.Copy), r=["pf0"], w=["aw"])
                    else:
                        op("dve", lambda e: e.tensor_copy(out=aw[:].rearrange("p d t -> p (d t)"), in_=PF[0][0:64, 256:512]), r=["pf0"], w=["aw"])
                    if S1 < 3:
                        continue
                    for kc in range(8):
                        op("pe", lambda e, kc=kc: e.matmul(PF[1][:, 0:128], gw1s[:, kc, :], xg[:, kc, :], start=(kc == 0), stop=(kc == 7)),
                           r=["wsm", "xg"], w=["pf1"], inc=(kc == 7))
                    op("act", lambda e: e.activation(out=gw[:], in_=PF[1][:, 0:128], func=AF.Sigmoid), r=["pf1"], w=["gw"])
                    if S1 < 4:
                        continue
                    for half in range(2):
                        for j in range(4):
                            fc = half * 4 + j
                            op("pe", lambda e, fc=fc, j=j: e.matmul(PF[1][:, j * 128:(j + 1) * 128], gw2s[:, fc * 128:(fc + 1) * 128], gw[:], start=True, stop=True),
                               r=["wsm", "gw"], w=["pf1"], inc=(j == 3))
                        op("act", lambda e, half=half: e.activation(out=gt[:, half * 4:half * 4 + 4, :], in_=PF[1][:, :].rearrange("p (j t) -> p j t", t=128), func=AF.Copy),
                           r=["pf1"], w=["gt"])
                    if S1 < 5:
                        continue
                    dma("sp", "b_gt", fm_dram(GT, t0, 128), gt[:], r=["gt"], w=["GT"])
                    if BSEC < 2:
                        continue
                    for fc in range(24):
                        pf = PF[2 + fc % 2]
                        pk = "pf%d" % (2 + fc % 2)
                        z = zs[fc % 2]
                        zk = "zs%d" % (fc % 2)
                        tt = t2[fc % 2]
                        tk = "t2%d" % (fc % 2)
                        for kc in range(8):
                            op("pe", lambda e, fc=fc, kc=kc, pf=pf: e.matmul(pf[:, 0:130], w_rkv[:, kc, fc * 128:(fc + 1) * 128], hx[:, kc, :], start=(kc == 0), stop=(kc == 7)),
                               r=[WK[fc // 8], "hx"], w=[pk], inc=(kc == 7))
                        op("act", lambda e, pf=pf, z=z: e.activation(out=z[:], in_=pf[:, 0:130], func=AF.Copy), r=[pk], w=[zk])
                        op("dve", lambda e, z=z, tt=tt: e.tensor_tensor(out=tt[:], in0=z[:, 0:128], in1=z[:, 2:130], op=ALU.add), r=[zk], w=[tk])
                        op("dve", lambda e, tt=tt, fc=fc: e.tensor_scalar(out=tt[:], in0=tt[:], scalar1=muh[:, fc % 8, fc // 8:fc // 8 + 1], scalar2=None, op0=ALU.mult),
                           r=[tk, "muh"], w=[tk])
                        op("dve", lambda e, z=z, tt=tt, fc=fc: e.scalar_tensor_tensor(out=rkv[:, fc, :], in0=z[:, 1:129], scalar=omm[:, fc % 8, fc // 8:fc // 8 + 1], in1=tt[:], op0=ALU.mult, op1=ALU.add),
                           r=[zk, tk, "omm"], w=["rkv"])
                    R = lambda c: rkv[:, c, :]
                    Kf = lambda c: rkv[:, 8 + c, :]
                    Vf = lambda c: rkv[:, 16 + c, :]
                    if BSEC < 3:
                        continue
                    for c in range(8):
                        op("dve", lambda e, c=c: e.tensor_scalar(out=kraw[:, c, :], in0=Kf(c), scalar1=colc("k_k", c), scalar2=None, op0=ALU.mult), r=["rkv"], w=["kraw"])
                    op("act", lambda e: e.activation(out=sq[:], in_=kraw[:], func=AF.Square), r=["kraw"], w=["sq"])
                    for half in range(2):
                        for j in range(4):
                            c = half * 4 + j
                            op("pe", lambda e, c=c, j=j: e.matmul(PF[4][:, j * 128:(j + 1) * 128], bdones[:], sq[:, c, :], start=True, stop=True),
                               r=["bdones", "sq"], w=["pf4"], inc=(j == 3))
                        op("act", lambda e, half=half: e.activation(out=kk[:, half * 4:half * 4 + 4, :], in_=PF[4][:, :].rearrange("p (j t) -> p j t", t=128), func=AF.Ln, bias=eps12[:]),
                           r=["pf4", "eps12"], w=["kk"])
                    op("act", lambda e: e.activation(out=kk[:], in_=kk[:], func=AF.Exp, scale=-0.5), r=["kk"], w=["kk"])
                    op("dve", lambda e: e.tensor_tensor(out=kk[:], in0=kk[:], in1=kraw[:], op=ALU.mult), r=["kk", "kraw"], w=["kk"])
                    for c in range(8):
                        op("dve", lambda e, c=c: e.scalar_tensor_tensor(out=sq[:, c, :], in0=R(c), scalar=colc("r_k", c), in1=Kf(c), op0=ALU.mult, op1=ALU.mult),
                           r=["rkv"], w=["sq"])
                    for half in range(2):
                        for j in range(4):
                            c = half * 4 + j
                            op("pe", lambda e, c=c, j=j: e.matmul(PF[4][:, j * 128:(j + 1) * 128], bdones[:], sq[:, c, :], start=True, stop=True),
                               r=["bdones", "sq"], w=["pf4"], inc=(j == 3))
                        op("dve", lambda e, half=half: e.tensor_tensor(out=bon[:, half * 4:half * 4 + 4, :], in0=PF[4][:, :].rearrange("p (j t) -> p j t", t=128), in1=rkv[:, 16 + half * 4:16 + half * 4 + 4, :], op=ALU.mult),
                           r=["pf4", "rkv"], w=["bon"])
                    dma("sp", "b_bon", fm_dram(BON, t0, 128), bon[:], r=["bon"], w=["BON"])
                    if BSEC < 4:
                        continue
                    for c in range(8):
                        op("pe", lambda e, c=c: e.transpose(PF[c // 4][:, (c % 4) * 128:(c % 4) * 128 + 128], rkv[:, 16 + c, :], identf[:]),
                           r=["rkv", "identf"], w=["pf%d" % (c // 4)], inc=(c % 4 == 3))
                    op("dve", lambda e: e.tensor_copy(out=vhf[:, 0:512], in_=PF[0][:, :]), r=["pf0"], w=["vhf"])
                    op("act", lambda e: e.activation(out=vhf[:, 512:1024], in_=PF[1][:, :], func=AF.Copy), r=["pf1"], w=["vhf"])
                    dma("sp", "b_vhf", VHF[t0:t0 + 128, :], vhf[:], r=["vhf"], w=["VHF"])
                    op("act", lambda e: e.activation(out=vht[:], in_=vhf[:], func=AF.Copy), r=["vhf"], w=["vht"])
                    dma("sp", "b_vht", VH[t0:t0 + 128, :], vht[:], r=["vht"], w=["VH"])
                    if BSEC < 5:
                        continue
                    for d in range(2):
                        for half in range(2):
                            op("pe", lambda e, d=d, half=half: e.matmul(PF[0][:, :], tw[:, d, :], wd2s[:, d, half * 512:(half + 1) * 512], start=True, stop=True),
                               r=["tw", "wsm"], w=["pf0"])
                            op("dve", lambda e, d=d, half=half: e.tensor_tensor(out=sig[:, half * 512:(half + 1) * 512], in0=PF[0][:, :], in1=WD0[d][:, half * 512:(half + 1) * 512], op=ALU.add),
                               r=["pf0", "wsm2"], w=["sig"])
                        op("act", lambda e: e.activation(out=sig[:], in_=sig[:], func=AF.Sigmoid), r=["sig"], w=["sig"])
                        a0n = "a0f" if d == 0 else "a0b"
                        for c in range(8):
                            pa = PF[4 + c % 4]
                            pak = "pf%d" % (4 + c % 4)
                            op("pe", lambda e, c=c, d=d, pa=pa: e.matmul(pa[:, 0:128], a2s[:, d, c * 128:(c + 1) * 128], aw[:, d, :], start=True, stop=True),
                               r=["wsm", "aw"], w=[pak])
                            op("act", lambda e, c=c, pa=pa, a0n=a0n: e.activation(out=afm[:, c, :], in_=pa[:, 0:128], func=AF.Sigmoid, bias=colc(a0n, c)),
                               r=[pak, "cols"], w=["afm%d" % c])
                            op("dve", lambda e, c=c: e.tensor_tensor(out=beta[:, c, :], in0=kk[:, c, :], in1=afm[:, c, :], op=ALU.mult), r=["kk", "afm%d" % c], w=["beta%d" % c])
                            op("dve", lambda e, c=c: e.tensor_scalar(out=kd[:, c, :], in0=afm[:, c, :], scalar1=colc("k_a", c), scalar2=omka[:, c:c + 1], op0=ALU.mult, op1=ALU.add),
                               r=["afm%d" % c, "cols", "omka"], w=["kd%d" % c])
                            op("dve", lambda e, c=c: e.tensor_tensor(out=kd[:, c, :], in0=kd[:, c, :], in1=Kf(c), op=ALU.mult), r=["kd%d" % c, "rkv"], w=["kd%d" % c])
                        for c in range(8):
                            pc = PF[2 + c % 2]
                            pck = "pf%d" % (2 + c % 2)
                            ee = ee2[c % 2]
                            ek = ["e%d_%d" % (c % 2, i) for i in range(4)]
                            op("pe", lambda e, c=c, d=d, pc=pc: e.matmul(pc[:, 0:384], sig[:, c * 128:(c + 1) * 128], tri[d][:], start=True, stop=True),
                               r=["sig", "wsm2"], w=[pck])
                            op("act", lambda e, pc=pc: e.activation(out=ee[0][:], in_=pc[:, 128:256], func=AF.Exp), r=[pck], w=[ek[0]])
                            op("act", lambda e, pc=pc: e.activation(out=ee[1][:], in_=pc[:, 0:128], func=AF.Exp), r=[pck], w=[ek[1]])
                            op("act", lambda e, pc=pc: e.activation(out=ee[2][:], in_=pc[:, 0:128], func=AF.Exp, scale=-1.0), r=[pck], w=[ek[2]])
                            op("act", lambda e, pc=pc: e.activation(out=ee[3][:], in_=pc[:, 256:384], func=AF.Exp), r=[pck], w=[ek[3]])
                            if d == 0:
                                src = pc[:, 63:128:64]
                            else:
                                src = pc[:, 0:128:64]
                            op("act", lambda e, c=c, src=src: e.activation(out=dtt[:, c, :], in_=src, func=AF.Exp), r=[pck], w=["dtt"])
                            op("dve", lambda e, c=c: e.scalar_tensor_tensor(out=o_at[:, c, :], in0=kk[:, c, :], scalar=-1.0, in1=ee[0][:], op0=ALU.mult, op1=ALU.mult),
                               r=["kk", ek[0]], w=["o_at"])
                            op("dve", lambda e, c=c: e.tensor_tensor(out=o_rt[:, c, :], in0=R(c), in1=ee[1][:], op=ALU.mult), r=["rkv", ek[1]], w=["o_rt"])
                            op("dve", lambda e, c=c: e.tensor_tensor(out=o_bt[:, c, :], in0=beta[:, c, :], in1=ee[2][:], op=ALU.mult), r=["beta%d" % c, ek[2]], w=["o_bt"])
                            op("dve", lambda e, c=c: e.tensor_tensor(out=o_kt[:, c, :], in0=kd[:, c, :], in1=ee[2][:], op=ALU.mult), r=["kd%d" % c, ek[2]], w=["o_kt"])
                            op("dve", lambda e, c=c: e.tensor_tensor(out=o_bh[:, c, :], in0=beta[:, c, :], in1=ee[3][:], op=ALU.mult), r=["beta%d" % c, ek[3]], w=["o_bh"])
                            op("dve", lambda e, c=c: e.tensor_tensor(out=o_kh[:, c, :], in0=kd[:, c, :], in1=ee[3][:], op=ALU.mult), r=["kd%d" % c, ek[3]], w=["o_kh"])
                        for src_t, dst_t, sk, dk, pb0 in ((o_bh, bht, "o_bh", "bht", 0), (o_kh, kht, "o_kh", "kht", 2)):
                            for c in range(8):
                                op("pe", lambda e, c=c, src_t=src_t, pb0=pb0: e.transpose(PF[pb0 + c // 4][:, (c % 4) * 128:(c % 4) * 128 + 128], src_t[:, c, :], identf[:]),
                                   r=[sk, "identf"], w=["pf%d" % (pb0 + c // 4)], inc=(c % 4 == 3))
                            op("dve", lambda e, dst_t=dst_t, pb0=pb0: e.tensor_copy(out=dst_t[:, 0:512], in_=PF[pb0][:, :]), r=["pf%d" % pb0], w=[dk])
                            op("act", lambda e, dst_t=dst_t, pb0=pb0: e.activation(out=dst_t[:, 512:1024], in_=PF[pb0 + 1][:, :], func=AF.Copy), r=["pf%d" % (pb0 + 1)], w=[dk])
                        dma("sp", "b_o0", fm_dram(AT[d], t0, 128), o_at[:], r=["o_at"], w=["AT"])
                        dma("sp", "b_o1", fm_dram(RT[d], t0, 128), o_rt[:], r=["o_rt"], w=["RT"])
                        dma("sp", "b_o2", fm_dram(BT[d], t0, 128), o_bt[:], r=["o_bt"], w=["BT"])
                        dma("sp", "b_o3", fm_dram(KT[d], t0, 128), o_kt[:], r=["o_kt"], w=["KT"])
                        dma("sp", "b_o4", BH[d][t0:t0 + 128, :], bht[:], r=["bht"], w=["BH"])
                        dma("sp", "b_o5", KH[d][t0:t0 + 128, :], kht[:], r=["kht"], w=["KH"])
                        dma("sp", "b_o6", DTO[d][:, :, 2 * i:2 * i + 2], dtt[:], r=["dtt"], w=["DTO"])
                kb.barrier()

        if run("C"):
            with ExitStack() as ph:
                masks = sb("c_masks", [128, 5, 512], BF16, ph)
                dma("pool", "c_mask", masks[:], c_mask.rearrange("m p n -> p m n"), w=["masks"])
                ident64 = identf[0:64, 0:64]
                SU, SL, IU, IL, IDS = range(5)
                satz = [sb("c_atz%d" % i, [128, 2, 8, 128], BF16, ph) for i in range(2)]
                srtz = [sb("c_rtz%d" % i, [128, 2, 8, 128], BF16, ph) for i in range(2)]
                sbt = [sb("c_bt%d" % i, [128, 8, 128], BF16, ph) for i in range(2)]
                skt = [sb("c_kt%d" % i, [128, 8, 128], BF16, ph) for i in range(2)]
                sbhz = [sb("c_bhz%d" % i, [128, 2, D], F32, ph) for i in range(2)]
                skhz = [sb("c_khz%d" % i, [128, 2, D], F32, ph) for i in range(2)]
                svfz = [sb("c_vfz%d" % i, [128, 2, D], F32, ph) for i in range(2)]
                svhz = [sb("c_vhz%d" % i, [128, 2, D], BF16, ph) for i in range(2)]
                sdt = [sb("c_dt%d" % i, [128, 8, 2], F32, ph) for i in range(2)]
                MKB = [sb("c_mkb%d" % i, [128, 16, 64], BF16, ph) for i in range(2)]
                MBR = [sb("c_mbr%d" % i, [128, 16, 64], BF16, ph) for i in range(2)]
                MKR = [sb("c_mkr%d" % i, [128, 16, 64], BF16, ph) for i in range(2)]
                TT = [sb("c_tt%d" % i, [128, 16, 64], BF16, ph) for i in range(2)]
                RZ = [sb("c_rz%d" % i, [128, 16, 64], BF16, ph) for i in range(2)]
                Pm = [sb("c_pm%d" % i, [128, 8, 64], BF16, ph) for i in range(3)]
                Nm = [sb("c_nm%d" % i, [128, 8, 64], BF16, ph) for i in range(3)]
                rtmp = sb("c_rtmp", [128, 512], F32, ph)
                Tm = [sb("c_tm%d" % i, [128, 8, 64], BF16, ph) for i in range(2)]
                Wz = sb("c_wz", [128, 2, D], BF16, ph)
                Uz = sb("c_uz", [128, 2, D], BF16, ph)
                Ufz = sb("c_ufz", [128, 2, D], F32, ph)
                ST = sb("c_st", [128, 8, 64], F32, ph)
                Sbz = sb("c_sbz", [128, 8, 2, 64], BF16, ph)
                S0 = sb("c_s0", [64, 16, 64], F32, ph)
                SO = sb("c_so", [64, 8, 128], F32, ph)
                Yt = [sb("c_y%d" % i, [128, 8, 128], F32, ph) for i in range(2)]
                for i in range(2):
                    for tz in (satz[i], srtz[i], sbhz[i], skhz[i], svhz[i], svfz[i]):
                        op("dve", lambda e, tz=tz: e.memset(tz[:], 0.0), w=["ld%d" % i])
                op("dve", lambda e: e.memset(Wz[:], 0.0), w=["Wz"])
                op("dve", lambda e: e.memset(Uz[:], 0.0), w=["Uz"])
                op("dve", lambda e: e.memset(Ufz[:], 0.0), w=["Ufz"])
                op("dve", lambda e: e.memset(Sbz[:], 0.0), w=["Sbz"])
                H0 = slice(0, 64)
                H1 = slice(64, 128)
                HS = (H0, H1)

                def copy_state_bf16():
                    op("act", lambda e: e.activation(out=Sbz[H0, :, 0, :], in_=ST[H0, :, :], func=AF.Copy), r=["ST"], w=["Sbz"])
                    op("act", lambda e: e.activation(out=Sbz[H1, :, 1, :], in_=ST[H1, :, :], func=AF.Copy), r=["ST"], w=["Sbz"])

                def prep(si, d, ti, b):
                    s0 = SEQS[si][0]
                    t0 = s0 + ti * 128
                    L = "ld%d" % b
                    mM, mN, mI = (SU, SL, IU) if d == 0 else (SL, SU, IL)
                    loads = []
                    for par in range(2):
                        loads.append((satz[b][HS[par], par, :, :], AT[d][HS[par], :, t0:t0 + 128]))
                        loads.append((srtz[b][HS[par], par, :, :], RT[d][HS[par], :, t0:t0 + 128]))
                        loads.append((sbhz[b][HS[par], par, :], BH[d][t0 + 64 * par:t0 + 64 * par + 64, :]))
                        loads.append((skhz[b][HS[par], par, :], KH[d][t0 + 64 * par:t0 + 64 * par + 64, :]))
                        loads.append((svhz[b][HS[par], par, :], VH[t0 + 64 * par:t0 + 64 * par + 64, :]))
                        loads.append((svfz[b][HS[par], par, :], VHF[t0 + 64 * par:t0 + 64 * par + 64, :]))
                    loads.append((sbt[b][:], fm_dram(BT[d], t0, 128)))
                    loads.append((skt[b][:], fm_dram(KT[d], t0, 128)))
                    loads.append((sdt[b][:], DTO[d][:, :, t0 // 64:t0 // 64 + 2]))
                    for j, (dst, src) in enumerate(loads):
                        dma("sp", "c_ld%d_%d" % (b, j), dst, src, w=[L])
                    yield
                    atz, rtz, bt_, kt_ = satz[b], srtz[b], sbt[b], skt[b]
                    for g in range(2):
                        def L_plain(t_):
                            return lambda h, cs: t_[:, h // 2, cs]

                        def L_z(tz_):
                            return lambda h, cs: tz_[:, h % 2, h // 2, cs]

                        gsl = slice(g * 8, g * 8 + 8)

                        def mtype(bank, lf, rf):
                            for hh in range(8):
                                h = g * 8 + hh
                                for cp in range(2):
                                    cs = slice(cp * 64, cp * 64 + 64)
                                    op("pe", lambda e, lf=lf, rf=rf, h=h, hh=hh, cs=cs, bank=bank: e.matmul(
                                        PF[bank][cs, hh * 64:(hh + 1) * 64], lf(h, cs), rf(h, cs), start=True, stop=True, skip_group_check=True),
                                       r=[L], w=["pf%d" % bank], inc=(hh == 7 and cp == 1))

                        mtype(0, L_plain(bt_), L_z(atz))
                        mtype(1, L_z(atz), L_plain(bt_))
                        mtype(2, L_plain(kt_), L_z(atz))
                        mtype(3, L_plain(bt_), L_z(rtz))
                        op("dve", lambda e: e.tensor_tensor(out=Pm[2][:].rearrange("p h t -> p (h t)"), in0=PF[0][:, :], in1=masks[:, mM, :], op=ALU.mult),
                           r=["pf0", "masks"], w=["Pm2"])
                        op("dve", lambda e: e.tensor_tensor(out=Nm[2][:].rearrange("p h t -> p (h t)"), in0=PF[1][:, :], in1=masks[:, mN, :], op=ALU.mult),
                           r=["pf1", "masks"], w=["Nm2"])
                        op("dve", lambda e: e.tensor_tensor(out=MKB[b][:, gsl, :].rearrange("p h t -> p (h t)"), in0=PF[2][:, :], in1=masks[:, mM, :], op=ALU.mult),
                           r=["pf2", "masks"], w=["MKB%d" % b])
                        op("dve", lambda e: e.tensor_tensor(out=MBR[b][:, gsl, :].rearrange("p h t -> p (h t)"), in0=PF[3][:, :], in1=masks[:, mI, :], op=ALU.mult),
                           r=["pf3", "masks"], w=["MBR%d" % b])
                        yield
                        mtype(3, L_plain(kt_), L_z(rtz))
                        op("dve", lambda e: e.tensor_tensor(out=MKR[b][:, gsl, :].rearrange("p h t -> p (h t)"), in0=PF[3][:, :], in1=masks[:, mI, :], op=ALU.mult),
                           r=["pf3", "masks"], w=["MKR%d" % b])
                        cur = 2
                        for lev in range(6):
                            nx = 0 if cur == 2 else 1 - cur
                            last = lev == 5
                            first = lev == 0

                            def blk(kind, cp, bank, cur=cur):
                                cs = slice(cp * 64, cp * 64 + 64)
                                for hh in range(8):
                                    if kind == "P":
                                        lt_, rh_ = Nm[cur], Pm[cur]
                                    elif kind == "N":
                                        lt_, rh_ = Pm[cur], Nm[cur]
                                    else:
                                        lt_, rh_ = Nm[cur], Tm[cur]
                                    rk = ["Nm%d" % cur, "Pm%d" % cur] + (["Tm%d" % cur] if kind == "T" else [])
                                    op("pe", lambda e, hh=hh, cs=cs, lt_=lt_, rh_=rh_, bank=bank: e.matmul(PF[bank][cs, hh * 64:(hh + 1) * 64], lt_[cs, hh, :], rh_[cs, hh, :], start=True, stop=True, skip_group_check=True),
                                       r=rk, w=["pf%d" % bank], inc=(hh == 7))

                            if first:
                                op("dve", lambda e, nx=nx, cur=cur: e.tensor_tensor(out=Tm[nx][:].rearrange("p h t -> p (h t)"), in0=Pm[cur][:].rearrange("p h t -> p (h t)"), in1=masks[:, IDS, :], op=ALU.add),
                                   r=["Pm%d" % cur, "masks"], w=["Tm%d" % nx])
                                blk("P", 0, 0); blk("N", 1, 1); blk("P", 1, 0); blk("N", 0, 1)
                            elif last:
                                blk("T", 0, 2); blk("T", 1, 3)
                            else:
                                blk("P", 0, 0); blk("N", 1, 1); blk("T", 0, 2); blk("P", 1, 0); blk("N", 0, 1); blk("T", 1, 2)
                            if not first:
                                if last:
                                    op("dve", lambda e, nx=nx, cur=cur: e.tensor_tensor(out=Tm[nx][H0].rearrange("p h t -> p (h t)"), in0=PF[2][H0, :], in1=Tm[cur][H0].rearrange("p h t -> p (h t)"), op=ALU.add),
                                       r=["pf2", "Tm%d" % cur], w=["Tm%d" % nx])
                                    op("dve", lambda e, nx=nx, cur=cur: e.tensor_tensor(out=Tm[nx][H1].rearrange("p h t -> p (h t)"), in0=PF[3][H1, :], in1=Tm[cur][H1].rearrange("p h t -> p (h t)"), op=ALU.add),
                                       r=["pf3", "Tm%d" % cur], w=["Tm%d" % nx])
                                else:
                                    op("dve", lambda e, nx=nx, cur=cur: e.tensor_tensor(out=Tm[nx][:].rearrange("p h t -> p (h t)"), in0=PF[2][:, :], in1=Tm[cur][:].rearrange("p h t -> p (h t)"), op=ALU.add),
                                       r=["pf2", "Tm%d" % cur], w=["Tm%d" % nx])
                            if not last:
                                op("act", lambda e, nx=nx: e.activation(out=Pm[nx][:].rearrange("p h t -> p (h t)"), in_=PF[0][:, :], func=AF.Copy), r=["pf0"], w=["Pm%d" % nx])
                                op("act", lambda e, nx=nx: e.activation(out=Nm[nx][:].rearrange("p h t -> p (h t)"), in_=PF[1][:, :], func=AF.Copy), r=["pf1"], w=["Nm%d" % nx])
                            cur = nx
                            yield
                        op("dve", lambda e, cur=cur: e.tensor_copy(out=TT[b][:, gsl, :], in_=Tm[cur][:]), r=["Tm%d" % cur], w=["TT%d" % b])
                        for cp in range(2):
                            cs = slice(cp * 64, cp * 64 + 64)
                            for hh in range(8):
                                op("pe", lambda e, hh=hh, cs=cs, cp=cp, cur=cur: e.matmul(PF[cp][cs, hh * 64:(hh + 1) * 64], Nm[2][cs, hh, :], Tm[cur][cs, hh, :], start=True, stop=True, skip_group_check=True),
                                   r=["Nm2", "Tm%d" % cur], w=["pf%d" % cp], inc=(hh == 7))
                        for cp in range(2):
                            cs = slice(cp * 64, cp * 64 + 64)
                            op("dve", lambda e, cs=cs, cp=cp, cur=cur: e.scalar_tensor_tensor(out=rtmp[cs, :], in0=Tm[cur][cs].rearrange("p h t -> p (h t)"), scalar=-1.0, in1=PF[cp][cs, :], op0=ALU.mult, op1=ALU.add),
                               r=["pf%d" % cp, "Tm%d" % cur], w=["rtmp"])
                            op("dve", lambda e, cs=cs: e.tensor_tensor(out=RZ[b][cs, gsl, :].rearrange("p h t -> p (h t)"), in0=rtmp[cs, :], in1=masks[cs, IDS, :], op=ALU.add),
                               r=["rtmp", "masks"], w=["RZ%d" % b])
                        yield

                def chain(si, d, ti, b, first_tile, last_tile):
                    s0 = SEQS[si][0]
                    t0 = s0 + ti * 128
                    L = "ld%d" % b
                    atz, rtz, bhz, khz, vhz, vfz, dt_ = satz[b], srtz[b], sbhz[b], skhz[b], svhz[b], svfz[b], sdt[b]
                    mkb, mbr, mkr, tt, rz = MKB[b], MBR[b], MKR[b], TT[b], RZ[b]
                    KM = ["MKB%d" % b, "MBR%d" % b, "MKR%d" % b, "TT%d" % b, "RZ%d" % b]
                    if first_tile:
                        if si < 2:
                            op("dve", lambda e: e.memset(ST[:], 0.0), w=["ST"])
                        else:
                            dma("sp", "c_s0", S0[:], st0[d].rearrange("h v k -> v h k"), w=["S0"])
                            for hp in range(8):
                                op("pe", lambda e, hp=hp: e.transpose(PF[4][:, hp * 64:(hp + 1) * 64], S0[:, 2 * hp:2 * hp + 2, :].rearrange("v h k -> v (h k)"), ident64),
                                   r=["S0", "identf"], w=["pf4"], inc=(hp == 7))
                            op("dve", lambda e: e.tensor_copy(out=ST[:], in_=PF[4][:, :].rearrange("p (h v) -> p h v", v=64)), r=["pf4"], w=["ST"])
                        copy_state_bf16()
                        yield
                    yb = Yt[b]
                    yk = "Y%d" % b
                    for cp in ((0, 1) if d == 0 else (1, 0)):
                        cs = slice(cp * 64, cp * 64 + 64)
                        for h in range(16):
                            pw = PF[4 + h // 8]
                            o = pw[cs, (h % 8) * 64:(h % 8) * 64 + 64]
                            op("pe", lambda e, o=o, h=h, cs=cs: e.matmul(o, atz[:, h % 2, h // 2, cs], Sbz[:, h // 2, h % 2, :], start=True, stop=False, skip_group_check=True),
                               r=[L, "Sbz"], w=["pf%d" % (4 + h // 8)], inc=False)
                            op("pe", lambda e, o=o, h=h, cp=cp: e.matmul(o, mkb[:, h, :], vhz[:, cp, h * 64:(h + 1) * 64], start=False, stop=True, skip_group_check=True),
                               r=[L, KM[0]], w=["pf%d" % (4 + h // 8)], inc=(h % 8 == 7))
                        op("act", lambda e, cs=cs, cp=cp: e.activation(out=Wz[cs, cp, 0:512], in_=PF[4][cs, :], func=AF.Copy), r=["pf4"], w=["Wz"])
                        op("dve", lambda e, cs=cs, cp=cp: e.tensor_copy(out=Wz[cs, cp, 512:1024], in_=PF[5][cs, :]), r=["pf5"], w=["Wz"])
                        yield
                        for h in range(16):
                            pu = PF[6 + h // 8]
                            op("pe", lambda e, pu=pu, h=h, cs=cs, cp=cp: e.matmul(pu[cs, (h % 8) * 64:(h % 8) * 64 + 64], tt[:, h, :], Wz[:, cp, h * 64:(h + 1) * 64], start=True, stop=True, skip_group_check=True),
                               r=[KM[3], "Wz"], w=["pf%d" % (6 + h // 8)], inc=(h % 8 == 7))
                        op("act", lambda e, cs=cs, cp=cp: e.activation(out=Uz[cs, cp, 0:512], in_=PF[6][cs, :], func=AF.Copy), r=["pf6"], w=["Uz"])
                        op("dve", lambda e, cs=cs, cp=cp: e.tensor_copy(out=Uz[cs, cp, 512:1024], in_=PF[7][cs, :]), r=["pf7"], w=["Uz"])
                        op("act", lambda e, cs=cs, cp=cp: e.activation(out=Ufz[cs, cp, 0:512], in_=PF[6][cs, :], func=AF.Copy), r=["pf6"], w=["Ufz"])
                        op("dve", lambda e, cs=cs, cp=cp: e.tensor_copy(out=Ufz[cs, cp, 512:1024], in_=PF[7][cs, :]), r=["pf7"], w=["Ufz"])
                        yield
                        for h in range(16):
                            pu = PF[6 + h // 8]
                            op("pe", lambda e, pu=pu, h=h, cs=cs, cp=cp: e.matmul(pu[cs, (h % 8) * 64:(h % 8) * 64 + 64], rz[:, h, :], Uz[:, cp, h * 64:(h + 1) * 64], start=True, stop=True, skip_group_check=True),
                               r=[KM[4], "Uz"], w=["pf%d" % (6 + h // 8)], inc=(h % 8 == 7))
                        for half in range(2):
                            hsl = slice(half * 512, half * 512 + 512)
                            op("dve", lambda e, cs=cs, cp=cp, half=half, hsl=hsl: e.tensor_tensor(out=Ufz[cs, cp, hsl], in0=PF[6 + half][cs, :], in1=Ufz[cs, cp, hsl], op=ALU.add),
                               r=["pf%d" % (6 + half), "Ufz"], w=["Ufz"])
                            op("act", lambda e, cs=cs, cp=cp, hsl=hsl: e.activation(out=Uz[cs, cp, hsl], in_=Ufz[cs, cp, hsl], func=AF.Copy), r=["Ufz"], w=["Uz"])
                        yield
                        for h in range(16):
                            hs = HS[h % 2]
                            hv = slice(h * 64, h * 64 + 64)
                            oy = PF[4][hs, (h // 2) * 64:(h // 2) * 64 + 64]
                            op("pe", lambda e, oy=oy, h=h, cs=cs: e.matmul(oy, Sbz[:, h // 2, h % 2, :], rtz[:, h % 2, h // 2, cs], start=True, stop=False, skip_group_check=True),
                               r=["Sbz", L], w=["pf4"], inc=False)
                            op("pe", lambda e, oy=oy, hv=hv, h=h, cp=cp: e.matmul(oy, Uz[:, cp, hv], mbr[:, h, :], start=False, stop=False, skip_group_check=True),
                               r=["Uz", KM[1]], w=["pf4"], inc=False)
                            op("pe", lambda e, oy=oy, hv=hv, h=h, cp=cp: e.matmul(oy, vhz[:, cp, hv], mkr[:, h, :], start=False, stop=True, skip_group_check=True),
                               r=[L, KM[2]], w=["pf4"], inc=(h == 15))
                        for h in range(16):
                            hs = HS[h % 2]
                            hv = slice(h * 64, h * 64 + 64)
                            osn = PF[5][hs, (h // 2) * 64:(h // 2) * 64 + 64]
                            op("pe", lambda e, osn=osn, hv=hv, cp=cp: e.matmul(osn, bhz[:, cp, hv], Ufz[:, cp, hv], start=True, stop=False, skip_group_check=True),
                               r=[L, "Ufz"], w=["pf5"], inc=False)
                            op("pe", lambda e, osn=osn, hv=hv, cp=cp: e.matmul(osn, khz[:, cp, hv], vfz[:, cp, hv], start=False, stop=True, skip_group_check=True),
                               r=[L], w=["pf5"], inc=(h == 15))
                        op("act", lambda e, yb=yb, cs=cs: e.activation(out=yb[:, :, cs], in_=PF[4][:, :].rearrange("p (c t) -> p c t", t=64), func=AF.Copy), r=["pf4"], w=[yk])
                        for hp in range(8):
                            op("dve", lambda e, hp=hp, cp=cp: e.scalar_tensor_tensor(out=ST[:, hp, :], in0=ST[:, hp, :], scalar=dt_[:, hp, cp:cp + 1], in1=PF[5][:, hp * 64:(hp + 1) * 64], op0=ALU.mult, op1=ALU.add),
                               r=["ST", L, "pf5"], w=["ST"])
                        copy_state_bf16()
                        yield
                    dma("sp", "c_y%d" % b, fm_dram(YS[d], t0, 128), yb[:], r=[yk], w=["YS"])
                    if last_tile and si < 2:
                        for hp in range(8):
                            op("pe", lambda e, hp=hp: e.transpose(PF[4 + hp // 4][0:64, (hp % 4) * 128:(hp % 4) * 128 + 128], ST[:, hp, :], identf[:]),
                               r=["ST", "identf"], w=["pf%d" % (4 + hp // 4)], inc=(hp % 4 == 3))
                        op("dve", lambda e: e.tensor_copy(out=SO[:, 0:4, :].rearrange("p a b -> p (a b)"), in_=PF[4][0:64, :]), r=["pf4"], w=["SO"])
                        op("dve", lambda e: e.tensor_copy(out=SO[:, 4:8, :].rearrange("p a b -> p (a b)"), in_=PF[5][0:64, :]), r=["pf5"], w=["SO"])
                        dma("sp", "c_so", sto[si, d].rearrange("h v k -> v h k"), SO[:].rearrange("v a (h k) -> v (a h) k", k=64), r=["SO"], w=["sto"])
                        yield

                units = []
                for si, (s0_, sl_) in enumerate(SEQS):
                    ntl = sl_ // 128
                    for d in range(2):
                        order = list(range(ntl)) if d == 0 else list(range(ntl - 1, -1, -1))
                        for j, ti in enumerate(order):
                            units.append((si, d, ti, j == 0, j == ntl - 1))
                PIPE = int(os.environ.get("C_PIPE", "1"))
                for _ in prep(units[0][0], units[0][1], units[0][2], 0):
                    pass
                for j, (si, d, ti, ft, ltile) in enumerate(units):
                    b = j % 2
                    g1 = chain(si, d, ti, b, ft, ltile)
                    g2 = prep(units[j + 1][0], units[j + 1][1], units[j + 1][2], 1 - b) if j + 1 < len(units) else iter(())
                    if not PIPE:
                        for _ in g1:
                            pass
                        for _ in g2:
                            pass
                        continue
                    a_done = b_done = False
                    while not (a_done and b_done):
                        if not b_done:
                            try:
                                next(g2)
                            except StopIteration:
                                b_done = True
                        if not a_done:
                            try:
                                next(g1)
                            except StopIteration:
                                a_done = True
                kb.barrier()

        if run("D"):
            with ExitStack() as ph:
                w_pg = sb("w_pg", [128, 8, 2560], BF16, ph)
                w_o = sb("w_o", [128, 8, D], BF16, ph)
                pws = sb("pws", [128, 4, 256], BF16, ph)
                pms = sb("pms", [128, 2, 4, 2, 256], BF16, ph)
                dma("pool", "d_w0", w_pg[:, :, 0:512], w_in[:, 0:512].rearrange("(c p) n -> p c n", p=128), w=["dw"])
                for q in range(2):
                    dma("pool", "d_w0", w_pg[:, :, 512 + q * 1024:512 + (q + 1) * 1024],
                        w_in[:, 3584 + q * 1024:3584 + (q + 1) * 1024].rearrange("(c p) n -> p c n", p=128), w=["dw"])
                dma("pool", "d_w0", w_o[:], w_out.rearrange("(c p) n -> p c n", p=128), w=["dw"])
                dma("pool", "d_w0", pws[:], pool_w.rearrange("g c n -> c g n"), w=["dw"])
                dma("pool", "d_w0", pms[:], c_pm.rearrange("k g p s t -> p k g s t"), w=["dw"])
                hT = sb("d_hT", [128, 8, 256], BF16, ph)
                yf = sb("d_yf", [128, 8, 256], F32, ph)
                yb2 = sb("d_yb", [128, 8, 256], F32, ph)
                bon = sb("d_bon", [128, 8, 256], F32, ph)
                gt = sb("d_gt", [128, 8, 256], F32, ph)
                xt = sb("d_x", [128, 2, D], F32, ph)
                yc2 = sb("d_yc", [128, 8, 256], F32, ph)
                gA = sb("d_gA", [128, 8, 256], F32, ph)
                gB = sb("d_gB", [128, 8, 256], F32, ph)
                zp = sb("d_zp", [128, 2, 512], BF16, ph)
                mixT = sb("d_mixT", [128, 4, 256], BF16, ph)
                t1 = sb("d_t1", [128, 256], F32, ph)
                mT = sb("d_mT", [128, 8, 256], BF16, ph)
                x1 = sb("d_x1", [128, 2, D], F32, ph)
                tmp = sb("d_tmp", [128, 512], F32, ph)
                ss = sb("d_ss", [128, 1], F32, ph)
                rstd = sb("d_rstd", [128, 1], F32, ph)
                xn = sb("d_xn", [128, D], F32, ph)
                h2 = sb("d_h2", [128, 8, 128], BF16, ph)
                eps_ln = sb("d_eps", [128, 1], F32, ph)
                op("dve", lambda e: e.memset(eps_ln[:], 64e-5), w=["eps_ln"])
                for blk in range(NT // 256):
                    t0 = blk * 256
                    cv = 0 if blk < 2 else 1
                    kind = 0 if blk < 2 else 1
                    dma("sp", "d_l0", hT[:], fm_dram(HT, t0, 256), r=[], w=["hT"])
                    dma("sp", "d_l1", yf[:], fm_dram(YS[0], t0, 256), w=["yf"])
                    dma("sp", "d_l2", yb2[:], fm_dram(YS[1], t0, 256), w=["yb"])
                    dma("sp", "d_l3", bon[:], fm_dram(BON, t0, 256), w=["bon"])
                    dma("sp", "d_l4", gt[:], fm_dram(GT, t0, 256), w=["gt"])
                    dma("sp", "d_l5", xt[:], xc[t0:t0 + 256, :].rearrange("(a p) n -> p a n", p=128), w=["xt"])
                    op("dve", lambda e: e.tensor_tensor(out=yf[:], in0=yf[:], in1=yb2[:], op=ALU.add), r=["yf", "yb"], w=["yf"])
                    for q in range(4):
                        for j in range(2):
                            c = q * 2 + j
                            op("pe", lambda e, c=c, j=j, q=q: e.matmul(PF[q % 2][:, j * 256:(j + 1) * 256], bdones[:], yf[:, c, :], start=True, stop=True),
                               r=["bdones", "yf"], w=["pf%d" % (q % 2)], inc=(j == 1))
                        op("dve", lambda e, q=q: e.scalar_tensor_tensor(out=yc2[:, 2 * q:2 * q + 2, :], in0=PF[q % 2][:, :].rearrange("p (j t) -> p j t", t=256), scalar=-1.0 / 64, in1=yf[:, 2 * q:2 * q + 2, :], op0=ALU.mult, op1=ALU.add),
                           r=["pf%d" % (q % 2), "yf"], w=["yc"])
                    op("act", lambda e: e.activation(out=yb2[:], in_=yc2[:], func=AF.Square), r=["yc"], w=["yb"])
                    for q in range(4):
                        for j in range(2):
                            c = q * 2 + j
                            op("pe", lambda e, c=c, j=j, q=q: e.matmul(PF[q % 2][:, j * 256:(j + 1) * 256], bdones[:], yb2[:, c, :], start=True, stop=True),
                               r=["bdones", "yb"], w=["pf%d" % (q % 2)], inc=(j == 1))
                        op("act", lambda e, q=q: e.activation(out=yf[:, 2 * q:2 * q + 2, :], in_=PF[q % 2][:, :].rearrange("p (j t) -> p j t", t=256), func=AF.Ln, scale=1.0 / 64, bias=eps_ln[:]),
                           r=["pf%d" % (q % 2), "eps_ln"], w=["yf"])
                    op("act", lambda e: e.activation(out=yf[:], in_=yf[:], func=AF.Exp, scale=-0.5), r=["yf"], w=["yf"])
                    op("dve", lambda e: e.tensor_tensor(out=yc2[:], in0=yc2[:], in1=yf[:], op=ALU.mult), r=["yc", "yf"], w=["yc"])
                    for c in range(8):
                        op("dve", lambda e, c=c: e.tensor_scalar(out=yc2[:, c, :], in0=yc2[:, c, :], scalar1=colc("ln_w", c), scalar2=colc("ln_b", c), op0=ALU.mult, op1=ALU.add),
                           r=["yc", "cols"], w=["yc"])
                    op("dve", lambda e: e.tensor_tensor(out=yc2[:], in0=yc2[:], in1=bon[:], op=ALU.add), r=["yc", "bon"], w=["yc"])
                    op("dve", lambda e: e.tensor_tensor(out=yc2[:], in0=yc2[:], in1=gt[:], op=ALU.mult), r=["yc", "gt"], w=["yc"])
                    for fc in range(16):
                        pf = PF[2 + fc % 2]
                        pk = "pf%d" % (2 + fc % 2)
                        for kc in range(8):
                            op("pe", lambda e, fc=fc, kc=kc, pf=pf: e.matmul(pf[:, 0:256], w_pg[:, kc, 512 + fc * 128:512 + (fc + 1) * 128], hT[:, kc, :], start=(kc == 0), stop=(kc == 7)),
                               r=["dw", "hT"], w=[pk], inc=(kc == 7))
                        dst = gA if fc < 8 else gB
                        op("act", lambda e, fc=fc, pf=pf, dst=dst: e.activation(out=dst[:, fc % 8, :], in_=pf[:, 0:256], func=AF.Sigmoid), r=[pk], w=["gA" if fc < 8 else "gB"])
                    op("dve", lambda e: e.tensor_tensor(out=yc2[:], in0=yc2[:], in1=gB[:], op=ALU.mult), r=["yc", "gB"], w=["yc"])
                    for a in range(2):
                        for kc in range(8):
                            op("pe", lambda e, a=a, kc=kc: e.matmul(PF[4][:, :], hT[:, kc, a * 128:(a + 1) * 128], w_pg[:, kc, 0:512], start=(kc == 0), stop=(kc == 7)),
                               r=["dw", "hT"], w=["pf4"], inc=(kc == 7))
                        op("act", lambda e, a=a: e.activation(out=zp[:, a, :], in_=PF[4][:, :], func=AF.Copy), r=["pf4"], w=["zp"])
                    for g in range(4):
                        for a in range(2):
                            op("pe", lambda e, g=g, a=a: e.matmul(PF[5][:, 0:256], zp[:, a, g * 128:(g + 1) * 128], pms[:, kind, g, a, :], start=(a == 0), stop=(a == 1)),
                               r=["zp", "dw"], w=["pf5"], inc=(a == 1))
                        op("act", lambda e, g=g: e.activation(out=mixT[:, g, :], in_=PF[5][:, 0:256], func=AF.Copy), r=["pf5"], w=["mixT"])
                    for dc in range(8):
                        g = dc // 2
                        pf = PF[dc % 2]
                        pk = "pf%d" % (dc % 2)
                        op("pe", lambda e, dc=dc, g=g, pf=pf: e.matmul(pf[:, 0:256], pws[:, g, (dc % 2) * 128:(dc % 2) * 128 + 128], mixT[:, g, :], start=True, stop=True),
                           r=["dw", "mixT"], w=[pk])
                        op("dve", lambda e, dc=dc, pf=pf: e.scalar_tensor_tensor(out=t1[:], in0=pf[:, 0:256], scalar=colc("pool_scale", dc), in1=gA[:, dc, :], op0=ALU.mult, op1=ALU.mult),
                           r=[pk, "gA", "cols"], w=["t1"])
                        op("dve", lambda e, dc=dc: e.tensor_tensor(out=mT[:, dc, :], in0=t1[:], in1=yc2[:, dc, :], op=ALU.add), r=["t1", "yc"], w=["mT"])
                    for a in range(2):
                        for half in range(2):
                            pf = PF[2 + half]
                            pk = "pf%d" % (2 + half)
                            for cc in range(8):
                                op("pe", lambda e, a=a, half=half, cc=cc, pf=pf: e.matmul(pf[:, :], mT[:, cc, a * 128:(a + 1) * 128], w_o[:, cc, half * 512:(half + 1) * 512], start=(cc == 0), stop=(cc == 7)),
                                   r=["mT", "dw"], w=[pk], inc=(cc == 7))
                            op("dve", lambda e, a=a, half=half, pf=pf: e.tensor_tensor(out=tmp[:], in0=pf[:, :], in1=GA1[cv][:, half * 512:(half + 1) * 512], op=ALU.mult),
                               r=[pk, "GA"], w=["tmp"])
                            op("dve", lambda e, a=a, half=half: e.tensor_tensor(out=x1[:, a, half * 512:(half + 1) * 512], in0=tmp[:], in1=xt[:, a, half * 512:(half + 1) * 512], op=ALU.add),
                               r=["tmp", "xt"], w=["x1"])
                        norm_transpose(ph, x1[:, a, :], "x1", A2, B2, cv, lambda c: h2[:, c, :], "h2", (ss, rstd, xn))
                        dma("sp", "d_s0", fm_dram(H2T, t0 + a * 128, 128), h2[:], r=["h2"], w=["H2T"])
                    dma("sp", "d_s1", X1[t0:t0 + 256, :].rearrange("(a p) n -> p a n", p=128), x1[:], r=["x1"], w=["X1"])
                kb.barrier()

        if run("E"):
            with ExitStack() as ph:
                wf1 = sb("wf1", [128, 8, 4096], BF16, ph)
                wf2 = sb("wf2", [128, 32, D], BF16, ph)
                for q in range(8):
                    dma("pool", "e_w1_%d" % q, wf1[:, :, q * 512:(q + 1) * 512], w_ff1[:, q * 512:(q + 1) * 512].rearrange("(c p) n -> p c n", p=128), w=["wf1_%d" % q])
                for q in range(8):
                    dma("pool", "e_w2_%d" % q, wf2[:, q * 4:(q + 1) * 4, :], w_ff2[q * 512:(q + 1) * 512, :].rearrange("(c p) n -> p c n", p=128), w=["wf2_%d" % q])
                h2 = sb("e_h2", [128, 8, 256], BF16, ph)
                x1 = sb("e_x1", [128, 2, D], F32, ph)
                u0 = [sb("e_u0%d" % i, [128, 512], BF16, ph) for i in range(2)]
                uT = sb("e_uT", [128, 32, 256], BF16, ph)
                tmp = sb("e_tmp", [128, 512], F32, ph)
                x2 = sb("e_x2", [128, D], F32, ph)
                junk = sb("e_junk", [128, D], BF16, ph)
                ss = sb("e_ss", [128, 1], F32, ph)
                rstd = sb("e_rstd", [128, 1], F32, ph)
                ot = [sb("e_o%d" % i, [128, D], F32, ph) for i in range(2)]
                for blk in range(NT // 256):
                    t0 = blk * 256
                    cv = 0 if blk < 2 else 1
                    dma("sp", "e_l0", h2[:], fm_dram(H2T, t0, 256), w=["h2"])
                    dma("sp", "e_l1", x1[:], X1[t0:t0 + 256, :].rearrange("(a p) n -> p a n", p=128), w=["x1"])
                    for fp in range(16):
                        pf = PF[fp % 2]
                        pk = "pf%d" % (fp % 2)
                        for j in range(2):
                            fc = fp * 2 + j
                            for kc in range(8):
                                op("pe", lambda e, fc=fc, kc=kc, j=j, pf=pf: e.matmul(pf[:, j * 256:(j + 1) * 256], wf1[:, kc, fc * 128:(fc + 1) * 128], h2[:, kc, :], start=(kc == 0), stop=(kc == 7)),
                                   r=["wf1_%d" % (fc // 4), "h2"], w=[pk], inc=(kc == 7 and j == 1))
                        u = u0[fp % 2]
                        uk = "u0%d" % (fp % 2)
                        op("act", lambda e, pf=pf, u=u: e.activation(out=u[:], in_=pf[:, :], func=AF.Relu), r=[pk], w=[uk])
                        op("dve", lambda e, fp=fp, u=u: e.tensor_tensor(out=uT[:, 2 * fp:2 * fp + 2, :].rearrange("p j t -> p (j t)"), in0=u[:], in1=u[:], op=ALU.mult), r=[uk], w=["uT"])
                    for a in range(2):
                        for half in range(2):
                            pf = PF[2 + half]
                            pk = "pf%d" % (2 + half)
                            for fc in range(32):
                                op("pe", lambda e, a=a, half=half, fc=fc, pf=pf: e.matmul(pf[:, :], uT[:, fc, a * 128:(a + 1) * 128], wf2[:, fc, half * 512:(half + 1) * 512], start=(fc == 0), stop=(fc == 31)),
                                   r=["uT", "wf2_%d" % (fc // 4)], w=[pk], inc=(fc == 31))
                            op("dve", lambda e, half=half, pf=pf: e.tensor_tensor(out=tmp[:], in0=pf[:, :], in1=GA2[cv][:, half * 512:(half + 1) * 512], op=ALU.mult),
                               r=[pk, "GA"], w=["tmp"])
                            op("dve", lambda e, a=a, half=half: e.tensor_tensor(out=x2[:, half * 512:(half + 1) * 512], in0=tmp[:], in1=x1[:, a, half * 512:(half + 1) * 512], op=ALU.add),
                               r=["tmp", "x1"], w=["x2"])
                        op("act", lambda e: e.activation(out=junk[:], in_=x2[:], func=AF.Square, accum_out=ss[:]), r=["x2"], w=["junk", "ss"])
                        op("dve", lambda e: e.tensor_scalar(out=rstd[:], in0=ss[:], scalar1=1.0 / D, scalar2=1e-6, op0=ALU.mult, op1=ALU.add), r=["ss"], w=["rstd"])
                        op("act", lambda e: e.activation(out=rstd[:], in_=rstd[:], func=AF.Sqrt), r=["rstd"], w=["rstd"])
                        op("dve", lambda e: e.reciprocal(out=rstd[:], in_=rstd[:]), r=["rstd"], w=["rstd"])
                        o = ot[a]
                        okey = "ot%d" % a
                        op("act", lambda e, o=o: e.activation(out=o[:], in_=x2[:], func=AF.Copy, scale=rstd[:]), r=["x2", "rstd"], w=[okey])
                        op("dve", lambda e, o=o: e.tensor_tensor(out=o[:], in0=o[:], in1=GF[:], op=ALU.mult), r=[okey, "GF"], w=[okey])
                        dma("sp", "e_o%d" % a, yc[t0 + a * 128:t0 + (a + 1) * 128, :], o[:], r=[okey], w=["yc"])
        kb.finish()
    kb.es.close()
    return nc, kb


def _constants():
    identf = np.eye(128, dtype=np.float32)
    tri = np.zeros((2, 128, 384), np.float32)
    C = -float(np.exp(-0.5))
    for t in range(128):
        for u in range(128):
            if t // 64 != u // 64:
                continue
            tri[0, t, u] = C if t <= u else 0.0
            tri[0, t, 128 + u] = C if t < u else 0.0
            tri[0, t, 256 + u] = C if t > u else 0.0
            tri[1, t, u] = C if t >= u else 0.0
            tri[1, t, 128 + u] = C if t > u else 0.0
            tri[1, t, 256 + u] = C if t < u else 0.0
    s = np.arange(64)[:, None]
    t = np.arange(64)[None, :]
    base = [(s < t), (s > t), (s <= t), (s >= t), (s == t)]
    mask = np.zeros((5, 128, 512), np.float32)
    for m in range(5):
        blk = base[m].astype(np.float32)
        mask[m] = np.tile(np.tile(blk, (2, 1)), (1, 8))
    bd = np.zeros((128, 128), np.float32)
    bd[:64, :64] = 1.0
    bd[64:, 64:] = 1.0
    pm = np.zeros((2, 4, 256, 256), np.float32)
    for kind, lr in ((0, 256), (1, 64)):
        for g, win in enumerate(WINS):
            for tt in range(256):
                r0 = (tt // lr) * lr
                tl = tt - r0
                lo = min(max(tl - win // 2, 0), lr)
                hi = min(max(tl + win - win // 2, 0), lr)
                pm[kind, g, r0 + lo:r0 + hi, tt] += 1.0 / (hi - lo)
                pm[kind, g, tt, tt] -= 1.0
    pm = pm.reshape(2, 4, 2, 128, 256).transpose(0, 1, 3, 2, 4)
    return identf, tri, mask, bd, np.ascontiguousarray(pm)


_CACHE = {}


def kernel(x_prompt, x_sample, state_rwkv, c, c_ctx, w_ada, b_ada, g_norm1, g_norm2, w_in,
           mu_rkv, mu_wag, w_dec0, w_dec1, w_dec2, a0, a1, a2, gate_w1, gate_w2, k_k, k_a,
           r_k, ln_x_w, ln_x_b, pool_w, pool_scale, w_out, w_ff1, w_ff2, g_final):
    in_maps = _prep(x_prompt, x_sample, state_rwkv, c, c_ctx, w_ada, b_ada, g_norm1, g_norm2, w_in,
                    mu_rkv, mu_wag, w_dec0, w_dec1, w_dec2, a0, a1, a2, gate_w1, gate_w2, k_k, k_a,
                    r_k, ln_x_w, ln_x_b, pool_w, pool_scale, w_out, w_ff1, w_ff2, g_final)
    if "nc" not in _CACHE:
        _CACHE["nc"] = build()[0]
    nc = _CACHE["nc"]
    res = run_bass_kernel_spmd(nc, in_maps, core_ids=list(range(8)))
    outs = res.results
    y_prompt = np.zeros((16, 256, D), np.float32)
    y_sample = np.zeros((4, 2048, D), np.float32)
    st_new = np.zeros((16, 1, 2, 16, 64, 64), np.float32)
    for core in range(8):
        yc_ = np.asarray(outs[core]["yc"], dtype=np.float32)
        y_prompt[2 * core] = yc_[0:256]
        y_prompt[2 * core + 1] = yc_[256:512]
        if core < 4:
            y_sample[core] = yc_[512:]
        so = np.asarray(outs[core]["sto"], dtype=np.float32)
        st_new[2 * core, 0] = so[0]
        st_new[2 * core + 1, 0] = so[1]
    return y_prompt, y_sample, st_new


def _prep(x_prompt, x_sample, state_rwkv, c, c_ctx, w_ada, b_ada, g_norm1, g_norm2, w_in,
          mu_rkv, mu_wag, w_dec0, w_dec1, w_dec2, a0, a1, a2, gate_w1, gate_w2, k_k, k_a,
          r_k, ln_x_w, ln_x_b, pool_w, pool_scale, w_out, w_ff1, w_ff2, g_final):
    f = lambda a: np.ascontiguousarray(np.asarray(a, dtype=np.float32))
    x_prompt, x_sample, state_rwkv = f(x_prompt), f(x_sample), f(state_rwkv)
    identf, tri, mask, bd, pm = _constants()
    b_ada6 = f(b_ada).reshape(6, D)
    shared = {
        "w_ada": f(w_ada)[0], "w_in": f(w_in)[0], "w_dec0": f(w_dec0)[0], "w_dec1": f(w_dec1)[0],
        "w_dec2": f(w_dec2)[0], "a1": f(a1)[0], "a2": f(a2)[0], "gate_w1": f(gate_w1)[0],
        "gate_w2": f(gate_w2)[0], "pool_w": f(pool_w)[0], "w_out": f(w_out)[0], "w_ff1": f(w_ff1)[0],
        "w_ff2": f(w_ff2)[0], "g_final": f(g_final).reshape(1, D), "b_ada": b_ada6,
        "c_identb": identf, "c_identf": identf, "c_tri": tri, "c_mask": mask, "c_bd": bd, "c_pm": pm,
    }
    in_maps = []
    for core in range(8):
        li = core % 4
        rows = np.zeros((NR, D), np.float32)
        vals = {"g_norm1": f(g_norm1)[0], "g_norm2": f(g_norm2)[0], "mu_r": f(mu_rkv)[0, 0], "mu_k": f(mu_rkv)[0, 1],
                "mu_v": f(mu_rkv)[0, 2], "mu_w": f(mu_wag)[0, 0], "mu_a": f(mu_wag)[0, 1], "mu_g": f(mu_wag)[0, 2],
                "a0f": f(a0)[0, 0], "a0b": f(a0)[0, 1], "k_k": f(k_k)[0], "k_a": f(k_a)[0], "r_k": f(r_k)[0].reshape(-1),
                "ln_w": f(ln_x_w)[0], "ln_b": f(ln_x_b)[0], "pool_scale": f(pool_scale)[0],
                "sh1": b_ada6[0], "sc1": b_ada6[1], "ga1": b_ada6[2], "sh2": b_ada6[3], "sc2": b_ada6[4], "ga2": b_ada6[5],
                "c0": f(c_ctx), "c1": f(c)[li]}
        for n, v in vals.items():
            rows[RI[n]] = v
        xcore = np.concatenate([x_prompt[2 * core], x_prompt[2 * core + 1], x_sample[li]], axis=0)
        m = dict(shared)
        m["xc"] = np.ascontiguousarray(xcore)
        m["rows"] = rows
        m["st0"] = np.ascontiguousarray(state_rwkv[li, 0])
        in_maps.append(m)
    return in_maps
```

```python
import os
import numpy as np
from contextlib import ExitStack
import concourse.bass as bass
import concourse.mybir as mybir
from concourse.bass_utils import run_bass_kernel_spmd

F32 = mybir.dt.float32
BF16 = mybir.dt.bfloat16
AF = mybir.ActivationFunctionType
ALU = mybir.AluOpType

NT = 2560
NTILE = NT // 128
D = 1024
SEQS = [(0, 256), (256, 256), (512, 2048)]
WINS = (2, 4, 8, 16)
ROWNAMES = ["g_norm1", "g_norm2", "mu_r", "mu_k", "mu_v", "mu_w", "mu_a", "mu_g", "a0f", "a0b",
            "k_k", "k_a", "r_k", "ln_w", "ln_b", "pool_scale", "sh1", "sc1", "ga1", "sh2", "sc2",
            "ga2", "c0", "c1"]
RI = {n: i for i, n in enumerate(ROWNAMES)}
NR = 32


def seq_of_tile(i):
    t = i * 128
    for si, (s0, ln) in enumerate(SEQS):
        if s0 <= t < s0 + ln:
            return si
    raise ValueError


SAME_ENGINE_WAITS = int(os.environ.get("SEW", "1"))


class KB:
    def __init__(self, nc):
        self.nc = nc
        self.es = ExitStack()
        self.E = {"pe": nc.tensor, "act": nc.scalar, "dve": nc.vector, "pool": nc.gpsimd, "sp": nc.sync}
        self.sems = {}
        self.cnt = {}
        self.seen = {e: {} for e in self.E}
        self.lastw = {}
        self.readers = {}
        self.pending = {e: {} for e in self.E}
        self.epoch = {e: 0 for e in self.E}
        self.pe_r = set()
        self.pe_w = set()
        self.n_ins = 0

    def sem(self, name):
        if name not in self.sems:
            self.sems[name] = self.es.enter_context(self.nc.semaphore(name))
            self.cnt[name] = 0
        return self.sems[name]

    def _deps(self, r, w):
        d = {}

        def add(s, v):
            if d.get(s, 0) < v:
                d[s] = v

        for k in r:
            if k in self.lastw:
                add(*self.lastw[k])
        for k in w:
            if k in self.lastw:
                add(*self.lastw[k])
            for s, v in self.readers.get(k, {}).items():
                add(s, v)
        return d

    def _emit_waits(self, eng, d):
        for s, v in self.pending[eng].items():
            if d.get(s, 0) < v:
                d[s] = v
        self.pending[eng] = {}
        for s, v in d.items():
            if eng == "pe" and s.startswith("c_pe"):
                continue
            if SAME_ENGINE_WAITS == 0 and s.startswith("c_" + eng):
                continue
            if self.seen[eng].get(s, 0) >= v:
                continue
            self.E[eng].wait_ge(self.sems[s], v)
            self.seen[eng][s] = v
            self.n_ins += 1

    def _record(self, tok, r, w):
        s, v = tok
        for k in r:
            rd = self.readers.setdefault(k, {})
            if rd.get(s, 0) < v:
                rd[s] = v
        for k in w:
            self.lastw[k] = tok
            self.readers[k] = {}

    def op(self, eng, fn, r=(), w=(), inc=True):
        d = self._deps(r, w)
        self._emit_waits(eng, d)
        ins = fn(self.E[eng])
        self.n_ins += 1
        if eng == "pe" and not inc:
            self.pe_r.update(r)
            self.pe_w.update(w)
            return
        name = "c_%s%d" % (eng, self.epoch[eng])
        sem = self.sem(name)
        self.cnt[name] += 1
        ins.then_inc(sem, 1)
        tok = (name, self.cnt[name])
        if eng == "pe":
            r = set(r) | self.pe_r
            w = set(w) | self.pe_w
            self.pe_r = set()
            self.pe_w = set()
        self._record(tok, r, w)
        if self.cnt[name] >= 30000:
            self.epoch[eng] += 1

    def dma(self, q, slot, out, in_, r=(), w=(), **kw):
        d = self._deps(r, w)
        self._emit_waits(q, d)
        name = "d_" + slot
        sem = self.sem(name)
        self.cnt[name] += 16
        self.E[q].dma_start(out=out, in_=in_, **kw).then_inc(sem, 16)
        self.n_ins += 1
        self._record((name, self.cnt[name]), r, w)

    def barrier(self):
        assert not self.pe_r and not self.pe_w
        toks = {n: c for n, c in self.cnt.items() if c > 0}
        for e in self.E:
            self.pending[e] = dict(toks)
        self.lastw = {}
        self.readers = {}

    def finish(self):
        self.barrier()
        for e in self.E:
            self._emit_waits(e, {})


def build(debug=False, stop_after="E"):
    run = lambda p: "MABCDE".index(p) <= "MABCDE".index(stop_after)
    nc = bass.Bass("TRN2", target_bir_lowering=False)
    kb = KB(nc)
    dram = {}

    def din(name, shape, dt=F32):
        dram[name] = nc.dram_tensor(name, list(shape), dt, kind="ExternalInput").ap()
        return dram[name]

    def dout(name, shape, dt=F32):
        dram[name] = nc.dram_tensor(name, list(shape), dt, kind="ExternalOutput").ap()
        return dram[name]

    def dscr(name, shape, dt):
        kind = "ExternalOutput" if debug else "Internal"
        dram[name] = nc.dram_tensor(name, list(shape), dt, kind=kind).ap()
        return dram[name]

    xc = din("xc", [NT, D])
    rows_in = din("rows", [NR, D])
    st0 = din("st0", [2, 16, 64, 64])
    w_ada = din("w_ada", [D, 6 * D])
    w_in = din("w_in", [D, 5632])
    wd0 = din("w_dec0", [2, D])
    wd1 = din("w_dec1", [2, D, 64])
    wd2 = din("w_dec2", [2, 64, D])
    a1 = din("a1", [2, D, 64])
    a2 = din("a2", [2, 64, D])
    gw1 = din("gate_w1", [D, 128])
    gw2 = din("gate_w2", [128, D])
    pool_w = din("pool_w", [4, 128, 256])
    w_out = din("w_out", [D, D])
    w_ff1 = din("w_ff1", [D, 4 * D])
    w_ff2 = din("w_ff2", [4 * D, D])
    g_final = din("g_final", [1, D])
    b_ada_in = din("b_ada", [6, D])
    c_identb = din("c_identb", [128, 128])
    c_identf = din("c_identf", [128, 128])
    c_tri = din("c_tri", [2, 128, 384])
    c_mask = din("c_mask", [5, 128, 512])
    c_bd = din("c_bd", [128, 128])
    c_pm = din("c_pm", [2, 4, 128, 2, 256])

    yc = dout("yc", [NT, D])
    sto = dout("sto", [2, 2, 16, 64, 64])

    HT = dscr("HT", [128, 8, NT], BF16)
    AT = [dscr("AT%d" % d, [128, 8, NT], BF16) for d in range(2)]
    RT = [dscr("RT%d" % d, [128, 8, NT], BF16) for d in range(2)]
    BT = [dscr("BT%d" % d, [128, 8, NT], BF16) for d in range(2)]
    KT = [dscr("KT%d" % d, [128, 8, NT], BF16) for d in range(2)]
    BH = [dscr("BH%d" % d, [NT, D], F32) for d in range(2)]
    KH = [dscr("KH%d" % d, [NT, D], F32) for d in range(2)]
    DTO = [dscr("DTO%d" % d, [128, 8, NT // 64], F32) for d in range(2)]
    VH = dscr("VH", [NT, D], BF16)
    VHF = dscr("VHF", [NT, D], F32)
    BON = dscr("BON", [128, 8, NT], F32)
    GT = dscr("GT", [128, 8, NT], F32)
    YS = [dscr("YS%d" % d, [128, 8, NT], F32) for d in range(2)]
    X1 = dscr("X1", [NT, D], F32)
    H2T = dscr("H2T", [128, 8, NT], BF16)

    op = kb.op
    dma = kb.dma

    with ExitStack() as gs:
        def sb(name, shape, dt, stack=gs):
            return stack.enter_context(nc.sbuf_tensor("s_" + name, list(shape), dt))

        def ps(name, shape, dt, stack=gs):
            return stack.enter_context(nc.psum_tensor("p_" + name, list(shape), dt))

        PF = [ps("pf%d" % i, [128, 512], F32) for i in range(8)]

        identb = sb("identb", [128, 128], BF16)
        identf = sb("identf", [128, 128], F32)
        bdones = sb("bdones", [128, 128], F32)
        cols = sb("cols", [128, 8, NR], F32)
        A1 = sb("A1", [128, 8, 2], F32)
        B1 = sb("B1", [128, 8, 2], F32)
        A2 = sb("A2", [128, 8, 2], F32)
        B2 = sb("B2", [128, 8, 2], F32)
        GA1 = [sb("GA1_%d" % i, [128, D], F32) for i in range(2)]
        GA2 = [sb("GA2_%d" % i, [128, D], F32) for i in range(2)]
        GF = sb("GF", [128, D], F32)
        muh = sb("muh", [128, 8, 3], F32)
        omm = sb("omm", [128, 8, 3], F32)
        omka = sb("omka", [128, 8], F32)
        eps12 = sb("eps12", [128, 1], F32)

        def col(name):
            return cols[:, :, RI[name]]

        def colc(name, c):
            return cols[:, c, RI[name]:RI[name] + 1]

        dma("pool", "initp", identb[:], c_identb[:, :], w=["identb"])
        dma("sp", "init", identf[:], c_identf[:, :], w=["identf"])
        dma("sp", "init2", bdones[:], c_bd[:, :], w=["bdones"])
        dma("sp", "init3", GF[:], g_final[0:1, :].partition_broadcast(128), w=["GF"])

        if run("M"):
            with ExitStack() as ph:
                rows = sb("rows", [NR, D], F32, ph)
                silu = sb("silu", [128, 8, 2], F32, ph)
                silub = sb("silub", [128, 8, 2], BF16, ph)
                silubc = [sb("silubc%d" % i, [128, 8, 128], F32, ph) for i in range(2)]
                onesb = sb("onesb", [128, 128], F32, ph)
                slab = [sb("mslab%d" % i, [128, 8, 512], F32, ph) for i in range(2)]
                modT = sb("modT", [128, 8, 6, 2], F32, ph)
                gab = [sb("gab%d" % i, [128, D], F32, ph) for i in range(2)]

                dma("sp", "rows", rows[:], rows_in[:, :], w=["rows"])
                dma("sp", "gab0", gab[0][:], b_ada_in[2:3, :].partition_broadcast(128), w=["gab0"])
                dma("sp", "gab1", gab[1][:], b_ada_in[5:6, :].partition_broadcast(128), w=["gab1"])
                for c in range(8):
                    op("pe", lambda e, c=c: e.transpose(PF[c % 2][:, 0:NR], rows[:, c * 128:(c + 1) * 128], identf[0:NR, 0:NR]),
                       r=["rows", "identf"], w=["pf%d" % (c % 2)])
                    op("dve", lambda e, c=c: e.tensor_copy(out=cols[:, c, :], in_=PF[c % 2][:, 0:NR]),
                       r=["pf%d" % (c % 2)], w=["cols"])
                op("dve", lambda e: e.memset(eps12[:], 1e-12), w=["eps12"])
                op("dve", lambda e: e.memset(onesb[:], 1.0), w=["onesb"])
                op("dve", lambda e: e.tensor_scalar(out=muh[:], in0=cols[:, :, RI["mu_r"]:RI["mu_r"] + 3], scalar1=0.5, scalar2=None, op0=ALU.mult),
                   r=["cols"], w=["muh"])
                op("dve", lambda e: e.tensor_scalar(out=omm[:], in0=cols[:, :, RI["mu_r"]:RI["mu_r"] + 3], scalar1=-1.0, scalar2=1.0, op0=ALU.mult, op1=ALU.add),
                   r=["cols"], w=["omm"])
                op("dve", lambda e: e.tensor_scalar(out=omka[:], in0=col("k_a"), scalar1=-1.0, scalar2=1.0, op0=ALU.mult, op1=ALU.add),
                   r=["cols"], w=["omka"])
                op("act", lambda e: e.activation(out=silu[:], in_=cols[:, :, RI["c0"]:RI["c0"] + 2], func=AF.Sigmoid),
                   r=["cols"], w=["silu"])
                op("dve", lambda e: e.tensor_tensor(out=silu[:], in0=silu[:], in1=cols[:, :, RI["c0"]:RI["c0"] + 2], op=ALU.mult),
                   r=["silu", "cols"], w=["silu"])
                op("dve", lambda e: e.tensor_copy(out=silub[:], in_=silu[:]), r=["silu"], w=["silub"])
                for b in range(2):
                    for c in range(8):
                        op("dve", lambda e, b=b, c=c: e.tensor_scalar(out=silubc[b][:, c, :], in0=onesb[:], scalar1=silu[:, c, b:b + 1], scalar2=None, op0=ALU.mult),
                           r=["silu", "onesb"], w=["silubc%d" % b])
                for sl in range(12):
                    sbuf = slab[sl % 2]
                    skey = "mslab%d" % (sl % 2)
                    dma("sp", skey, sbuf[:], w_ada[:, sl * 512:(sl + 1) * 512].rearrange("(c p) n -> p c n", p=128), w=[skey])
                    which = sl // 2
                    pf = PF[sl % 2]
                    for j in range(4):
                        for kc in range(8):
                            op("pe", lambda e, j=j, kc=kc, sbuf=sbuf, pf=pf: e.matmul(pf[:, j * 2:j * 2 + 2], sbuf[:, kc, j * 128:(j + 1) * 128], silu[:, kc, :], start=(kc == 0), stop=(kc == 7)),
                               r=[skey, "silu"], w=["pf%d" % (sl % 2)], inc=(j == 3 and kc == 7))
                    cbase = (sl % 2) * 4
                    op("dve", lambda e, pf=pf, which=which, cbase=cbase: e.tensor_copy(out=modT[:, cbase:cbase + 4, which, :], in_=pf[:, 0:8].rearrange("p (j b) -> p j b", b=2)),
                       r=["pf%d" % (sl % 2)], w=["modT"])
                    if which in (2, 5):
                        gi = 0 if which == 2 else 1
                        for b in range(2):
                            pg = PF[2 + b]
                            for kc in range(8):
                                op("pe", lambda e, kc=kc, b=b, sbuf=sbuf, pg=pg: e.matmul(pg[:, :], silubc[b][:, kc, :], sbuf[:, kc, :], start=(kc == 0), stop=(kc == 7)),
                                   r=[skey, "silubc%d" % b], w=["pf%d" % (2 + b)], inc=(kc == 7))
                            dst = (GA1 if gi == 0 else GA2)[b]
                            half = sl % 2
                            op("dve", lambda e, dst=dst, pg=pg, gi=gi, half=half: e.tensor_tensor(out=dst[:, half * 512:(half + 1) * 512], in0=pg[:, :], in1=gab[gi][:, half * 512:(half + 1) * 512], op=ALU.add),
                               r=["pf%d" % (2 + b), "gab%d" % gi], w=["GA%d_%d" % (gi + 1, b)])
                for b in range(2):
                    for wi, nm in enumerate(["sh1", "sc1", "ga1", "sh2", "sc2", "ga2"]):
                        op("dve", lambda e, b=b, wi=wi, nm=nm: e.tensor_tensor(out=modT[:, :, wi, b], in0=modT[:, :, wi, b], in1=col(nm), op=ALU.add),
                           r=["modT", "cols"], w=["modT"])
                    op("dve", lambda e, b=b: e.scalar_tensor_tensor(out=A1[:, :, b], in0=modT[:, :, 1, b], scalar=1.0, in1=col("g_norm1"), op0=ALU.add, op1=ALU.mult),
                       r=["modT", "cols"], w=["A1"])
                    op("dve", lambda e, b=b: e.scalar_tensor_tensor(out=A2[:, :, b], in0=modT[:, :, 4, b], scalar=1.0, in1=col("g_norm2"), op0=ALU.add, op1=ALU.mult),
                       r=["modT", "cols"], w=["A2"])
                    op("dve", lambda e, b=b: e.tensor_copy(out=B1[:, :, b], in_=modT[:, :, 0, b]), r=["modT"], w=["B1"])
                    op("dve", lambda e, b=b: e.tensor_copy(out=B2[:, :, b], in_=modT[:, :, 3, b]), r=["modT"], w=["B2"])
                kb.barrier()

        def norm_transpose(ph, xt, xkey, Acol, Bcol, cv, out_fm, okey, scratch):
            ss, rstd, xn = scratch
            op("act", lambda e: e.activation(out=xn[:], in_=xt, func=AF.Square, accum_out=ss[:]),
               r=[xkey], w=["nt_xn", "nt_ss"])
            op("dve", lambda e: e.tensor_scalar(out=rstd[:], in0=ss[:], scalar1=1.0 / D, scalar2=1e-6, op0=ALU.mult, op1=ALU.add),
               r=["nt_ss"], w=["nt_rstd"])
            op("act", lambda e: e.activation(out=rstd[:], in_=rstd[:], func=AF.Sqrt), r=["nt_rstd"], w=["nt_rstd"])
            op("dve", lambda e: e.reciprocal(out=rstd[:], in_=rstd[:]), r=["nt_rstd"], w=["nt_rstd"])
            op("act", lambda e: e.activation(out=xn[:], in_=xt, func=AF.Copy, scale=rstd[:]),
               r=[xkey, "nt_rstd"], w=["nt_xn"])
            for c in range(8):
                op("pe", lambda e, c=c: e.transpose(PF[4 + c // 4][:, (c % 4) * 128:(c % 4) * 128 + 128], xn[:, c * 128:(c + 1) * 128], identf[:]),
                   r=["nt_xn", "identf"], w=["pf%d" % (4 + c // 4)], inc=(c % 4 == 3))
            for c in range(8):
                op("dve", lambda e, c=c: e.tensor_scalar(out=out_fm(c), in0=PF[4 + c // 4][:, (c % 4) * 128:(c % 4) * 128 + 128], scalar1=Acol[:, c, cv:cv + 1], scalar2=Bcol[:, c, cv:cv + 1], op0=ALU.mult, op1=ALU.add),
                   r=["pf%d" % (4 + c // 4), "A1", "A2", "B1", "B2"], w=[okey])

        def fm_dram(t, tok0, n):
            return t[:, :, tok0:tok0 + n]

        if run("A"):
            with ExitStack() as ph:
                xt = [sb("a_x%d" % i, [128, D], F32, ph) for i in range(2)]
                hT = [sb("a_h%d" % i, [128, 8, 128], BF16, ph) for i in range(2)]
                ss = sb("a_ss", [128, 1], F32, ph)
                rstd = sb("a_rstd", [128, 1], F32, ph)
                xn = sb("a_xn", [128, D], F32, ph)
                for i in range(NTILE):
                    cv = 0 if i < 4 else 1
                    b = i % 2
                    dma("sp", "a_x%d" % b, xt[b][:], xc[i * 128:(i + 1) * 128, :], w=["a_x%d" % b])
                    norm_transpose(ph, xt[b][:], "a_x%d" % b, A1, B1, cv, lambda c, b=b: hT[b][:, c, :], "a_h%d" % b, (ss, rstd, xn))
                    dma("sp", "a_h%d" % b, fm_dram(HT, i * 128, 128), hT[b][:], r=["a_h%d" % b], w=["HT"])
                kb.barrier()

        if run("B"):
            with ExitStack() as ph:
                w_rkv = sb("w_rkv", [128, 8, 3072], BF16, ph)
                wd1s = sb("wd1s", [128, 2, 8, 64], BF16, ph)
                a1s = sb("a1s", [128, 2, 8, 64], BF16, ph)
                wd2s = sb("wd2s", [64, 2, D], BF16, ph)
                a2s = sb("a2s", [64, 2, D], BF16, ph)
                gw1s = sb("gw1s", [128, 8, 128], BF16, ph)
                gw2s = sb("gw2s", [128, D], BF16, ph)
                WD0 = [sb("WD0_%d" % d, [128, D], F32, ph) for d in range(2)]
                tri = [sb("tri%d" % d, [128, 384], F32, ph) for d in range(2)]
                for q in range(3):
                    dma("pool", "w_rkv%d" % q, w_rkv[:, :, q * 1024:(q + 1) * 1024],
                        w_in[:, 512 + q * 1024:512 + (q + 1) * 1024].rearrange("(c p) n -> p c n", p=128), w=["w_rkv%d" % q])
                for d in range(2):
                    dma("pool", "wsm", wd1s[:, d, :, :], wd1[d].rearrange("(c p) n -> p c n", p=128), w=["wsm"])
                    dma("pool", "wsm", a1s[:, d, :, :], a1[d].rearrange("(c p) n -> p c n", p=128), w=["wsm"])
                    dma("pool", "wsm", wd2s[:, d, :], wd2[d], w=["wsm"])
                    dma("pool", "wsm", a2s[:, d, :], a2[d], w=["wsm"])
                    dma("sp", "wsm2", WD0[d][:], wd0[d:d + 1, :].partition_broadcast(128), w=["wsm2"])
                    dma("sp", "wsm2", tri[d][:], c_tri[d], w=["wsm2"])
                dma("pool", "wsm", gw1s[:], gw1.rearrange("(c p) n -> p c n", p=128), w=["wsm"])
                dma("pool", "wsm", gw2s[:], gw2[:, :], w=["wsm"])
                WK = ["w_rkv0", "w_rkv1", "w_rkv2"]

                hx = sb("b_hx", [128, 8, 130], BF16, ph)
                hd = sb("b_hd", [128, 8, 128], F32, ph)
                xw = sb("b_xw", [128, 8, 128], BF16, ph)
                xa = sb("b_xa", [128, 8, 128], BF16, ph)
                xg = sb("b_xg", [128, 8, 128], BF16, ph)
                tw = sb("b_tw", [64, 2, 128], BF16, ph)
                aw = sb("b_aw", [64, 2, 128], BF16, ph)
                gw = sb("b_gw", [128, 128], BF16, ph)
                zs = [sb("b_zs%d" % i, [128, 130], F32, ph) for i in range(2)]
                t2 = [sb("b_t2%d" % i, [128, 128], F32, ph) for i in range(2)]
                rkv = sb("b_rkv", [128, 24, 128], F32, ph)
                kraw = sb("b_kraw", [128, 8, 128], F32, ph)
                sq = sb("b_sq", [128, 8, 128], F32, ph)
                kk = sb("b_kk", [128, 8, 128], F32, ph)
                bon = sb("b_bon", [128, 8, 128], F32, ph)
                gt = sb("b_gt", [128, 8, 128], F32, ph)
                vb = sb("b_vb", [128, 8, 128], BF16, ph)
                vht = sb("b_vht", [128, D], BF16, ph)
                sig = sb("b_sig", [128, D], F32, ph)
                afm = sb("b_a", [128, 8, 128], F32, ph)
                beta = sb("b_beta", [128, 8, 128], F32, ph)
                kd = sb("b_kd", [128, 8, 128], F32, ph)
                ee2 = [[sb("b_e%d_%d" % (j, i), [128, 128], F32, ph) for i in range(4)] for j in range(2)]
                o_at = sb("b_oat", [128, 8, 128], BF16, ph)
                o_rt = sb("b_ort", [128, 8, 128], BF16, ph)
                o_bt = sb("b_obt", [128, 8, 128], BF16, ph)
                o_kt = sb("b_okt", [128, 8, 128], BF16, ph)
                o_bh = sb("b_obh", [128, 8, 128], F32, ph)
                o_kh = sb("b_okh", [128, 8, 128], F32, ph)
                bht = sb("b_bht", [128, D], F32, ph)
                kht = sb("b_kht", [128, D], F32, ph)
                vhf = sb("b_vhf", [128, D], F32, ph)
                dtt = sb("b_dtt", [128, 8, 2], F32, ph)

                BSEC = int(os.environ.get("B_SEC", "99"))
                for i in range(int(os.environ.get("B_MAXT", NTILE))):
                    t0 = i * 128
                    s0, sl = SEQS[seq_of_tile(i)]
                    has_prev = t0 > s0
                    has_next = t0 + 128 < s0 + sl
                    lo = t0 - 1 if has_prev else t0
                    hi = t0 + 129 if has_next else t0 + 128
                    if not has_prev:
                        op("dve", lambda e: e.memset(hx[:, :, 0:1], 0.0), w=["hx"])
                    if not has_next:
                        op("dve", lambda e: e.memset(hx[:, :, 129:130], 0.0), w=["hx"])
                    dma("sp", "b_hx", hx[:, :, (lo - (t0 - 1)):(hi - (t0 - 1))], HT[:, :, lo:hi], r=["HT"], w=["hx"])
                    if BSEC < 1:
                        continue
                    op("dve", lambda e: e.tensor_tensor(out=hd[:], in0=hx[:, :, 0:128], in1=hx[:, :, 2:130], op=ALU.add), r=["hx"], w=["hd"])
                    op("dve", lambda e: e.scalar_tensor_tensor(out=hd[:], in0=hd[:], scalar=0.5, in1=hx[:, :, 1:129], op0=ALU.mult, op1=ALU.subtract),
                       r=["hd", "hx"], w=["hd"])
                    for c in range(8):
                        for nm, dst, key in (("mu_w", xw, "xw"), ("mu_a", xa, "xa"), ("mu_g", xg, "xg")):
                            op("dve", lambda e, c=c, nm=nm, dst=dst: e.scalar_tensor_tensor(out=dst[:, c, :], in0=hd[:, c, :], scalar=colc(nm, c), in1=hx[:, c, 1:129], op0=ALU.mult, op1=ALU.add),
                               r=["hd", "hx"], w=[key])
                    S1 = int(os.environ.get("S1", "9"))
                    if S1 < 1:
                        continue
                    for d in range(2):
                        for kc in range(8):
                            op("pe", lambda e, d=d, kc=kc: e.matmul(PF[0][0:64, d * 128:(d + 1) * 128], wd1s[:, d, kc, :], xw[:, kc, :], start=(kc == 0), stop=(kc == 7)),
                               r=["wsm", "xw"], w=["pf0"], inc=(kc == 7))
                        for kc in range(8):
                            op("pe", lambda e, d=d, kc=kc: e.matmul(PF[0][0:64, 256 + d * 128:256 + (d + 1) * 128], a1s[:, d, kc, :], xa[:, kc, :], start=(kc == 0), stop=(kc == 7)),
                               r=["wsm", "xa"], w=["pf0"], inc=(kc == 7))
                    if S1 < 2:
                        continue
                    op("act", lambda e: e.activation(out=tw[:].rearrange("p d t -> p (d t)"), in_=PF[0][0:64, 0:256], func=AF.Tanh), r=["pf0"], w=["tw"])
                    if True:
                        op("act", lambda e: e.activation(out=aw[:].rearrange("p d t -> p (d t)"), in_=PF[0][0:64, 256:512], func=AF.Copy), r=["pf0"], w=["aw"])
                    else:
                        op("dve", lambda e: e.tensor_copy(out=aw[:].rearrange("p d t -> p (d t)"), in_=PF[0][0:64, 256:512]), r=["pf0"], w=["aw"])
                    if S1 < 3:
                        continue
                    for kc in range(8):
                        op("pe", lambda e, kc=kc: e.matmul(PF[1][:, 0:128], gw1s[:, kc, :], xg[:, kc, :], start=(kc == 0), stop=(kc == 7)),
                           r=["wsm", "xg"], w=["pf1"], inc=(kc == 7))
                    op("act", lambda e: e.activation(out=gw[:], in_=PF[1][:, 0:128], func=AF.Sigmoid), r=["pf1"], w=["gw"])
                    if S1 < 4:
                        continue
                    for half in range(2):
                        for j in range(4):
                            fc = half * 4 + j
                            op("pe", lambda e, fc=fc, j=j: e.matmul(PF[1][:, j * 128:(j + 1) * 128], gw2s[:, fc * 128:(fc + 1) * 128], gw[:], start=True, stop=True),
                               r=["wsm", "gw"], w=["pf1"], inc=(j == 3))
                        op("act", lambda e, half=half: e.activation(out=gt[:, half * 4:half * 4 + 4, :], in_=PF[1][:, :].rearrange("p (j t) -> p j t", t=128), func=AF.Copy),
                           r=["pf1"], w=["gt"])
                    if S1 < 5:
                        continue
                    dma("sp", "b_gt", fm_dram(GT, t0, 128), gt[:], r=["gt"], w=["GT"])
                    if BSEC < 2:
                        continue
                    for fc in range(24):
                        pf = PF[2 + fc % 2]
                        pk = "pf%d" % (2 + fc % 2)
                        z = zs[fc % 2]
                        zk = "zs%d" % (fc % 2)
                        tt = t2[fc % 2]
                        tk = "t2%d" % (fc % 2)
                        for kc in range(8):
                            op("pe", lambda e, fc=fc, kc=kc, pf=pf: e.matmul(pf[:, 0:130], w_rkv[:, kc, fc * 128:(fc + 1) * 128], hx[:, kc, :], start=(kc == 0), stop=(kc == 7)),
                               r=[WK[fc // 8], "hx"], w=[pk], inc=(kc == 7))
                        op("act", lambda e, pf=pf, z=z: e.activation(out=z[:], in_=pf[:, 0:130], func=AF.Copy), r=[pk], w=[zk])
                        op("dve", lambda e, z=z, tt=tt: e.tensor_tensor(out=tt[:], in0=z[:, 0:128], in1=z[:, 2:130], op=ALU.add), r=[zk], w=[tk])
                        op("dve", lambda e, tt=tt, fc=fc: e.tensor_scalar(out=tt[:], in0=tt[:], scalar1=muh[:, fc % 8, fc // 8:fc // 8 + 1], scalar2=None, op0=ALU.mult),
                           r=[tk, "muh"], w=[tk])
                        op("dve", lambda e, z=z, tt=tt, fc=fc: e.scalar_tensor_tensor(out=rkv[:, fc, :], in0=z[:, 1:129], scalar=omm[:, fc % 8, fc // 8:fc // 8 + 1], in1=tt[:], op0=ALU.mult, op1=ALU.add),
                           r=[zk, tk, "omm"], w=["rkv"])
                    R = lambda c: rkv[:, c, :]
                    Kf = lambda c: rkv[:, 8 + c, :]
                    Vf = lambda c: rkv[:, 16 + c, :]
                    if BSEC < 3:
                        continue
                    for c in range(8):
                        op("dve", lambda e, c=c: e.tensor_scalar(out=kraw[:, c, :], in0=Kf(c), scalar1=colc("k_k", c), scalar2=None, op0=ALU.mult), r=["rkv"], w=["kraw"])
                    op("act", lambda e: e.activation(out=sq[:], in_=kraw[:], func=AF.Square), r=["kraw"], w=["sq"])
                    for half in range(2):
                        for j in range(4):
                            c = half * 4 + j
                            op("pe", lambda e, c=c, j=j: e.matmul(PF[4][:, j * 128:(j + 1) * 128], bdones[:], sq[:, c, :], start=True, stop=True),
                               r=["bdones", "sq"], w=["pf4"], inc=(j == 3))
                        op("act", lambda e, half=half: e.activation(out=kk[:, half * 4:half * 4 + 4, :], in_=PF[4][:, :].rearrange("p (j t) -> p j t", t=128), func=AF.Ln, bias=eps12[:]),
                           r=["pf4", "eps12"], w=["kk"])
                    op("act", lambda e: e.activation(out=kk[:], in_=kk[:], func=AF.Exp, scale=-0.5), r=["kk"], w=["kk"])
                    op("dve", lambda e: e.tensor_tensor(out=kk[:], in0=kk[:], in1=kraw[:], op=ALU.mult), r=["kk", "kraw"], w=["kk"])
                    for c in range(8):
                        op("dve", lambda e, c=c: e.scalar_tensor_tensor(out=sq[:, c, :], in0=R(c), scalar=colc("r_k", c), in1=Kf(c), op0=ALU.mult, op1=ALU.mult),
                           r=["rkv"], w=["sq"])
                    for half in range(2):
                        for j in range(4):
                            c = half * 4 + j
                            op("pe", lambda e, c=c, j=j: e.matmul(PF[4][:, j * 128:(j + 1) * 128], bdones[:], sq[:, c, :], start=True, stop=True),
                               r=["bdones", "sq"], w=["pf4"], inc=(j == 3))
                        op("dve", lambda e, half=half: e.tensor_tensor(out=bon[:, half * 4:half * 4 + 4, :], in0=PF[4][:, :].rearrange("p (j t) -> p j t", t=128), in1=rkv[:, 16 + half * 4:16 + half * 4 + 4, :], op=ALU.mult),
                           r=["pf4", "rkv"], w=["bon"])
                    dma("sp", "b_bon", fm_dram(BON, t0, 128), bon[:], r=["bon"], w=["BON"])
                    if BSEC < 4:
                        continue
                    for c in range(8):
                        op("pe", lambda e, c=c: e.transpose(PF[c // 4][:, (c % 4) * 128:(c % 4) * 128 + 128], rkv[:, 16 + c, :], identf[:]),
                           r=["rkv", "identf"], w=["pf%d" % (c // 4)], inc=(c % 4 == 3))
                    op("dve", lambda e: e.tensor_copy(out=vhf[:, 0:512], in_=PF[0][:, :]), r=["pf0"], w=["vhf"])
                    op("act", lambda e: e.activation(out=vhf[:, 512:1024], in_=PF[1][:, :], func=AF.Copy), r=["pf1"], w=["vhf"])
                    dma("sp", "b_vhf", VHF[t0:t0 + 128, :], vhf[:], r=["vhf"], w=["VHF"])
                    op("act", lambda e: e.activation(out=vht[:], in_=vhf[:], func=AF.Copy), r=["vhf"], w=["vht"])
                    dma("sp", "b_vht", VH[t0:t0 + 128, :], vht[:], r=["vht"], w=["VH"])
                    if BSEC < 5:
                        continue
                    for d in range(2):
                        for half in range(2):
                            op("pe", lambda e, d=d, half=half: e.matmul(PF[0][:, :], tw[:, d, :], wd2s[:, d, half * 512:(half + 1) * 512], start=True, stop=True),
                               r=["tw", "wsm"], w=["pf0"])
                            op("dve", lambda e, d=d, half=half: e.tensor_tensor(out=sig[:, half * 512:(half + 1) * 512], in0=PF[0][:, :], in1=WD0[d][:, half * 512:(half + 1) * 512], op=ALU.add),
                               r=["pf0", "wsm2"], w=["sig"])
                        op("act", lambda e: e.activation(out=sig[:], in_=sig[:], func=AF.Sigmoid), r=["sig"], w=["sig"])
                        a0n = "a0f" if d == 0 else "a0b"
                        for c in range(8):
                            pa = PF[4 + c % 4]
                            pak = "pf%d" % (4 + c % 4)
                            op("pe", lambda e, c=c, d=d, pa=pa: e.matmul(pa[:, 0:128], a2s[:, d, c * 128:(c + 1) * 128], aw[:, d, :], start=True, stop=True),
                               r=["wsm", "aw"], w=[pak])
                            op("act", lambda e, c=c, pa=pa, a0n=a0n: e.activation(out=afm[:, c, :], in_=pa[:, 0:128], func=AF.Sigmoid, bias=colc(a0n, c)),
                               r=[pak, "cols"], w=["afm%d" % c])
                            op("dve", lambda e, c=c: e.tensor_tensor(out=beta[:, c, :], in0=kk[:, c, :], in1=afm[:, c, :], op=ALU.mult), r=["kk", "afm%d" % c], w=["beta%d" % c])
                            op("dve", lambda e, c=c: e.tensor_scalar(out=kd[:, c, :], in0=afm[:, c, :], scalar1=colc("k_a", c), scalar2=omka[:, c:c + 1], op0=ALU.mult, op1=ALU.add),
                               r=["afm%d" % c, "cols", "omka"], w=["kd%d" % c])
                            op("dve", lambda e, c=c: e.tensor_tensor(out=kd[:, c, :], in0=kd[:, c, :], in1=Kf(c), op=ALU.mult), r=["kd%d" % c, "rkv"], w=["kd%d" % c])
                        for c in range(8):
                            pc = PF[2 + c % 2]
                            pck = "pf%d" % (2 + c % 2)
                            ee = ee2[c % 2]
                            ek = ["e%d_%d" % (c % 2, i) for i in range(4)]
                            op("pe", lambda e, c=c, d=d, pc=pc: e.matmul(pc[:, 0:384], sig[:, c * 128:(c + 1) * 128], tri[d][:], start=True, stop=True),
                               r=["sig", "wsm2"], w=[pck])
                            op("act", lambda e, pc=pc: e.activation(out=ee[0][:], in_=pc[:, 128:256], func=AF.Exp), r=[pck], w=[ek[0]])
                            op("act", lambda e, pc=pc: e.activation(out=ee[1][:], in_=pc[:, 0:128], func=AF.Exp), r=[pck], w=[ek[1]])
                            op("act", lambda e, pc=pc: e.activation(out=ee[2][:], in_=pc[:, 0:128], func=AF.Exp, scale=-1.0), r=[pck], w=[ek[2]])
                            op("act", lambda e, pc=pc: e.activation(out=ee[3][:], in_=pc[:, 256:384], func=AF.Exp), r=[pck], w=[ek[3]])
                            if d == 0:
                                src = pc[:, 63:128:64]
                            else:
                                src = pc[:, 0:128:64]
                            op("act", lambda e, c=c, src=src: e.activation(out=dtt[:, c, :], in_=src, func=AF.Exp), r=[pck], w=["dtt"])
                            op("dve", lambda e, c=c: e.scalar_tensor_tensor(out=o_at[:, c, :], in0=kk[:, c, :], scalar=-1.0, in1=ee[0][:], op0=ALU.mult, op1=ALU.mult),
                               r=["kk", ek[0]], w=["o_at"])
                            op("dve", lambda e, c=c: e.tensor_tensor(out=o_rt[:, c, :], in0=R(c), in1=ee[1][:], op=ALU.mult), r=["rkv", ek[1]], w=["o_rt"])
                            op("dve", lambda e, c=c: e.tensor_tensor(out=o_bt[:, c, :], in0=beta[:, c, :], in1=ee[2][:], op=ALU.mult), r=["beta%d" % c, ek[2]], w=["o_bt"])
                            op("dve", lambda e, c=c: e.tensor_tensor(out=o_kt[:, c, :], in0=kd[:, c, :], in1=ee[2][:], op=ALU.mult), r=["kd%d" % c, ek[2]], w=["o_kt"])
                            op("dve", lambda e, c=c: e.tensor_tensor(out=o_bh[:, c, :], in0=beta[:, c, :], in1=ee[3][:], op=ALU.mult), r=["beta%d" % c, ek[3]], w=["o_bh"])
                            op("dve", lambda e, c=c: e.tensor_tensor(out=o_kh[:, c, :], in0=kd[:, c, :], in1=ee[3][:], op=ALU.mult), r=["kd%d" % c, ek[3]], w=["o_kh"])
                        for src_t, dst_t, sk, dk, pb0 in ((o_bh, bht, "o_bh", "bht", 0), (o_kh, kht, "o_kh", "kht", 2)):
                            for c in range(8):
                                op("pe", lambda e, c=c, src_t=src_t, pb0=pb0: e.transpose(PF[pb0 + c // 4][:, (c % 4) * 128:(c % 4) * 128 + 128], src_t[:, c, :], identf[:]),
                                   r=[sk, "identf"], w=["pf%d" % (pb0 + c // 4)], inc=(c % 4 == 3))
                            op("dve", lambda e, dst_t=dst_t, pb0=pb0: e.tensor_copy(out=dst_t[:, 0:512], in_=PF[pb0][:, :]), r=["pf%d" % pb0], w=[dk])
                            op("act", lambda e, dst_t=dst_t, pb0=pb0: e.activation(out=dst_t[:, 512:1024], in_=PF[pb0 + 1][:, :], func=AF.Copy), r=["pf%d" % (pb0 + 1)], w=[dk])
                        dma("sp", "b_o0", fm_dram(AT[d], t0, 128), o_at[:], r=["o_at"], w=["AT"])
                        dma("sp", "b_o1", fm_dram(RT[d], t0, 128), o_rt[:], r=["o_rt"], w=["RT"])
                        dma("sp", "b_o2", fm_dram(BT[d], t0, 128), o_bt[:], r=["o_bt"], w=["BT"])
                        dma("sp", "b_o3", fm_dram(KT[d], t0, 128), o_kt[:], r=["o_kt"], w=["KT"])
                        dma("sp", "b_o4", BH[d][t0:t0 + 128, :], bht[:], r=["bht"], w=["BH"])
                        dma("sp", "b_o5", KH[d][t0:t0 + 128, :], kht[:], r=["kht"], w=["KH"])
                        dma("sp", "b_o6", DTO[d][:, :, 2 * i:2 * i + 2], dtt[:], r=["dtt"], w=["DTO"])
                kb.barrier()

        if run("C"):
            with ExitStack() as ph:
                masks = sb("c_masks", [128, 5, 512], BF16, ph)
                dma("pool", "c_mask", masks[:], c_mask.rearrange("m p n -> p m n"), w=["masks"])
                ident64 = identf[0:64, 0:64]
                SU, SL, IU, IL, IDS = range(5)
                satz = [sb("c_atz%d" % i, [128, 2, 8, 128], BF16, ph) for i in range(2)]
                srtz = [sb("c_rtz%d" % i, [128, 2, 8, 128], BF16, ph) for i in range(2)]
                sbt = [sb("c_bt%d" % i, [128, 8, 128], BF16, ph) for i in range(2)]
                skt = [sb("c_kt%d" % i, [128, 8, 128], BF16, ph) for i in range(2)]
                sbhz = [sb("c_bhz%d" % i, [128, 2, D], F32, ph) for i in range(2)]
                skhz = [sb("c_khz%d" % i, [128, 2, D], F32, ph) for i in range(2)]
                svfz = [sb("c_vfz%d" % i, [128, 2, D], F32, ph) for i in range(2)]
                svhz = [sb("c_vhz%d" % i, [128, 2, D], BF16, ph) for i in range(2)]
                sdt = [sb("c_dt%d" % i, [128, 8, 2], F32, ph) for i in range(2)]
                MKB = [sb("c_mkb%d" % i, [128, 16, 64], BF16, ph) for i in range(2)]
                MBR = [sb("c_mbr%d" % i, [128, 16, 64], BF16, ph) for i in range(2)]
                MKR = [sb("c_mkr%d" % i, [128, 16, 64], BF16, ph) for i in range(2)]
                TT = [sb("c_tt%d" % i, [128, 16, 64], BF16, ph) for i in range(2)]
                RZ = [sb("c_rz%d" % i, [128, 16, 64], BF16, ph) for i in range(2)]
                Pm = [sb("c_pm%d" % i, [128, 8, 64], BF16, ph) for i in range(3)]
                Nm = [sb("c_nm%d" % i, [128, 8, 64], BF16, ph) for i in range(3)]
                rtmp = sb("c_rtmp", [128, 512], F32, ph)
                Tm = [sb("c_tm%d" % i, [128, 8, 64], BF16, ph) for i in range(2)]
                Wz = sb("c_wz", [128, 2, D], BF16, ph)
                Uz = sb("c_uz", [128, 2, D], BF16, ph)
                Ufz = sb("c_ufz", [128, 2, D], F32, ph)
                ST = sb("c_st", [128, 8, 64], F32, ph)
                Sbz = sb("c_sbz", [128, 8, 2, 64], BF16, ph)
                S0 = sb("c_s0", [64, 16, 64], F32, ph)
                SO = sb("c_so", [64, 8, 128], F32, ph)
                Yt = [sb("c_y%d" % i, [128, 8, 128], F32, ph) for i in range(2)]
                for i in range(2):
                    for tz in (satz[i], srtz[i], sbhz[i], skhz[i], svhz[i], svfz[i]):
                        op("dve", lambda e, tz=tz: e.memset(tz[:], 0.0), w=["ld%d" % i])
                op("dve", lambda e: e.memset(Wz[:], 0.0), w=["Wz"])
                op("dve", lambda e: e.memset(Uz[:], 0.0), w=["Uz"])
                op("dve", lambda e: e.memset(Ufz[:], 0.0), w=["Ufz"])
                op("dve", lambda e: e.memset(Sbz[:], 0.0), w=["Sbz"])
                H0 = slice(0, 64)
                H1 = slice(64, 128)
                HS = (H0, H1)

                def copy_state_bf16():
                    op("act", lambda e: e.activation(out=Sbz[H0, :, 0, :], in_=ST[H0, :, :], func=AF.Copy), r=["ST"], w=["Sbz"])
                    op("act", lambda e: e.activation(out=Sbz[H1, :, 1, :], in_=ST[H1, :, :], func=AF.Copy), r=["ST"], w=["Sbz"])

                def prep(si, d, ti, b):
                    s0 = SEQS[si][0]
                    t0 = s0 + ti * 128
                    L = "ld%d" % b
                    mM, mN, mI = (SU, SL, IU) if d == 0 else (SL, SU, IL)
                    loads = []
                    for par in range(2):
                        loads.append((satz[b][HS[par], par, :, :], AT[d][HS[par], :, t0:t0 + 128]))
                        loads.append((srtz[b][HS[par], par, :, :], RT[d][HS[par], :, t0:t0 + 128]))
                        loads.append((sbhz[b][HS[par], par, :], BH[d][t0 + 64 * par:t0 + 64 * par + 64, :]))
                        loads.append((skhz[b][HS[par], par, :], KH[d][t0 + 64 * par:t0 + 64 * par + 64, :]))
                        loads.append((svhz[b][HS[par], par, :], VH[t0 + 64 * par:t0 + 64 * par + 64, :]))
                        loads.append((svfz[b][HS[par], par, :], VHF[t0 + 64 * par:t0 + 64 * par + 64, :]))
                    loads.append((sbt[b][:], fm_dram(BT[d], t0, 128)))
                    loads.append((skt[b][:], fm_dram(KT[d], t0, 128)))
                    loads.append((sdt[b][:], DTO[d][:, :, t0 // 64:t0 // 64 + 2]))
                    for j, (dst, src) in enumerate(loads):
                        dma("sp", "c_ld%d_%d" % (b, j), dst, src, w=[L])
                    yield
                    atz, rtz, bt_, kt_ = satz[b], srtz[b], sbt[b], skt[b]
                    for g in range(2):
                        def L_plain(t_):
                            return lambda h, cs: t_[:, h // 2, cs]

                        def L_z(tz_):
                            return lambda h, cs: tz_[:, h % 2, h // 2, cs]

                        gsl = slice(g * 8, g * 8 + 8)

                        def mtype(bank, lf, rf):
                            for hh in range(8):
                                h = g * 8 + hh
                                for cp in range(2):
                                    cs = slice(cp * 64, cp * 64 + 64)
                                    op("pe", lambda e, lf=lf, rf=rf, h=h, hh=hh, cs=cs, bank=bank: e.matmul(
                                        PF[bank][cs, hh * 64:(hh + 1) * 64], lf(h, cs), rf(h, cs), start=True, stop=True, skip_group_check=True),
                                       r=[L], w=["pf%d" % bank], inc=(hh == 7 and cp == 1))

                        mtype(0, L_plain(bt_), L_z(atz))
                        mtype(1, L_z(atz), L_plain(bt_))
                        mtype(2, L_plain(kt_), L_z(atz))
                        mtype(3, L_plain(bt_), L_z(rtz))
                        op("dve", lambda e: e.tensor_tensor(out=Pm[2][:].rearrange("p h t -> p (h t)"), in0=PF[0][:, :], in1=masks[:, mM, :], op=ALU.mult),
                           r=["pf0", "masks"], w=["Pm2"])
                        op("dve", lambda e: e.tensor_tensor(out=Nm[2][:].rearrange("p h t -> p (h t)"), in0=PF[1][:, :], in1=masks[:, mN, :], op=ALU.mult),
                           r=["pf1", "masks"], w=["Nm2"])
                        op("dve", lambda e: e.tensor_tensor(out=MKB[b][:, gsl, :].rearrange("p h t -> p (h t)"), in0=PF[2][:, :], in1=masks[:, mM, :], op=ALU.mult),
                           r=["pf2", "masks"], w=["MKB%d" % b])
                        op("dve", lambda e: e.tensor_tensor(out=MBR[b][:, gsl, :].rearrange("p h t -> p (h t)"), in0=PF[3][:, :], in1=masks[:, mI, :], op=ALU.mult),
                           r=["pf3", "masks"], w=["MBR%d" % b])
                        yield
                        mtype(3, L_plain(kt_), L_z(rtz))
                        op("dve", lambda e: e.tensor_tensor(out=MKR[b][:, gsl, :].rearrange("p h t -> p (h t)"), in0=PF[3][:, :], in1=masks[:, mI, :], op=ALU.mult),
                           r=["pf3", "masks"], w=["MKR%d" % b])
                        cur = 2
                        for lev in range(6):
                            nx = 0 if cur == 2 else 1 - cur
                            last = lev == 5
                            first = lev == 0

                            def blk(kind, cp, bank, cur=cur):
                                cs = slice(cp * 64, cp * 64 + 64)
                                for hh in range(8):
                                    if kind == "P":
                                        lt_, rh_ = Nm[cur], Pm[cur]
                                    elif kind == "N":
                                        lt_, rh_ = Pm[cur], Nm[cur]
                                    else:
                                        lt_, rh_ = Nm[cur], Tm[cur]
                                    rk = ["Nm%d" % cur, "Pm%d" % cur] + (["Tm%d" % cur] if kind == "T" else [])
                                    op("pe", lambda e, hh=hh, cs=cs, lt_=lt_, rh_=rh_, bank=bank: e.matmul(PF[bank][cs, hh * 64:(hh + 1) * 64], lt_[cs, hh, :], rh_[cs, hh, :], start=True, stop=True, skip_group_check=True),
                                       r=rk, w=["pf%d" % bank], inc=(hh == 7))

                            if first:
                                op("dve", lambda e, nx=nx, cur=cur: e.tensor_tensor(out=Tm[nx][:].rearrange("p h t -> p (h t)"), in0=Pm[cur][:].rearrange("p h t -> p (h t)"), in1=masks[:, IDS, :], op=ALU.add),
                                   r=["Pm%d" % cur, "masks"], w=["Tm%d" % nx])
                                blk("P", 0, 0); blk("N", 1, 1); blk("P", 1, 0); blk("N", 0, 1)
                            elif last:
                                blk("T", 0, 2); blk("T", 1, 3)
                            else:
                                blk("P", 0, 0); blk("N", 1, 1); blk("T", 0, 2); blk("P", 1, 0); blk("N", 0, 1); blk("T", 1, 2)
                            if not first:
                                if last:
                                    op("dve", lambda e, nx=nx, cur=cur: e.tensor_tensor(out=Tm[nx][H0].rearrange("p h t -> p (h t)"), in0=PF[2][H0, :], in1=Tm[cur][H0].rearrange("p h t -> p (h t)"), op=ALU.add),
                                       r=["pf2", "Tm%d" % cur], w=["Tm%d" % nx])
                                    op("dve", lambda e, nx=nx, cur=cur: e.tensor_tensor(out=Tm[nx][H1].rearrange("p h t -> p (h t)"), in0=PF[3][H1, :], in1=Tm[cur][H1].rearrange("p h t -> p (h t)"), op=ALU.add),
                                       r=["pf3", "Tm%d" % cur], w=["Tm%d" % nx])
                                else:
                                    op("dve", lambda e, nx=nx, cur=cur: e.tensor_tensor(out=Tm[nx][:].rearrange("p h t -> p (h t)"), in0=PF[2][:, :], in1=Tm[cur][:].rearrange("p h t -> p (h t)"), op=ALU.add),
                                       r=["pf2", "Tm%d" % cur], w=["Tm%d" % nx])
                            if not last:
                                op("act", lambda e, nx=nx: e.activation(out=Pm[nx][:].rearrange("p h t -> p (h t)"), in_=PF[0][:, :], func=AF.Copy), r=["pf0"], w=["Pm%d" % nx])
                                op("act", lambda e, nx=nx: e.activation(out=Nm[nx][:].rearrange("p h t -> p (h t)"), in_=PF[1][:, :], func=AF.Copy), r=["pf1"], w=["Nm%d" % nx])
                            cur = nx
                            yield
                        op("dve", lambda e, cur=cur: e.tensor_copy(out=TT[b][:, gsl, :], in_=Tm[cur][:]), r=["Tm%d" % cur], w=["TT%d" % b])
                        for cp in range(2):
                            cs = slice(cp * 64, cp * 64 + 64)
                            for hh in range(8):
                                op("pe", lambda e, hh=hh, cs=cs, cp=cp, cur=cur: e.matmul(PF[cp][cs, hh * 64:(hh + 1) * 64], Nm[2][cs, hh, :], Tm[cur][cs, hh, :], start=True, stop=True, skip_group_check=True),
                                   r=["Nm2", "Tm%d" % cur], w=["pf%d" % cp], inc=(hh == 7))
                        for cp in range(2):
                            cs = slice(cp * 64, cp * 64 + 64)
                            op("dve", lambda e, cs=cs, cp=cp, cur=cur: e.scalar_tensor_tensor(out=rtmp[cs, :], in0=Tm[cur][cs].rearrange("p h t -> p (h t)"), scalar=-1.0, in1=PF[cp][cs, :], op0=ALU.mult, op1=ALU.add),
                               r=["pf%d" % cp, "Tm%d" % cur], w=["rtmp"])
                            op("dve", lambda e, cs=cs: e.tensor_tensor(out=RZ[b][cs, gsl, :].rearrange("p h t -> p (h t)"), in0=rtmp[cs, :], in1=masks[cs, IDS, :], op=ALU.add),
                               r=["rtmp", "masks"], w=["RZ%d" % b])
                        yield

                def chain(si, d, ti, b, first_tile, last_tile):
                    s0 = SEQS[si][0]
                    t0 = s0 + ti * 128
                    L = "ld%d" % b
                    atz, rtz, bhz, khz, vhz, vfz, dt_ = satz[b], srtz[b], sbhz[b], skhz[b], svhz[b], svfz[b], sdt[b]
                    mkb, mbr, mkr, tt, rz = MKB[b], MBR[b], MKR[b], TT[b], RZ[b]
                    KM = ["MKB%d" % b, "MBR%d" % b, "MKR%d" % b, "TT%d" % b, "RZ%d" % b]
                    if first_tile:
                        if si < 2:
                            op("dve", lambda e: e.memset(ST[:], 0.0), w=["ST"])
                        else:
                            dma("sp", "c_s0", S0[:], st0[d].rearrange("h v k -> v h k"), w=["S0"])
                            for hp in range(8):
                                op("pe", lambda e, hp=hp: e.transpose(PF[4][:, hp * 64:(hp + 1) * 64], S0[:, 2 * hp:2 * hp + 2, :].rearrange("v h k -> v (h k)"), ident64),
                                   r=["S0", "identf"], w=["pf4"], inc=(hp == 7))
                            op("dve", lambda e: e.tensor_copy(out=ST[:], in_=PF[4][:, :].rearrange("p (h v) -> p h v", v=64)), r=["pf4"], w=["ST"])
                        copy_state_bf16()
                        yield
                    yb = Yt[b]
                    yk = "Y%d" % b
                    for cp in ((0, 1) if d == 0 else (1, 0)):
                        cs = slice(cp * 64, cp * 64 + 64)
                        for h in range(16):
                            pw = PF[4 + h // 8]
                            o = pw[cs, (h % 8) * 64:(h % 8) * 64 + 64]
                            op("pe", lambda e, o=o, h=h, cs=cs: e.matmul(o, atz[:, h % 2, h // 2, cs], Sbz[:, h // 2, h % 2, :], start=True, stop=False, skip_group_check=True),
                               r=[L, "Sbz"], w=["pf%d" % (4 + h // 8)], inc=False)
                            op("pe", lambda e, o=o, h=h, cp=cp: e.matmul(o, mkb[:, h, :], vhz[:, cp, h * 64:(h + 1) * 64], start=False, stop=True, skip_group_check=True),
                               r=[L, KM[0]], w=["pf%d" % (4 + h // 8)], inc=(h % 8 == 7))
                        op("act", lambda e, cs=cs, cp=cp: e.activation(out=Wz[cs, cp, 0:512], in_=PF[4][cs, :], func=AF.Copy), r=["pf4"], w=["Wz"])
                        op("dve", lambda e, cs=cs, cp=cp: e.tensor_copy(out=Wz[cs, cp, 512:1024], in_=PF[5][cs, :]), r=["pf5"], w=["Wz"])
                        yield
                        for h in range(16):
                            pu = PF[6 + h // 8]
                            op("pe", lambda e, pu=pu, h=h, cs=cs, cp=cp: e.matmul(pu[cs, (h % 8) * 64:(h % 8) * 64 + 64], tt[:, h, :], Wz[:, cp, h * 64:(h + 1) * 64], start=True, stop=True, skip_group_check=True),
                               r=[KM[3], "Wz"], w=["pf%d" % (6 + h // 8)], inc=(h % 8 == 7))
                        op("act", lambda e, cs=cs, cp=cp: e.activation(out=Uz[cs, cp, 0:512], in_=PF[6][cs, :], func=AF.Copy), r=["pf6"], w=["Uz"])
                        op("dve", lambda e, cs=cs, cp=cp: e.tensor_copy(out=Uz[cs, cp, 512:1024], in_=PF[7][cs, :]), r=["pf7"], w=["Uz"])
                        op("act", lambda e, cs=cs, cp=cp: e.activation(out=Ufz[cs, cp, 0:512], in_=PF[6][cs, :], func=AF.Copy), r=["pf6"], w=["Ufz"])
                        op("dve", lambda e, cs=cs, cp=cp: e.tensor_copy(out=Ufz[cs, cp, 512:1024], in_=PF[7][cs, :]), r=["pf7"], w=["Ufz"])
                        yield
                        for h in range(16):
                            pu = PF[6 + h // 8]
                            op("pe", lambda e, pu=pu, h=h, cs=cs, cp=cp: e.matmul(pu[cs, (h % 8) * 64:(h % 8) * 64 + 64], rz[:, h, :], Uz[:, cp, h * 64:(h + 1) * 64], start=True, stop=True, skip_group_check=True),
                               r=[KM[4], "Uz"], w=["pf%d" % (6 + h // 8)], inc=(h % 8 == 7))
                        for half in range(2):
                            hsl = slice(half * 512, half * 512 + 512)
                            op("dve", lambda e, cs=cs, cp=cp, half=half, hsl=hsl: e.tensor_tensor(out=Ufz[cs, cp, hsl], in0=PF[6 + half][cs, :], in1=Ufz[cs, cp, hsl], op=ALU.add),
                               r=["pf%d" % (6 + half), "Ufz"], w=["Ufz"])
                            op("act", lambda e, cs=cs, cp=cp, hsl=hsl: e.activation(out=Uz[cs, cp, hsl], in_=Ufz[cs, cp, hsl], func=AF.Copy), r=["Ufz"], w=["Uz"])
                        yield
                        for h in range(16):
                            hs = HS[h % 2]
                            hv = slice(h * 64, h * 64 + 64)
                            oy = PF[4][hs, (h // 2) * 64:(h // 2) * 64 + 64]
                            op("pe", lambda e, oy=oy, h=h, cs=cs: e.matmul(oy, Sbz[:, h // 2, h % 2, :], rtz[:, h % 2, h // 2, cs], start=True, stop=False, skip_group_check=True),
                               r=["Sbz", L], w=["pf4"], inc=False)
                            op("pe", lambda e, oy=oy, hv=hv, h=h, cp=cp: e.matmul(oy, Uz[:, cp, hv], mbr[:, h, :], start=False, stop=False, skip_group_check=True),
                               r=["Uz", KM[1]], w=["pf4"], inc=False)
                            op("pe", lambda e, oy=oy, hv=hv, h=h, cp=cp: e.matmul(oy, vhz[:, cp, hv], mkr[:, h, :], start=False, stop=True, skip_group_check=True),
                               r=[L, KM[2]], w=["pf4"], inc=(h == 15))
                        for h in range(16):
                            hs = HS[h % 2]
                            hv = slice(h * 64, h * 64 + 64)
                            osn = PF[5][hs, (h // 2) * 64:(h // 2) * 64 + 64]
                            op("pe", lambda e, osn=osn, hv=hv, cp=cp: e.matmul(osn, bhz[:, cp, hv], Ufz[:, cp, hv], start=True, stop=False, skip_group_check=True),
                               r=[L, "Ufz"], w=["pf5"], inc=False)
                            op("pe", lambda e, osn=osn, hv=hv, cp=cp: e.matmul(osn, khz[:, cp, hv], vfz[:, cp, hv], start=False, stop=True, skip_group_check=True),
                               r=[L], w=["pf5"], inc=(h == 15))
                        op("act", lambda e, yb=yb, cs=cs: e.activation(out=yb[:, :, cs], in_=PF[4][:, :].rearrange("p (c t) -> p c t", t=64), func=AF.Copy), r=["pf4"], w=[yk])
                        for hp in range(8):
                            op("dve", lambda e, hp=hp, cp=cp: e.scalar_tensor_tensor(out=ST[:, hp, :], in0=ST[:, hp, :], scalar=dt_[:, hp, cp:cp + 1], in1=PF[5][:, hp * 64:(hp + 1) * 64], op0=ALU.mult, op1=ALU.add),
                               r=["ST", L, "pf5"], w=["ST"])
                        copy_state_bf16()
                        yield
                    dma("sp", "c_y%d" % b, fm_dram(YS[d], t0, 128), yb[:], r=[yk], w=["YS"])
                    if last_tile and si < 2:
                        for hp in range(8):
                            op("pe", lambda e, hp=hp: e.transpose(PF[4 + hp // 4][0:64, (hp % 4) * 128:(hp % 4) * 128 + 128], ST[:, hp, :], identf[:]),
                               r=["ST", "identf"], w=["pf%d" % (4 + hp // 4)], inc=(hp % 4 == 3))
                        op("dve", lambda e: e.tensor_copy(out=SO[:, 0:4, :].rearrange("p a b -> p (a b)"), in_=PF[4][0:64, :]), r=["pf4"], w=["SO"])
                        op("dve", lambda e: e.tensor_copy(out=SO[:, 4:8, :].rearrange("p a b -> p (a b)"), in_=PF[5][0:64, :]), r=["pf5"], w=["SO"])
                        dma("sp", "c_so", sto[si, d].rearrange("h v k -> v h k"), SO[:].rearrange("v a (h k) -> v (a h) k", k=64), r=["SO"], w=["sto"])
                        yield

                units = []
                for si, (s0_, sl_) in enumerate(SEQS):
                    ntl = sl_ // 128
                    for d in range(2):
                        order = list(range(ntl)) if d == 0 else list(range(ntl - 1, -1, -1))
                        for j, ti in enumerate(order):
                            units.append((si, d, ti, j == 0, j == ntl - 1))
                PIPE = int(os.environ.get("C_PIPE", "1"))
                for _ in prep(units[0][0], units[0][1], units[0][2], 0):
                    pass
                for j, (si, d, ti, ft, ltile) in enumerate(units):
                    b = j % 2
                    g1 = chain(si, d, ti, b, ft, ltile)
                    g2 = prep(units[j + 1][0], units[j + 1][1], units[j + 1][2], 1 - b) if j + 1 < len(units) else iter(())
                    if not PIPE:
                        for _ in g1:
                            pass
                        for _ in g2:
                            pass
                        continue
                    a_done = b_done = False
                    while not (a_done and b_done):
                        if not a_done:
                            try:
                                next(g1)
                            except StopIteration:
                                a_done = True
                        if not b_done:
                            try:
                                next(g2)
                            except StopIteration:
                                b_done = True
                kb.barrier()

        if run("D"):
            with ExitStack() as ph:
                w_pg = sb("w_pg", [128, 8, 2560], BF16, ph)
                w_o = sb("w_o", [128, 8, D], BF16, ph)
                pws = sb("pws", [128, 4, 256], BF16, ph)
                pms = sb("pms", [128, 2, 4, 2, 256], BF16, ph)
                dma("pool", "d_w0", w_pg[:, :, 0:512], w_in[:, 0:512].rearrange("(c p) n -> p c n", p=128), w=["dw"])
                for q in range(2):
                    dma("pool", "d_w0", w_pg[:, :, 512 + q * 1024:512 + (q + 1) * 1024],
                        w_in[:, 3584 + q * 1024:3584 + (q + 1) * 1024].rearrange("(c p) n -> p c n", p=128), w=["dw"])
                dma("pool", "d_w0", w_o[:], w_out.rearrange("(c p) n -> p c n", p=128), w=["dw"])
                dma("pool", "d_w0", pws[:], pool_w.rearrange("g c n -> c g n"), w=["dw"])
                dma("pool", "d_w0", pms[:], c_pm.rearrange("k g p s t -> p k g s t"), w=["dw"])
                hT = sb("d_hT", [128, 8, 256], BF16, ph)
                yf = sb("d_yf", [128, 8, 256], F32, ph)
                yb2 = sb("d_yb", [128, 8, 256], F32, ph)
                bon = sb("d_bon", [128, 8, 256], F32, ph)
                gt = sb("d_gt", [128, 8, 256], F32, ph)
                xt = sb("d_x", [128, 2, D], F32, ph)
                yc2 = sb("d_yc", [128, 8, 256], F32, ph)
                gA = sb("d_gA", [128, 8, 256], F32, ph)
                gB = sb("d_gB", [128, 8, 256], F32, ph)
                zp = sb("d_zp", [128, 2, 512], BF16, ph)
                mixT = sb("d_mixT", [128, 4, 256], BF16, ph)
                t1 = sb("d_t1", [128, 256], F32, ph)
                mT = sb("d_mT", [128, 8, 256], BF16, ph)
                x1 = sb("d_x1", [128, 2, D], F32, ph)
                tmp = sb("d_tmp", [128, 512], F32, ph)
                ss = sb("d_ss", [128, 1], F32, ph)
                rstd = sb("d_rstd", [128, 1], F32, ph)
                xn = sb("d_xn", [128, D], F32, ph)
                h2 = sb("d_h2", [128, 8, 128], BF16, ph)
                eps_ln = sb("d_eps", [128, 1], F32, ph)
                op("dve", lambda e: e.memset(eps_ln[:], 64e-5), w=["eps_ln"])
                for blk in range(NT // 256):
                    t0 = blk * 256
                    cv = 0 if blk < 2 else 1
                    kind = 0 if blk < 2 else 1
                    dma("sp", "d_l0", hT[:], fm_dram(HT, t0, 256), r=[], w=["hT"])
                    dma("sp", "d_l1", yf[:], fm_dram(YS[0], t0, 256), w=["yf"])
                    dma("sp", "d_l2", yb2[:], fm_dram(YS[1], t0, 256), w=["yb"])
                    dma("sp", "d_l3", bon[:], fm_dram(BON, t0, 256), w=["bon"])
                    dma("sp", "d_l4", gt[:], fm_dram(GT, t0, 256), w=["gt"])
                    dma("sp", "d_l5", xt[:], xc[t0:t0 + 256, :].rearrange("(a p) n -> p a n", p=128), w=["xt"])
                    op("dve", lambda e: e.tensor_tensor(out=yf[:], in0=yf[:], in1=yb2[:], op=ALU.add), r=["yf", "yb"], w=["yf"])
                    for q in range(4):
                        for j in range(2):
                            c = q * 2 + j
                            op("pe", lambda e, c=c, j=j, q=q: e.matmul(PF[q % 2][:, j * 256:(j + 1) * 256], bdones[:], yf[:, c, :], start=True, stop=True),
                               r=["bdones", "yf"], w=["pf%d" % (q % 2)], inc=(j == 1))
                        op("dve", lambda e, q=q: e.scalar_tensor_tensor(out=yc2[:, 2 * q:2 * q + 2, :], in0=PF[q % 2][:, :].rearrange("p (j t) -> p j t", t=256), scalar=-1.0 / 64, in1=yf[:, 2 * q:2 * q + 2, :], op0=ALU.mult, op1=ALU.add),
                           r=["pf%d" % (q % 2), "yf"], w=["yc"])
                    op("act", lambda e: e.activation(out=yb2[:], in_=yc2[:], func=AF.Square), r=["yc"], w=["yb"])
                    for q in range(4):
                        for j in range(2):
                            c = q * 2 + j
                            op("pe", lambda e, c=c, j=j, q=q: e.matmul(PF[q % 2][:, j * 256:(j + 1) * 256], bdones[:], yb2[:, c, :], start=True, stop=True),
                               r=["bdones", "yb"], w=["pf%d" % (q % 2)], inc=(j == 1))
                        op("act", lambda e, q=q: e.activation(out=yf[:, 2 * q:2 * q + 2, :], in_=PF[q % 2][:, :].rearrange("p (j t) -> p j t", t=256), func=AF.Ln, scale=1.0 / 64, bias=eps_ln[:]),
                           r=["pf%d" % (q % 2), "eps_ln"], w=["yf"])
                    op("act", lambda e: e.activation(out=yf[:], in_=yf[:], func=AF.Exp, scale=-0.5), r=["yf"], w=["yf"])
                    op("dve", lambda e: e.tensor_tensor(out=yc2[:], in0=yc2[:], in1=yf[:], op=ALU.mult), r=["yc", "yf"], w=["yc"])
                    for c in range(8):
                        op("dve", lambda e, c=c: e.tensor_scalar(out=yc2[:, c, :], in0=yc2[:, c, :], scalar1=colc("ln_w", c), scalar2=colc("ln_b", c), op0=ALU.mult, op1=ALU.add),
                           r=["yc", "cols"], w=["yc"])
                    op("dve", lambda e: e.tensor_tensor(out=yc2[:], in0=yc2[:], in1=bon[:], op=ALU.add), r=["yc", "bon"], w=["yc"])
                    op("dve", lambda e: e.tensor_tensor(out=yc2[:], in0=yc2[:], in1=gt[:], op=ALU.mult), r=["yc", "gt"], w=["yc"])
                    for fc in range(16):
                        pf = PF[2 + fc % 2]
                        pk = "pf%d" % (2 + fc % 2)
                        for kc in range(8):
                            op("pe", lambda e, fc=fc, kc=kc, pf=pf: e.matmul(pf[:, 0:256], w_pg[:, kc, 512 + fc * 128:512 + (fc + 1) * 128], hT[:, kc, :], start=(kc == 0), stop=(kc == 7)),
                               r=["dw", "hT"], w=[pk], inc=(kc == 7))
                        dst = gA if fc < 8 else gB
                        op("act", lambda e, fc=fc, pf=pf, dst=dst: e.activation(out=dst[:, fc % 8, :], in_=pf[:, 0:256], func=AF.Sigmoid), r=[pk], w=["gA" if fc < 8 else "gB"])
                    op("dve", lambda e: e.tensor_tensor(out=yc2[:], in0=yc2[:], in1=gB[:], op=ALU.mult), r=["yc", "gB"], w=["yc"])
                    for a in range(2):
                        for kc in range(8):
                            op("pe", lambda e, a=a, kc=kc: e.matmul(PF[4][:, :], hT[:, kc, a * 128:(a + 1) * 128], w_pg[:, kc, 0:512], start=(kc == 0), stop=(kc == 7)),
                               r=["dw", "hT"], w=["pf4"], inc=(kc == 7))
                        op("act", lambda e, a=a: e.activation(out=zp[:, a, :], in_=PF[4][:, :], func=AF.Copy), r=["pf4"], w=["zp"])
                    for g in range(4):
                        for a in range(2):
                            op("pe", lambda e, g=g, a=a: e.matmul(PF[5][:, 0:256], zp[:, a, g * 128:(g + 1) * 128], pms[:, kind, g, a, :], start=(a == 0), stop=(a == 1)),
                               r=["zp", "dw"], w=["pf5"], inc=(a == 1))
                        op("act", lambda e, g=g: e.activation(out=mixT[:, g, :], in_=PF[5][:, 0:256], func=AF.Copy), r=["pf5"], w=["mixT"])
                    for dc in range(8):
                        g = dc // 2
                        pf = PF[dc % 2]
                        pk = "pf%d" % (dc % 2)
                        op("pe", lambda e, dc=dc, g=g, pf=pf: e.matmul(pf[:, 0:256], pws[:, g, (dc % 2) * 128:(dc % 2) * 128 + 128], mixT[:, g, :], start=True, stop=True),
                           r=["dw", "mixT"], w=[pk])
                        op("dve", lambda e, dc=dc, pf=pf: e.scalar_tensor_tensor(out=t1[:], in0=pf[:, 0:256], scalar=colc("pool_scale", dc), in1=gA[:, dc, :], op0=ALU.mult, op1=ALU.mult),
                           r=[pk, "gA", "cols"], w=["t1"])
                        op("dve", lambda e, dc=dc: e.tensor_tensor(out=mT[:, dc, :], in0=t1[:], in1=yc2[:, dc, :], op=ALU.add), r=["t1", "yc"], w=["mT"])
                    for a in range(2):
                        for half in range(2):
                            pf = PF[2 + half]
                            pk = "pf%d" % (2 + half)
                            for cc in range(8):
                                op("pe", lambda e, a=a, half=half, cc=cc, pf=pf: e.matmul(pf[:, :], mT[:, cc, a * 128:(a + 1) * 128], w_o[:, cc, half * 512:(half + 1) * 512], start=(cc == 0), stop=(cc == 7)),
                                   r=["mT", "dw"], w=[pk], inc=(cc == 7))
                            op("dve", lambda e, a=a, half=half, pf=pf: e.tensor_tensor(out=tmp[:], in0=pf[:, :], in1=GA1[cv][:, half * 512:(half + 1) * 512], op=ALU.mult),
                               r=[pk, "GA"], w=["tmp"])
                            op("dve", lambda e, a=a, half=half: e.tensor_tensor(out=x1[:, a, half * 512:(half + 1) * 512], in0=tmp[:], in1=xt[:, a, half * 512:(half + 1) * 512], op=ALU.add),
                               r=["tmp", "xt"], w=["x1"])
                        norm_transpose(ph, x1[:, a, :], "x1", A2, B2, cv, lambda c: h2[:, c, :], "h2", (ss, rstd, xn))
                        dma("sp", "d_s0", fm_dram(H2T, t0 + a * 128, 128), h2[:], r=["h2"], w=["H2T"])
                    dma("sp", "d_s1", X1[t0:t0 + 256, :].rearrange("(a p) n -> p a n", p=128), x1[:], r=["x1"], w=["X1"])
                kb.barrier()

        if run("E"):
            with ExitStack() as ph:
                wf1 = sb("wf1", [128, 8, 4096], BF16, ph)
                wf2 = sb("wf2", [128, 32, D], BF16, ph)
                for q in range(8):
                    dma("pool", "e_w1_%d" % q, wf1[:, :, q * 512:(q + 1) * 512], w_ff1[:, q * 512:(q + 1) * 512].rearrange("(c p) n -> p c n", p=128), w=["wf1_%d" % q])
                for q in range(8):
                    dma("pool", "e_w2_%d" % q, wf2[:, q * 4:(q + 1) * 4, :], w_ff2[q * 512:(q + 1) * 512, :].rearrange("(c p) n -> p c n", p=128), w=["wf2_%d" % q])
                h2 = sb("e_h2", [128, 8, 256], BF16, ph)
                x1 = sb("e_x1", [128, 2, D], F32, ph)
                u0 = [sb("e_u0%d" % i, [128, 512], BF16, ph) for i in range(2)]
                uT = sb("e_uT", [128, 32, 256], BF16, ph)
                tmp = sb("e_tmp", [128, 512], F32, ph)
                x2 = sb("e_x2", [128, D], F32, ph)
                junk = sb("e_junk", [128, D], BF16, ph)
                ss = sb("e_ss", [128, 1], F32, ph)
                rstd = sb("e_rstd", [128, 1], F32, ph)
                ot = [sb("e_o%d" % i, [128, D], F32, ph) for i in range(2)]
                for blk in range(NT // 256):
                    t0 = blk * 256
                    cv = 0 if blk < 2 else 1
                    dma("sp", "e_l0", h2[:], fm_dram(H2T, t0, 256), w=["h2"])
                    dma("sp", "e_l1", x1[:], X1[t0:t0 + 256, :].rearrange("(a p) n -> p a n", p=128), w=["x1"])
                    for fp in range(16):
                        pf = PF[fp % 2]
                        pk = "pf%d" % (fp % 2)
                        for j in range(2):
                            fc = fp * 2 + j
                            for kc in range(8):
                                op("pe", lambda e, fc=fc, kc=kc, j=j, pf=pf: e.matmul(pf[:, j * 256:(j + 1) * 256], wf1[:, kc, fc * 128:(fc + 1) * 128], h2[:, kc, :], start=(kc == 0), stop=(kc == 7)),
                                   r=["wf1_%d" % (fc // 4), "h2"], w=[pk], inc=(kc == 7 and j == 1))
                        u = u0[fp % 2]
                        uk = "u0%d" % (fp % 2)
                        op("act", lambda e, pf=pf, u=u: e.activation(out=u[:], in_=pf[:, :], func=AF.Relu), r=[pk], w=[uk])
                        op("dve", lambda e, fp=fp, u=u: e.tensor_tensor(out=uT[:, 2 * fp:2 * fp + 2, :].rearrange("p j t -> p (j t)"), in0=u[:], in1=u[:], op=ALU.mult), r=[uk], w=["uT"])
                    for a in range(2):
                        for half in range(2):
                            pf = PF[2 + half]
                            pk = "pf%d" % (2 + half)
                            for fc in range(32):
                                op("pe", lambda e, a=a, half=half, fc=fc, pf=pf: e.matmul(pf[:, :], uT[:, fc, a * 128:(a + 1) * 128], wf2[:, fc, half * 512:(half + 1) * 512], start=(fc == 0), stop=(fc == 31)),
                                   r=["uT", "wf2_%d" % (fc // 4)], w=[pk], inc=(fc == 31))
                            op("dve", lambda e, half=half, pf=pf: e.tensor_tensor(out=tmp[:], in0=pf[:, :], in1=GA2[cv][:, half * 512:(half + 1) * 512], op=ALU.mult),
                               r=[pk, "GA"], w=["tmp"])
                            op("dve", lambda e, a=a, half=half: e.tensor_tensor(out=x2[:, half * 512:(half + 1) * 512], in0=tmp[:], in1=x1[:, a, half * 512:(half + 1) * 512], op=ALU.add),
                               r=["tmp", "x1"], w=["x2"])
                        op("act", lambda e: e.activation(out=junk[:], in_=x2[:], func=AF.Square, accum_out=ss[:]), r=["x2"], w=["junk", "ss"])
                        op("dve", lambda e: e.tensor_scalar(out=rstd[:], in0=ss[:], scalar1=1.0 / D, scalar2=1e-6, op0=ALU.mult, op1=ALU.add), r=["ss"], w=["rstd"])
                        op("act", lambda e: e.activation(out=rstd[:], in_=rstd[:], func=AF.Sqrt), r=["rstd"], w=["rstd"])
                        op("dve", lambda e: e.reciprocal(out=rstd[:], in_=rstd[:]), r=["rstd"], w=["rstd"])
                        o = ot[a]
                        okey = "ot%d" % a
                        op("act", lambda e, o=o: e.activation(out=o[:], in_=x2[:], func=AF.Copy, scale=rstd[:]), r=["x2", "rstd"], w=[okey])
                        op("dve", lambda e, o=o: e.tensor_tensor(out=o[:], in0=o[:], in1=GF[:], op=ALU.mult), r=[okey, "GF"], w=[okey])
                        dma("sp", "e_o%d" % a, yc[t0 + a * 128:t0 + (a + 1) * 128, :], o[:], r=[okey], w=["yc"])
        kb.finish()
    kb.es.close()
    return nc, kb


def _constants():
    identf = np.eye(128, dtype=np.float32)
    tri = np.zeros((2, 128, 384), np.float32)
    C = -float(np.exp(-0.5))
    for t in range(128):
        for u in range(128):
            if t // 64 != u // 64:
                continue
            tri[0, t, u] = C if t <= u else 0.0
            tri[0, t, 128 + u] = C if t < u else 0.0
            tri[0, t, 256 + u] = C if t > u else 0.0
            tri[1, t, u] = C if t >= u else 0.0
            tri[1, t, 128 + u] = C if t > u else 0.0
            tri[1, t, 256 + u] = C if t < u else 0.0
    s = np.arange(64)[:, None]
    t = np.arange(64)[None, :]
    base = [(s < t), (s > t), (s <= t), (s >= t), (s == t)]
    mask = np.zeros((5, 128, 512), np.float32)
    for m in range(5):
        blk = base[m].astype(np.float32)
        mask[m] = np.tile(np.tile(blk, (2, 1)), (1, 8))
    bd = np.zeros((128, 128), np.float32)
    bd[:64, :64] = 1.0
    bd[64:, 64:] = 1.0
    pm = np.zeros((2, 4, 256, 256), np.float32)
    for kind, lr in ((0, 256), (1, 64)):
        for g, win in enumerate(WINS):
            for tt in range(256):
                r0 = (tt // lr) * lr
                tl = tt - r0
                lo = min(max(tl - win // 2, 0), lr)
                hi = min(max(tl + win - win // 2, 0), lr)
                pm[kind, g, r0 + lo:r0 + hi, tt] += 1.0 / (hi - lo)
                pm[kind, g, tt, tt] -= 1.0
    pm = pm.reshape(2, 4, 2, 128, 256).transpose(0, 1, 3, 2, 4)
    return identf, tri, mask, bd, np.ascontiguousarray(pm)


_CACHE = {}


def kernel(x_prompt, x_sample, state_rwkv, c, c_ctx, w_ada, b_ada, g_norm1, g_norm2, w_in,
           mu_rkv, mu_wag, w_dec0, w_dec1, w_dec2, a0, a1, a2, gate_w1, gate_w2, k_k, k_a,
           r_k, ln_x_w, ln_x_b, pool_w, pool_scale, w_out, w_ff1, w_ff2, g_final):
    in_maps = _prep(x_prompt, x_sample, state_rwkv, c, c_ctx, w_ada, b_ada, g_norm1, g_norm2, w_in,
                    mu_rkv, mu_wag, w_dec0, w_dec1, w_dec2, a0, a1, a2, gate_w1, gate_w2, k_k, k_a,
                    r_k, ln_x_w, ln_x_b, pool_w, pool_scale, w_out, w_ff1, w_ff2, g_final)
    if "nc" not in _CACHE:
        _CACHE["nc"] = build()[0]
    nc = _CACHE["nc"]
    res = run_bass_kernel_spmd(nc, in_maps, core_ids=list(range(8)))
    outs = res.results
    y_prompt = np.zeros((16, 256, D), np.float32)
    y_sample = np.zeros((4, 2048, D), np.float32)
    st_new = np.zeros((16, 1, 2, 16, 64, 64), np.float32)
    for core in range(8):
        yc_ = np.asarray(outs[core]["yc"], dtype=np.float32)
        y_prompt[2 * core] = yc_[0:256]
        y_prompt[2 * core + 1] = yc_[256:512]
        if core < 4:
            y_sample[core] = yc_[512:]
        so = np.asarray(outs[core]["sto"], dtype=np.float32)
        st_new[2 * core, 0] = so[0]
        st_new[2 * core + 1, 0] = so[1]
    return y_prompt, y_sample, st_new


def _prep(x_prompt, x_sample, state_rwkv, c, c_ctx, w_ada, b_ada, g_norm1, g_norm2, w_in,
          mu_rkv, mu_wag, w_dec0, w_dec1, w_dec2, a0, a1, a2, gate_w1, gate_w2, k_k, k_a,
          r_k, ln_x_w, ln_x_b, pool_w, pool_scale, w_out, w_ff1, w_ff2, g_final):
    f = lambda a: np.ascontiguousarray(np.asarray(a, dtype=np.float32))
    x_prompt, x_sample, state_rwkv = f(x_prompt), f(x_sample), f(state_rwkv)
    identf, tri, mask, bd, pm = _constants()
    b_ada6 = f(b_ada).reshape(6, D)
    shared = {
        "w_ada": f(w_ada)[0], "w_in": f(w_in)[0], "w_dec0": f(w_dec0)[0], "w_dec1": f(w_dec1)[0],
        "w_dec2": f(w_dec2)[0], "a1": f(a1)[0], "a2": f(a2)[0], "gate_w1": f(gate_w1)[0],
        "gate_w2": f(gate_w2)[0], "pool_w": f(pool_w)[0], "w_out": f(w_out)[0], "w_ff1": f(w_ff1)[0],
        "w_ff2": f(w_ff2)[0], "g_final": f(g_final).reshape(1, D), "b_ada": b_ada6,
        "c_identb": identf, "c_identf": identf, "c_tri": tri, "c_mask": mask, "c_bd": bd, "c_pm": pm,
    }
    in_maps = []
    for core in range(8):
        li = core % 4
        rows = np.zeros((NR, D), np.float32)
        vals = {"g_norm1": f(g_norm1)[0], "g_norm2": f(g_norm2)[0], "mu_r": f(mu_rkv)[0, 0], "mu_k": f(mu_rkv)[0, 1],
                "mu_v": f(mu_rkv)[0, 2], "mu_w": f(mu_wag)[0, 0], "mu_a": f(mu_wag)[0, 1], "mu_g": f(mu_wag)[0, 2],
                "a0f": f(a0)[0, 0], "a0b": f(a0)[0, 1], "k_k": f(k_k)[0], "k_a": f(k_a)[0], "r_k": f(r_k)[0].reshape(-1),
                "ln_w": f(ln_x_w)[0], "ln_b": f(ln_x_b)[0], "pool_scale": f(pool_scale)[0],
                "sh1": b_ada6[0], "sc1": b_ada6[1], "ga1": b_ada6[2], "sh2": b_ada6[3], "sc2": b_ada6[4], "ga2": b_ada6[5],
                "c0": f(c_ctx), "c1": f(c)[li]}
        for n, v in vals.items():
            rows[RI[n]] = v
        xcore = np.concatenate([x_prompt[2 * core], x_prompt[2 * core + 1], x_sample[li]], axis=0)
        m = dict(shared)
        m["xc"] = np.ascontiguousarray(xcore)
        m["rows"] = rows
        m["st0"] = np.ascontiguousarray(state_rwkv[li, 0])
        in_maps.append(m)
    return in_maps
```
